# Optimizing a Trainium2 kernel written in Bass

```python
import jax, jax.numpy as jnp
from jax import lax
import numpy as np

D_MODEL = 1024
BATCH = 4
SEQ = 4096
DEPTH = 1
DEC_BATCH = 32
DEC_SEQ = 4
PAST_LEN = 8192
PAGE_SIZE = 128

HEAD_DIM = 64
N_HEADS_A = 8
N_KV_A = 2
GROUP_A = N_HEADS_A // N_KV_A
N_IDX_HEADS = 8
IDX_DIM = 64
TOPK_MAX = 256
N_HEADS_B = 8
Q_A_W = N_HEADS_A * HEAD_DIM
KV_A_W = 2 * N_KV_A * HEAD_DIM
Q_I_W = N_IDX_HEADS * IDX_DIM
Q_B_W = N_HEADS_B * HEAD_DIM
KV_B_W = 2 * N_HEADS_B * HEAD_DIM
IN_W = Q_A_W + KV_A_W + Q_I_W + IDX_DIM + N_IDX_HEADS + Q_B_W + KV_B_W + 2 * D_MODEL
D_FF = -(-8 * D_MODEL // (3 * 256)) * 256
ROPE_THETA = 10000.0
QBLK = 128
RMS_EPS = 1e-6

kernel_name = 'dsa_stickbreak_gated_hybrid_step'


def _rmsnorm(x, g):
    x32 = x.astype(jnp.float32)
    y = x32 * lax.rsqrt(jnp.mean(x32 * x32, axis=-1, keepdims=True) + RMS_EPS)
    return (y * g.astype(jnp.float32)).astype(x.dtype)


def _rope(x, pos):
    half = x.shape[-1] // 2
    inv_freq = ROPE_THETA ** (-jnp.arange(half, dtype=jnp.float32) / half)
    ang = pos.astype(jnp.float32)[:, None] * inv_freq[None, :]
    cos = jnp.cos(ang)[:, None, :]
    sin = jnp.sin(ang)[:, None, :]
    x1 = x[..., :half].astype(jnp.float32)
    x2 = x[..., half:].astype(jnp.float32)
    return jnp.concatenate([x1 * cos - x2 * sin, x2 * cos + x1 * sin], axis=-1).astype(x.dtype)


def _project(xn, w_in, pos):
    b, t, _ = xn.shape
    y = xn @ w_in
    sizes = (Q_A_W, KV_A_W, Q_I_W, IDX_DIM, N_IDX_HEADS, Q_B_W, KV_B_W)
    offs = []
    acc = 0
    for s in sizes:
        acc += s
        offs.append(acc)
    p = jnp.split(y, offs, axis=-1)
    q_a = _rope(p[0].reshape(b, t, N_HEADS_A, HEAD_DIM), pos)
    kv = p[1].reshape(b, t, 2, N_KV_A, HEAD_DIM)
    kv_a = jnp.stack([_rope(kv[:, :, 0], pos), kv[:, :, 1]], axis=2)
    q_i = _rope(p[2].reshape(b, t, N_IDX_HEADS, IDX_DIM), pos)
    k_i = _rope(p[3].reshape(b, t, 1, IDX_DIM), pos)[:, :, 0]
    w_i = p[4] * (N_IDX_HEADS ** -0.5)
    q_b = p[5].reshape(b, t, N_HEADS_B, HEAD_DIM)
    kv_b = p[6].reshape(b, t, 2, N_HEADS_B, HEAD_DIM)
    gates = jax.nn.sigmoid(p[7].astype(jnp.float32)).astype(xn.dtype).reshape(b, t, 2, D_MODEL)
    return q_a, kv_a, q_i, k_i, w_i, q_b, kv_b, gates


def _dsa_block(qpos, q_a, q_i, w_i, kv_a, k_i):
    b, tb = q_a.shape[:2]
    n_keys = k_i.shape[1]
    n_sel = max(1, min(TOPK_MAX, n_keys // 4))
    rel = jax.nn.relu(jnp.einsum('bthe,bse->bths', q_i, k_i).astype(jnp.float32))
    score = jnp.einsum('bth,bths->bts', w_i.astype(jnp.float32), rel) * (IDX_DIM ** -0.5)
    kpos = jnp.arange(n_keys, dtype=jnp.int32)
    causal = kpos[None, :] <= qpos[:, None]
    score = jnp.where(causal[None], score, -jnp.inf)
    _, idx = lax.top_k(score, n_sel)
    valid = idx <= qpos[None, :, None]
    sel = jax.vmap(lambda rows, ii: rows[ii])(kv_a, idx)
    qg = q_a.reshape(b, tb, N_KV_A, GROUP_A, HEAD_DIM)
    s = jnp.einsum('btcgd,btncd->btcgn', qg, sel[:, :, :, 0]).astype(jnp.float32) * (HEAD_DIM ** -0.5)
    s = jnp.where(valid[:, :, None, None, :], s, -jnp.inf)
    p = jax.nn.softmax(s, axis=-1).astype(q_a.dtype)
    o = jnp.einsum('btcgn,btncd->btcgd', p, sel[:, :, :, 1])
    return o.reshape(b, tb, Q_A_W)


def _sb_block(qpos, q_b, kv_b):
    b, tb = q_b.shape[:2]
    n_keys = kv_b.shape[1]
    z = jnp.einsum('bthd,bshd->bhts', q_b, kv_b[:, :, 0]).astype(jnp.float32) * (HEAD_DIM ** -0.5)
    kpos = jnp.arange(n_keys, dtype=jnp.int32)
    strict = (kpos[None, :] < qpos[:, None])[None, None]
    log_beta = jax.nn.log_sigmoid(z)
    log_rest = jnp.where(strict, log_beta - z, 0.0)
    later = lax.cumsum(log_rest, axis=3, reverse=True) - log_rest
    a = jnp.where(strict, jnp.exp(log_beta + later), 0.0)
    o = jnp.einsum('bhts,bshd->bthd', a.astype(q_b.dtype), kv_b[:, :, 1])
    return o.reshape(b, tb, Q_B_W)


def _sweep(fn, pos, *qs):
    t = pos.shape[0]
    if t % QBLK != 0 or t <= QBLK:
        return fn(pos, *qs)
    nb = t // QBLK

    def split(a):
        return jnp.moveaxis(a.reshape(a.shape[0], nb, QBLK, *a.shape[2:]), 1, 0)

    out = lax.map(lambda args: fn(*args), (pos.reshape(nb, QBLK),) + tuple(split(a) for a in qs))
    out = jnp.moveaxis(out, 0, 1)
    return out.reshape(out.shape[0], t, *out.shape[3:])


def _mixers(pos, q_a, kv_a_ctx, q_i, k_i_ctx, w_i, q_b, kv_b_ctx, gates, w_br_a, w_br_b, w_o):
    o_a = _sweep(lambda qp, qa, qi, wi: _dsa_block(qp, qa, qi, wi, kv_a_ctx, k_i_ctx), pos, q_a, q_i, w_i)
    o_b = _sweep(lambda qp, qb: _sb_block(qp, qb, kv_b_ctx), pos, q_b)
    merged = gates[:, :, 0] * (o_a @ w_br_a) + gates[:, :, 1] * (o_b @ w_br_b)
    return merged @ w_o


def _ffn(h, w_gate, w_up, w_down):
    return (jax.nn.silu(h @ w_gate) * (h @ w_up)) @ w_down


def _gather_pages(pool, page_table):
    rows = pool[page_table]
    b, n_pages, page = rows.shape[:3]
    return rows.reshape(b, n_pages * page, *rows.shape[3:])


def setup_inputs(seed: int = 0) -> dict:
    key = jax.random.key(seed)
    ks = jax.random.split(key, 20)
    f32 = jnp.float32
    n_pages = PAST_LEN // PAGE_SIZE
    n_used = DEC_BATCH * n_pages
    n_pool = n_used + max(1, n_used // 4)

    def nrm(k, shape, scale=1.0):
        return jax.random.normal(k, shape, f32) * scale

    page_table = jax.random.permutation(ks[5], n_pool)[:n_used].reshape(DEC_BATCH, n_pages).astype(jnp.int32)
    return {
        'x_prompt': nrm(ks[0], (BATCH, SEQ, D_MODEL)),
        'x_sample': nrm(ks[1], (DEC_BATCH, DEC_SEQ, D_MODEL)),
        'cache_kv_a': nrm(ks[2], (DEPTH, n_pool, PAGE_SIZE, 2, N_KV_A, HEAD_DIM)),
        'cache_k_idx': nrm(ks[3], (DEPTH, n_pool, PAGE_SIZE, IDX_DIM)),
        'cache_kv_b': nrm(ks[4], (DEPTH, n_pool, PAGE_SIZE, 2, N_HEADS_B, HEAD_DIM)),
        'page_table': page_table,
        'w_in': nrm(ks[6], (DEPTH, D_MODEL, IN_W), D_MODEL ** -0.5),
        'w_br_a': nrm(ks[7], (DEPTH, Q_A_W, D_MODEL), Q_A_W ** -0.5),
        'w_br_b': nrm(ks[8], (DEPTH, Q_B_W, D_MODEL), Q_B_W ** -0.5),
        'w_o': nrm(ks[9], (DEPTH, D_MODEL, D_MODEL), D_MODEL ** -0.5),
        'norm_attn': 1.0 + nrm(ks[10], (DEPTH, D_MODEL), 0.01),
        'norm_ffn': 1.0 + nrm(ks[11], (DEPTH, D_MODEL), 0.01),
        'w_ffn_gate': nrm(ks[12], (DEPTH, D_MODEL, D_FF), D_MODEL ** -0.5),
        'w_ffn_up': nrm(ks[13], (DEPTH, D_MODEL, D_FF), D_MODEL ** -0.5),
        'w_ffn_down': nrm(ks[14], (DEPTH, D_FF, D_MODEL), D_FF ** -0.5),
        'norm_final': 1.0 + nrm(ks[15], (D_MODEL,), 0.01),
    }


def reference(x_prompt, x_sample, cache_kv_a, cache_k_idx, cache_kv_b, page_table, w_in, w_br_a, w_br_b,
              w_o, norm_attn, norm_ffn, w_ffn_gate, w_ffn_up, w_ffn_down, norm_final):
    pos_p = jnp.arange(x_prompt.shape[1], dtype=jnp.int32)
    past_len = page_table.shape[1] * cache_kv_a.shape[2]
    pos_s = past_len + jnp.arange(x_sample.shape[1], dtype=jnp.int32)
    hp, hs = x_prompt, x_sample
    kv_a_p, k_i_p, kv_b_p, kv_a_s, k_i_s, kv_b_s = [], [], [], [], [], []
    for layer in range(DEPTH):
        xn = _rmsnorm(hp, norm_attn[layer])
        q_a, kv_a, q_i, k_i, w_i, q_b, kv_b, gates = _project(xn, w_in[layer], pos_p)
        hp = hp + _mixers(pos_p, q_a, kv_a, q_i, k_i, w_i, q_b, kv_b, gates,
                          w_br_a[layer], w_br_b[layer], w_o[layer])
        hp = hp + _ffn(_rmsnorm(hp, norm_ffn[layer]), w_ffn_gate[layer], w_ffn_up[layer], w_ffn_down[layer])
        kv_a_p.append(kv_a)
        k_i_p.append(k_i)
        kv_b_p.append(kv_b)

        xn = _rmsnorm(hs, norm_attn[layer])
        q_a, kv_a, q_i, k_i, w_i, q_b, kv_b, gates = _project(xn, w_in[layer], pos_s)
        kv_a_ctx = jnp.concatenate([_gather_pages(cache_kv_a[layer], page_table), kv_a], axis=1)
        k_i_ctx = jnp.concatenate([_gather_pages(cache_k_idx[layer], page_table), k_i], axis=1)
        kv_b_ctx = jnp.concatenate([_gather_pages(cache_kv_b[layer], page_table), kv_b], axis=1)
        hs = hs + _mixers(pos_s, q_a, kv_a_ctx, q_i, k_i_ctx, w_i, q_b, kv_b_ctx, gates,
                          w_br_a[layer], w_br_b[layer], w_o[layer])
        hs = hs + _ffn(_rmsnorm(hs, norm_ffn[layer]), w_ffn_gate[layer], w_ffn_up[layer], w_ffn_down[layer])
        kv_a_s.append(kv_a)
        k_i_s.append(k_i)
        kv_b_s.append(kv_b)
    y_prompt = _rmsnorm(hp, norm_final)
    y_sample = _rmsnorm(hs, norm_final)
    return (y_prompt, y_sample, jnp.stack(kv_a_p), jnp.stack(k_i_p), jnp.stack(kv_b_p),
            jnp.stack(kv_a_s), jnp.stack(k_i_s), jnp.stack(kv_b_s))
```

```python
from contextlib import ExitStack
import numpy as np
import concourse.bass as bass
import concourse.mybir as mybir
from concourse.bass_utils import run_bass_kernel_spmd

F32 = mybir.dt.float32
BF16 = mybir.dt.bfloat16
I32 = mybir.dt.int32
AF = mybir.ActivationFunctionType
ALU = mybir.AluOpType
AX = mybir.AxisListType

D = 1024
SEQ = 4096
NSEQ_S = 4
TS = 4
NPAGES = 64
DFF = 2816
NFF = 22
TOPK = 256
NBIS = 18
ENG = ("pe", "act", "dve", "pool", "sp")
NEG = -1.0e30


class Buf:
    __slots__ = ("w", "r")

    def __init__(self):
        self.w = []
        self.r = []


class Op:
    __slots__ = ("id", "eng", "fn", "deps", "cost", "isdma", "skip_self", "tok", "waits")

    def __init__(self, id_, eng, fn, deps, cost, isdma, skip_self):
        self.id, self.eng, self.fn, self.deps, self.cost = id_, eng, fn, deps, cost
        self.isdma, self.skip_self = isdma, skip_self
        self.tok = None
        self.waits = None


class Prog:
    def __init__(self, nc, n_dma_sems=12):
        self.nc = nc
        self.es = ExitStack()
        self.sem = {e: self.es.enter_context(nc.semaphore("c_" + e)) for e in ENG}
        self.dq = ("sp", "pool", "act")
        self.dsem = {q: [self.es.enter_context(nc.semaphore("d%s%d" % (q, i))) for i in range(n_dma_sems)]
                     for q in self.dq}
        self.epochs = [[]]
        self.nops = 0

    def sb(self, name, shape, dt):
        return self.es.enter_context(self.nc.sbuf_tensor(name, list(shape), dt))

    def ps(self, name, shape, dt):
        return self.es.enter_context(self.nc.psum_tensor(name, list(shape), dt))

    def _record(self, eng, fn, reads, writes, cost, isdma, skip_self):
        deps = set()
        for b in reads:
            deps.update(b.w)
        for b in writes:
            deps.update(b.w)
            deps.update(b.r)
        op = Op(self.nops, eng, fn, deps, cost, isdma, skip_self)
        self.nops += 1
        for b in reads:
            b.r.append(op.id)
        for b in writes:
            b.w = [op.id]
            b.r = []
        self.epochs[-1].append(op)

    def op(self, eng, fn, reads=(), writes=(), skip_self=False, cost=0.3):
        self._record(eng, fn, reads, writes, cost, False, skip_self)

    def dma_fn(self, eng, fn, reads=(), writes=(), cost=2.5):
        self._record(eng, fn, reads, writes, cost, True, False)

    def dma(self, eng, out, in_, reads=(), writes=(), cost=2.5):
        self.dma_fn(eng, lambda e: e.dma_start(out=out, in_=in_), reads, writes, cost)

    def barrier(self):
        if self.epochs[-1]:
            self.epochs.append([])

    @staticmethod
    def _schedule(ops):
        import heapq
        ids = {o.id: o for o in ops}
        indeg = {}
        children = {}
        for o in ops:
            d = [x for x in o.deps if x in ids]
            o.deps = d
            indeg[o.id] = len(d)
            for x in d:
                children.setdefault(x, []).append(o.id)
        finish = {}
        engfree = {e: 0.0 for e in ENG}
        heaps = {e: [] for e in ENG}
        for o in ops:
            if indeg[o.id] == 0:
                heapq.heappush(heaps[o.eng], (0.0, o.id))
        order = {e: [] for e in ENG}
        left = len(ops)
        while left:
            best = None
            for e in ENG:
                if heaps[e]:
                    rt, oid = heaps[e][0]
                    st = rt if rt > engfree[e] else engfree[e]
                    if best is None or (st, oid) < best[:2]:
                        best = (st, oid, e)
            st, oid, e = best
            heapq.heappop(heaps[e])
            o = ids[oid]
            if o.isdma:
                engfree[e] = st + 0.08
                finish[oid] = st + o.cost
            else:
                engfree[e] = st + o.cost
                finish[oid] = st + o.cost + 0.06
            order[e].append(o)
            left -= 1
            for c in children.get(oid, ()):
                indeg[c] -= 1
                if indeg[c] == 0:
                    oc = ids[c]
                    rt = max(finish[x] for x in oc.deps)
                    heapq.heappush(heaps[oc.eng], (rt, c))
        return order

    def emit(self):
        ndm = len(self.dsem["sp"])
        cnt = {e: 0 for e in ENG}
        dval = {q: [0] * ndm for q in self.dq}
        dnext = {q: 0 for q in self.dq}
        seen = {e: {} for e in ENG}
        stream = {e: [] for e in ENG}
        for ops in self.epochs:
            if not ops:
                continue
            order = self._schedule(ops)
            ids = {o.id: o for o in ops}
            pre = {}
            for e in ENG:
                for o in order[e]:
                    if o.isdma:
                        i = dnext[e]
                        dnext[e] = (i + 1) % ndm
                        pre[o.id] = (("d", e, i), dval[e][i])
                        dval[e][i] += 16
                        o.tok = (("d", e, i), dval[e][i])
                    else:
                        cnt[e] += 1
                        o.tok = (("e", e), cnt[e])
            for e in ENG:
                sn = seen[e]
                for o in order[e]:
                    need = {}
                    for x in o.deps:
                        k, v = ids[x].tok
                        if o.skip_self and k == ("e", e):
                            continue
                        if need.get(k, 0) < v:
                            need[k] = v
                    if o.isdma:
                        k, v = pre[o.id]
                        if v > 0 and need.get(k, 0) < v:
                            need[k] = v
                    waits = []
                    for k, v in need.items():
                        if sn.get(k, 0) < v:
                            sn[k] = v
                            waits.append((k, v))
                    stream[e].append((waits, o.fn, (o.tok[0], 16 if o.isdma else 1)))
            for e in ENG:
                waits = []
                for f in ENG:
                    k = ("e", f)
                    if cnt[f] > 0 and seen[e].get(k, 0) < cnt[f]:
                        seen[e][k] = cnt[f]
                        waits.append((k, cnt[f]))
                for q in self.dq:
                    for i in range(ndm):
                        k = ("d", q, i)
                        v = dval[q][i]
                        if v > 0 and seen[e].get(k, 0) < v:
                            seen[e][k] = v
                            waits.append((k, v))
                if waits:
                    stream[e].append((waits, None, None))
        self._check(stream)

        def semof(k):
            return self.sem[k[1]] if k[0] == "e" else self.dsem[k[1]][k[2]]

        engmap = {"pe": "tensor", "act": "scalar", "dve": "vector", "pool": "gpsimd", "sp": "sync"}
        with self.nc.Block() as block:
            for e in ENG:
                ops = stream[e]

                def body(engine, ops=ops):
                    for waits, fn, inc in ops:
                        for k, v in waits:
                            engine.wait_ge(semof(k), v)
                        if fn is not None:
                            fn(engine).then_inc(semof(inc[0]), inc[1])

                getattr(block, engmap[e])(body)
        self.es.close()

    @staticmethod
    def _check(stream):
        val = {}
        pos = {e: 0 for e in ENG}
        progress = True
        while progress:
            progress = False
            for e in ENG:
                while pos[e] < len(stream[e]):
                    waits, fn, inc = stream[e][pos[e]]
                    if any(val.get(k, 0) < v for k, v in waits):
                        break
                    if inc is not None:
                        val[inc[0]] = val.get(inc[0], 0) + inc[1]
                    pos[e] += 1
                    progress = True
        for e in ENG:
            assert pos[e] == len(stream[e]), ("schedule deadlock", e, pos[e], len(stream[e]))


class Builder:
    def __init__(self, debug=False, npool=2560):
        nc = bass.Bass("TRN2", target_bir_lowering=False)
        self.nc = nc
        self.P = Prog(nc)
        P = self.P

        def din(name, shape, dt=F32):
            return nc.dram_tensor(name, list(shape), dt, kind="ExternalInput").ap()

        def dout(name, shape):
            return nc.dram_tensor(name, list(shape), F32, kind="ExternalOutput").ap()

        self.xall = din("xall", [SEQ, D])
        self.xown = din("xown", [SEQ // 2, D])
        self.xs = din("xs", [NSEQ_S * TS, D])
        self.pt = din("pt", [1, NSEQ_S * NPAGES], I32)
        self.ckva = din("ckva", [npool, 128, 256])
        self.cki = din("cki", [npool, 128, 64])
        self.ckvb = din("ckvb", [npool, 128, 1024])
        self.w_in = din("w_in", [D, 4936])
        self.w_bra = din("w_bra", [512, D])
        self.w_brb = din("w_brb", [512, D])
        self.w_o = din("w_o", [D, D])
        self.w_g = din("w_g", [D, DFF])
        self.w_u = din("w_u", [D, DFF])
        self.w_d = din("w_d", [DFF, D])
        self.gains = din("gains", [3, D])
        self.cs_all = din("cs_all", [SEQ, 64])
        self.cs_own = din("cs_own", [SEQ // 2, 64])
        self.cs_s = din("cs_s", [TS, 64])
        self.msb = din("msb", [8, 128, 512])
        self.mdsa = din("mdsa", [4, 128, 1024])
        self.msb_s = din("msb_s", [TS, TS])
        self.md_s = din("md_s", [TS, TS])
        self.consts = din("consts", [3, 128, 128])
        self.pcol = din("pcol", [128, 1])
        self.y_own = dout("y_own", [SEQ // 2, D])
        self.y_s = dout("y_s", [NSEQ_S * TS, D])
        self.kv_all = dout("kv_all", [SEQ, 1344])
        self.kv_s = dout("kv_s", [NSEQ_S * TS, 1344])
        if debug:
            self.dbg_ob = dout("dbg_ob", [128, 4 * 2064])
            self.dbg_oa = dout("dbg_oa", [128, 4 * 2064])
            self.dbg_st = dout("dbg_st", [128, 8])
            self.dbg_sc = dout("dbg_sc", [128, 1024])

        self.ident = P.sb("ident", [128, 128], BF16)
        self.negtri = P.sb("negtri", [128, 128], BF16)
        self.negones = P.sb("negones", [128, 128], BF16)
        self.bconst = Buf()
        self.g = P.sb("g", [128, 3, D], F32)
        self.bg = Buf()
        self.psT = [P.ps("psT%d" % i, [128, 1024], BF16) for i in range(2)]
        self.bpsT = [Buf(), Buf()]
        self.psA = [P.ps("psA%d" % i, [128, 512], F32) for i in range(3)]
        self.bpsA = [Buf(), Buf(), Buf()]
        self.psB = [P.ps("psB%d" % i, [128, 512], F32) for i in range(2)]
        self.bpsB = [Buf(), Buf()]
        self.psO = P.ps("psO", [128, 512], F32)
        self.bpsO = Buf()
        self.arena = P.sb("arena", [128, 49152], BF16)
        self.oaT = P.sb("oaT", [128, 4 * 2064], BF16)
        self.obT = P.sb("obT", [128, 4 * 2064], BF16)
        self.boaT = [Buf() for _ in range(5)]
        self.bobT = [Buf() for _ in range(5)]
        self.st = P.sb("st", [128, 8], F32)
        self.bst = Buf()
        self.SCR = 33280
        self.scr = P.sb("scr", [128, self.SCR], BF16)
        self.sptr = 0
        self.nxt = 0

    def sreset(self):
        self.P.barrier()
        self.sptr = 0

    def salloc(self, cols, dt):
        n = cols * (2 if dt == F32 else 1)
        n = (n + 1) // 2 * 2
        assert self.sptr + n <= self.SCR, ("scratch overflow", self.sptr, n)
        v = self.scr[:, self.sptr:self.sptr + n]
        self.sptr += n
        return (v.bitcast(F32) if dt == F32 else v), Buf()

    def salloc_n(self, k, cols, dt):
        r = [self.salloc(cols, dt) for _ in range(k)]
        return [x[0] for x in r], [x[1] for x in r]

    def common_scratch(self, nxt=2):
        self.xt, self.bxt = self.salloc_n(nxt, D, F32)
        self.xn, self.bxn = self.salloc(D, BF16)
        self.junk, self.bjunk = self.salloc(D, BF16)

    @staticmethod
    def _fs(ap):
        n = 1
        for d in ap.shape[1:]:
            n *= int(d)
        return n

    def _cost(self, eng, ap):
        n = self._fs(ap)
        if eng == "pe":
            return 0.05 + n / 2400.0
        if eng == "act":
            return 0.2 + n / 960.0
        if eng == "dve":
            return 0.08 + n / 960.0
        return 0.1 + n / 480.0

    def mm(self, out, lhsT, rhs, start, stop, reads, writes):
        self.P.op("pe", lambda e: e.matmul(out, lhsT=lhsT, rhs=rhs, start=start, stop=stop),
                  reads, writes, skip_self=True, cost=self._cost("pe", out))

    def tr(self, out, in_, rows, reads, writes):
        ident = self.ident[:rows, :rows]
        self.P.op("pe", lambda e: e.transpose(out=out, in_=in_, identity=ident),
                  list(reads) + [self.bconst], writes, skip_self=True, cost=self._cost("pe", out))

    def act(self, out, in_, func, reads, writes, **kw):
        self.P.op("act", lambda e: e.activation(out=out, in_=in_, func=func, **kw), reads, writes,
                  cost=self._cost("act", in_))

    def copy(self, eng, out, in_, reads, writes):
        c = self._cost(eng, out)
        if eng == "act":
            self.P.op("act", lambda e: e.copy(out=out, in_=in_), reads, writes, cost=c)
        else:
            self.P.op(eng, lambda e: e.tensor_copy(out=out, in_=in_), reads, writes, cost=c)

    def tt(self, eng, out, in0, in1, op, reads, writes):
        self.P.op(eng, lambda e: e.tensor_tensor(out=out, in0=in0, in1=in1, op=op), reads, writes,
                  cost=self._cost(eng, out))

    def ts(self, eng, out, in0, s1, s2, op0, op1, reads, writes, accum=None):
        c = self._cost(eng, in0)
        if accum is None:
            if op1 is None:
                self.P.op(eng, lambda e: e.tensor_scalar(out=out, in0=in0, scalar1=s1, scalar2=None, op0=op0),
                          reads, writes, cost=c)
            else:
                self.P.op(eng, lambda e: e.tensor_scalar(out=out, in0=in0, scalar1=s1, scalar2=s2, op0=op0,
                                                         op1=op1), reads, writes, cost=c)
        else:
            self.P.op(eng, lambda e: e.tensor_scalar(out=out, in0=in0, scalar1=s1, scalar2=s2, op0=op0, op1=op1,
                                                     accum_out=accum), reads, writes, cost=c)

    def stt(self, eng, out, in0, scalar, in1, op0, op1, reads, writes, accum=None):
        c = self._cost(eng, in0)
        if accum is None:
            self.P.op(eng, lambda e: e.scalar_tensor_tensor(out=out, in0=in0, scalar=scalar, in1=in1, op0=op0,
                                                            op1=op1), reads, writes, cost=c)
        else:
            self.P.op(eng, lambda e: e.scalar_tensor_tensor(out=out, in0=in0, scalar=scalar, in1=in1, op0=op0,
                                                            op1=op1, accum_out=accum), reads, writes, cost=c)

    def memset(self, eng, ap, val, writes):
        self.P.op(eng, lambda e: e.memset(ap, val), (), writes, cost=self._cost(eng, ap))

    def load_consts(self):
        P = self.P
        P.dma("pool", self.ident[:], self.consts[0], writes=[self.bconst])
        P.dma("pool", self.negtri[:], self.consts[1], writes=[self.bconst])
        P.dma("pool", self.negones[:], self.consts[2], writes=[self.bconst])
        for i in range(3):
            P.dma("sp", self.g[:, i, :], self.gains[i].partition_broadcast(128), writes=[self.bg])

    def load_w(self, dst, src_rows_view, c0, c1, nk, wb):
        src = src_rows_view.rearrange("(k p) n -> p k n", p=128)[:, :, c0:c1]
        self.P.dma("pool", dst, src, writes=[wb])

    def norm_T(self, x_ap, bx, rows, gi, dest, bdest, ncols_dest_off=0):
        ss, rstd = self.st[:rows, 0:1], self.st[:rows, 1:2]
        self.memset("pool", ss, 0.0, [self.bst])
        self.act(self.junk[:rows, :], x_ap, AF.Square, [bx, self.bst], [self.bjunk, self.bst], accum_out=ss)
        self.act(rstd, ss, AF.Ln, [self.bst], [self.bst], scale=1.0 / D, bias=1e-6)
        self.act(rstd, rstd, AF.Exp, [self.bst], [self.bst], scale=-0.5)
        self.stt("dve", self.xn[:rows, :], x_ap, rstd, self.g[:rows, gi, :], ALU.mult, ALU.mult,
                 [bx, self.bst, self.bg], [self.bxn])
        k = self.nxt = (self.nxt + 1) % 2
        ps = self.psT[k][:].rearrange("p (c t) -> p c t", c=8)
        for kc in range(8):
            self.tr(ps[:, kc, 0:rows], self.xn[:rows, kc * 128:(kc + 1) * 128], rows, [self.bxn], [self.bpsT[k]])
        self.copy("act", dest, ps[:, :, 0:rows], [self.bpsT[k]], [bdest])

    def rope(self, src, dst, cs, rows, h, reads, writes):
        x1, x2 = src[:, :, 0:32], src[:, :, 32:64]
        cb = cs[:rows, 0:32].unsqueeze(1).to_broadcast([rows, h, 32])
        sb = cs[:rows, 32:64].unsqueeze(1).to_broadcast([rows, h, 32])
        t = [self.rt[i][:rows, 0:h * 32].rearrange("p (h j) -> p h j", h=h) for i in range(4)]
        bt = self.brt
        self.tt("dve", t[0], x1, cb, ALU.mult, reads, [bt[0]])
        self.tt("dve", t[1], x2, sb, ALU.mult, reads, [bt[1]])
        self.tt("dve", t[2], x2, cb, ALU.mult, reads, [bt[2]])
        self.tt("dve", t[3], x1, sb, ALU.mult, reads, [bt[3]])
        self.tt("pool", dst[:, :, 0:32], t[0], t[1], ALU.subtract, [bt[0], bt[1]], writes)
        self.tt("pool", dst[:, :, 32:64], t[2], t[3], ALU.add, [bt[2], bt[3]], writes)

    def kv_build(self, kvst, bkv, rows, dKa=None, dKb=None, dVa=None, dVb=None, bdest=None):
        if dVa is not None:
            self.copy("pool", dVa, kvst[:rows, 128:256], [bkv], [bdest])
        if dVb is not None:
            self.copy("pool", dVb, kvst[:rows, 832:1344], [bkv], [bdest])
        c0 = 0 if dKa is not None else 320
        c1 = 832 if dKb is not None else 320
        self.copy("pool", self.kst[:rows, c0:c1], kvst[:rows, c0:c1], [bkv], [self.bkst])
        k = self.nxt = (self.nxt + 1) % 2
        ps = self.psT[k][:].rearrange("p (c t) -> p c t", c=8)
        if dKa is not None:
            for s, col in enumerate((0, 64, 256)):
                for half in range(2):
                    self.tr(ps[half * 64:(half + 1) * 64, s, 0:rows], self.kst[:rows, col:col + 64], rows,
                            [self.bkst], [self.bpsT[k]])
            self.copy("act", dKa, ps[:, 0:3, 0:rows], [self.bpsT[k]], [bdest])
        if dKb is not None:
            for j in range(4):
                self.tr(ps[:, 3 + j, 0:rows], self.kst[:rows, 320 + 128 * j:320 + 128 * (j + 1)], rows,
                        [self.bkst], [self.bpsT[k]])
            self.copy("act", dKb, ps[:, 3:7, 0:rows], [self.bpsT[k]], [bdest])

    def kv_proj(self, xnT, bxnT, rows, cs, bcs, kvst, bkv, wkv):
        groups = ((0, 320, 0), (320, 832, 1), (832, 1344, 2))
        for c0, c1, a in groups:
            for kc in range(8):
                self.mm(self.psA[a][:rows, 0:c1 - c0], xnT[:, kc, 0:rows], wkv[:, kc, c0:c1], kc == 0, kc == 7,
                        [bxnT, self.bwbuf], [self.bpsA[a]])
        p0 = self.psA[0]
        self.rope(p0[:rows, 0:128].rearrange("p (h j) -> p h j", h=2),
                  kvst[:rows, 0:128].rearrange("p (h j) -> p h j", h=2), cs, rows, 2,
                  [self.bpsA[0], bcs], [bkv])
        self.rope(p0[:rows, 256:320].rearrange("p (h j) -> p h j", h=1),
                  kvst[:rows, 256:320].rearrange("p (h j) -> p h j", h=1), cs, rows, 1,
                  [self.bpsA[0], bcs], [bkv])
        self.copy("act", kvst[:rows, 128:256], p0[:rows, 128:256], [self.bpsA[0]], [bkv])
        self.copy("act", kvst[:rows, 320:832], self.psA[1][:rows, :], [self.bpsA[1]], [bkv])
        self.copy("dve", kvst[:rows, 832:1344], self.psA[2][:rows, :], [self.bpsA[2]], [bkv])

    def sb_run(self, W, qT, bq, blocks, Rb, bRb, oacc, boacc, first, last, dest, bdest, part=None, bpart=None):
        n = 0
        for h in range(8):
            for bi, blk in enumerate(blocks):
                nk = blk["nk"]
                isfirst = first and bi == 0
                islast = last and bi == len(blocks) - 1
                n += 1
                za, zb = self.psA[n % 2], self.bpsA[n % 2]
                ca, cb = self.psB[n % 2], self.bpsB[n % 2]
                e_sb, be = self.wk[n % 2], self.bwk[n % 2]
                L, bL = self.wkb[n % 2], self.bwkb[n % 2]
                a_sb, ba = self.wkb[2 + n % 2], self.bwkb[2 + n % 2]
                kT, q = blk["kT"](h), qT(h)
                self.mm(za[:nk, 0:W], kT, q, True, True, [blk["bk"], bq], [zb])
                self.act(e_sb[:nk, 0:W], za[:nk, 0:W], AF.Exp, [zb], [be], scale=0.125)
                self.act(L[:nk, 0:W], e_sb[:nk, 0:W], AF.Ln, [be], [bL], bias=1.0)
                if blk["mask"] is not None:
                    self.tt("pool", L[:nk, 0:W], L[:nk, 0:W], blk["mask"], ALU.mult, [bL, blk["bm"]], [bL])
                self.mm(ca[:nk, 0:W], kT, q, True, False, [blk["bk"], bq], [cb])
                self.mm(ca[:nk, 0:W], self.negtri[:nk, :nk], L[:nk, 0:W], False, isfirst, [bL, self.bconst], [cb])
                if not isfirst:
                    self.mm(ca[:nk, 0:W], self.negones[:, :nk], Rb(h), False, True, [bRb(h), self.bconst], [cb])
                self.act(a_sb[:nk, 0:W], ca[:nk, 0:W], AF.Exp, [cb], [ba], scale=0.125)
                if blk["mask"] is not None:
                    self.tt("pool", a_sb[:nk, 0:W], a_sb[:nk, 0:W], blk["mask"], ALU.mult, [ba, blk["bm"]], [ba])
                self.mm(oacc(h), blk["v"](h), a_sb[:nk, 0:W], bi == 0, bi == len(blocks) - 1, [blk["bk"], ba],
                        [boacc(h)])
                if isfirst:
                    self.memset("pool", Rb(h), 0.0, [bRb(h)])
                    self.copy("pool", Rb(h)[:nk, :], L[:nk, 0:W], [bL], [bRb(h)])
                elif not islast:
                    self.tt("pool", Rb(h)[:nk, :], Rb(h)[:nk, :], L[:nk, 0:W], ALU.add, [bL, bRb(h)], [bRb(h)])
            if first and last:
                self.copy("act", dest(h), oacc(h), [boacc(h)], [bdest])
            elif first:
                self.copy("act", part(h), oacc(h), [boacc(h)], [bpart])
            elif last:
                self.tt("dve", dest(h), oacc(h), part(h), ALU.add, [boacc(h), bpart], [bdest])
            else:
                self.tt("dve", part(h), oacc(h), part(h), ALU.add, [boacc(h), bpart], [bpart])

    def dsa_tile(self, rows, qaT, qiT, bq, w_t, bw, chunks, score, bscore, otok, botok):
        P = self.P
        N = sum(c["nk"] for c in chunks)
        col = 0
        cols = []
        for c_ in chunks:
            nk, k0 = c_["nk"], c_["k0"]
            cols.append(col)
            for h in range(8):
                hp = (h % 2) * 64
                pa, bpa = self.psA[h % 2], self.bpsA[h % 2]
                r, br = self.wk[h % 2], self.bwk[h % 2]
                self.mm(pa[:rows, 0:nk], qiT(h), c_["KT"][hp:hp + 64, 2, k0:k0 + nk], True, True,
                        [bq] + c_["bk"], [bpa])
                self.act(r[:rows, 0:nk], pa[:rows, 0:nk], AF.Relu, [bpa], [br])
                sc = score[:rows, col:col + nk]
                if h == 0:
                    self.ts("dve", sc, r[:rows, 0:nk], w_t[:rows, 0:1], None, ALU.mult, None, [br, bw], [bscore])
                else:
                    self.stt("dve", sc, r[:rows, 0:nk], w_t[:rows, h:h + 1], sc, ALU.mult, ALU.add,
                             [br, bw, bscore], [bscore])
            col += nk
        st = self.st
        lo, w0, mid, cnt, tq, rmax = (st[:rows, i:i + 1] for i in range(2, 8))
        bs = self.bst
        sc = score[:rows, 0:N]
        sel = self.sel[:rows, 0:N]
        self.ts("dve", sel, sc, -1.0, -3.0e38, ALU.mult, ALU.max, [bscore, bs], [self.bsel, bs], accum=lo)
        for ci, c_ in enumerate(chunks):
            if c_["mb"] is not None:
                s_ = score[:rows, cols[ci]:cols[ci] + c_["nk"]]
                self.tt("pool", s_, s_, c_["mb"], ALU.add, [bscore, c_["bmb"]], [bscore])
        P.op("dve", lambda e: e.reduce_max(out=rmax, in_=sc, axis=AX.X), [bscore], [bs],
             cost=self._cost("dve", sc))
        self.ts("dve", lo, lo, -1.0, -1.0, ALU.mult, ALU.add, [bs], [bs])
        self.tt("dve", w0, rmax, lo, ALU.subtract, [bs], [bs])
        for it in range(NBIS):
            hstep = 2.0 ** -(it + 1)
            self.stt("dve", mid, w0, hstep, lo, ALU.mult, ALU.add, [bs], [bs])
            self.ts("dve", sel, sc, mid, 0.0, ALU.is_gt, ALU.add, [bscore, bs], [self.bsel, bs], accum=cnt)
            self.ts("dve", tq, cnt, float(TOPK), hstep, ALU.is_ge, ALU.mult, [bs], [bs])
            self.stt("dve", lo, tq, w0, lo, ALU.mult, ALU.add, [bs], [bs])
        self.ts("dve", sel, sc, lo, None, ALU.is_gt, None, [bscore, bs], [self.bsel])
        den = self.den
        nch = len(chunks)
        nblk_total = sum((c_["nk"] + 127) // 128 for c_ in chunks)
        for h in range(8):
            hp = (h % 2) * 64
            c = h // 4
            bdone = 0
            for ci, c_ in enumerate(chunks):
                nk, k0 = c_["nk"], c_["k0"]
                pa, bpa = self.psA[ci % 2], self.bpsA[ci % 2]
                p_sb, bp = self.wkb[ci % 2], self.bwkb[ci % 2]
                pm, bpm = self.wkb[2 + ci % 2], self.bwkb[2 + ci % 2]
                pT, bpT = self.wkb[4 + ci % 2], self.bwkb[4 + ci % 2]
                self.mm(pa[:rows, 0:nk], qaT(h), c_["KT"][hp:hp + 64, c, k0:k0 + nk], True, True,
                        [bq] + c_["bk"], [bpa])
                self.act(p_sb[:rows, 0:nk], pa[:rows, 0:nk], AF.Exp, [bpa], [bp], scale=0.125)
                self.stt("dve", pm[:rows, 0:nk], p_sb[:rows, 0:nk], 1.0, sel[:, cols[ci]:cols[ci] + nk], ALU.mult,
                         ALU.mult, [bp, self.bsel, self.bden], [bpm, self.bden], accum=den[:rows, ci:ci + 1])
                k = self.nxt = (self.nxt + 1) % 2
                ps = self.psT[k][:].rearrange("p (c t) -> p c t", c=8)
                nb = (nk + 127) // 128
                pTv = pT[:, 0:4 * 128].rearrange("p (c t) -> p c t", c=4)
                for j in range(nb):
                    kk = min(128, nk - j * 128)
                    self.tr(ps[:kk, j, 0:rows], pm[:rows, j * 128:j * 128 + kk], rows, [bpm], [self.bpsT[k]])
                kk_last = nk - (nb - 1) * 128
                if kk_last == 128:
                    self.copy("act", pTv[:, 0:nb, 0:rows], ps[:, 0:nb, 0:rows], [self.bpsT[k]], [bpT])
                else:
                    if nb > 1:
                        self.copy("act", pTv[:, 0:nb - 1, 0:rows], ps[:, 0:nb - 1, 0:rows], [self.bpsT[k]], [bpT])
                    self.copy("act", pTv[:kk_last, nb - 1, 0:rows], ps[:kk_last, nb - 1, 0:rows], [self.bpsT[k]],
                              [bpT])
                for j in range(nb):
                    kk = min(128, nk - j * 128)
                    self.mm(self.psO[:rows, 0:64], pTv[:kk, j, 0:rows], c_["va"](j)[:kk, c * 64:(c + 1) * 64],
                            bdone == 0, bdone == nblk_total - 1, [bpT] + c_["bk"], [self.bpsO])
                    bdone += 1
            dsum, rden = st[:rows, 0:1], st[:rows, 1:2]
            P.op("dve", lambda e, dsum=dsum: e.reduce_sum(out=dsum, in_=den[:rows, 0:nch], axis=AX.X),
                 [self.bden], [bs])
            P.op("dve", lambda e, dsum=dsum, rden=rden: e.reciprocal(out=rden, in_=dsum), [bs], [bs])
            self.ts("dve", otok[:rows, h * 64:(h + 1) * 64], self.psO[:rows, 0:64], rden, None, ALU.mult, None,
                    [self.bpsO, bs], [botok])


def _rope_tables(pos):
    half = 32
    inv = (np.float32(10000.0) ** (-(np.arange(half, dtype=np.float32)) / np.float32(half))).astype(np.float32)
    ang = pos.astype(np.float32)[:, None] * inv[None, :]
    return np.concatenate([np.cos(ang), np.sin(ang)], axis=1).astype(np.float32)


def _phase1(B):
    P = B.P
    B.sreset()
    B.common_scratch(2)
    B.wbuf, B.bwbuf = B.salloc(8 * 1344, BF16)
    B.kvst, B.bkvst = B.salloc_n(2, 1344, F32)
    B.rt, B.brt = B.salloc_n(4, 256, F32)
    B.cs, B.bcs = B.salloc_n(2, 64, F32)
    B.kst, B.bkst = B.salloc(832, BF16)
    xnT1, bxnT1 = B.salloc(1024, BF16)
    wkv = B.wbuf.rearrange("p (k n) -> p k n", k=8)
    B.load_w(wkv[:, :, 0:256], B.w_in, 512, 768, 8, B.bwbuf)
    B.load_w(wkv[:, :, 256:320], B.w_in, 1280, 1344, 8, B.bwbuf)
    B.load_w(wkv[:, :, 320:1344], B.w_in, 1864, 2888, 8, B.bwbuf)
    B.KT = B.arena[:, 0:7 * SEQ].rearrange("p (s k) -> p s k", s=7)
    B.Vb = B.arena[:, 28672:28672 + 16384].rearrange("p (b n) -> p b n", b=32)
    B.Va = B.arena[:, 45056:45056 + 4096].rearrange("p (b n) -> p b n", b=32)
    B.bkvblk = [Buf() for _ in range(32)]
    xnT = xnT1.rearrange("p (c t) -> p c t", c=8)
    for tt in range(32):
        xt, bxt = B.xt[tt % 2], B.bxt[tt % 2]
        cs, bcs = B.cs[tt % 2], B.bcs[tt % 2]
        kvst, bkv = B.kvst[tt % 2], B.bkvst[tt % 2]
        P.dma("sp", xt[:, :], B.xall[tt * 128:(tt + 1) * 128, :], writes=[bxt])
        P.dma("sp", cs[:, :], B.cs_all[tt * 128:(tt + 1) * 128, :], writes=[bcs])
        B.norm_T(xt[:, :], bxt, 128, 0, xnT, bxnT1)
        B.kv_proj(xnT, bxnT1, 128, cs, bcs, kvst, bkv, wkv)
        P.dma("sp", B.kv_all[tt * 128:(tt + 1) * 128, :], kvst[:, :], reads=[bkv])
        B.kv_build(kvst, bkv, 128, dKa=B.KT[:, 0:3, tt * 128:(tt + 1) * 128],
                   dKb=B.KT[:, 3:7, tt * 128:(tt + 1) * 128], dVa=B.Va[:, tt, :], dVb=B.Vb[:, tt, :],
                   bdest=B.bkvblk[tt])


def _phase_sb(B):
    P = B.P
    B.sreset()
    B.common_scratch(2)
    B.wbuf, B.bwbuf = B.salloc(8 * 512, BF16)
    xnTg, bxnTg = B.salloc(8 * 512, BF16)
    qT, bqT = B.salloc(4 * 512, BF16)
    B.wk, B.bwk = B.salloc_n(2, 512, F32)
    B.wkb, B.bwkb = B.salloc_n(4, 512, BF16)
    rb, brb = B.salloc_n(2, 512, BF16)
    mask, bmask = B.salloc(8 * 512, BF16)
    wqb = B.wbuf.rearrange("p (k n) -> p k n", k=8)
    B.load_w(wqb, B.w_in, 1352, 1864, 8, B.bwbuf)
    maskv = mask.rearrange("p (j t) -> p j t", j=8)
    P.dma("pool", maskv, B.msb.rearrange("j p t -> p j t"), writes=[bmask])
    xnTv = xnTg.rearrange("p (c t) -> p c t", c=8)
    qTv = qT.rearrange("p (c t) -> p c t", c=4)
    obTv = B.obT[:, :].rearrange("p (c t) -> p c t", c=4)
    for J in range(4):
        for i in range(4):
            ot = J * 4 + i
            xt, bxt = B.xt[ot % 2], B.bxt[ot % 2]
            P.dma("sp", xt[:, :], B.xown[ot * 128:(ot + 1) * 128, :], writes=[bxt])
            B.norm_T(xt[:, :], bxt, 128, 0, xnTv[:, :, i * 128:(i + 1) * 128], bxnTg)
        for ch in range(4):
            for kc in range(8):
                B.mm(B.psB[ch % 2][:, :], wqb[:, kc, ch * 128:(ch + 1) * 128], xnTv[:, kc, :], kc == 0, kc == 7,
                     [B.bwbuf, bxnTg], [B.bpsB[ch % 2]])
            B.copy("dve", qTv[:, ch, :], B.psB[ch % 2][:, :], [B.bpsB[ch % 2]], [bqT])
        blocks = []
        for kb in range(8 * J + 7, -1, -1):
            j = kb - 8 * J
            blocks.append(dict(
                nk=128,
                kT=(lambda h, kb=kb: B.KT[(h % 2) * 64:(h % 2) * 64 + 64, 3 + h // 2, kb * 128:(kb + 1) * 128]),
                v=(lambda h, kb=kb: B.Vb[:, kb, h * 64:(h + 1) * 64]),
                mask=(maskv[:, j, :] if j >= 0 else None), bk=B.bkvblk[kb], bm=bmask))
        B.sb_run(512, (lambda h: qTv[(h % 2) * 64:(h % 2) * 64 + 64, h // 2, :]), bqT, blocks,
                 (lambda h: rb[h % 2]), (lambda h: brb[h % 2]),
                 (lambda h: (B.psO if h % 2 == 0 else B.psA[2])[0:64, :]),
                 (lambda h: (B.bpsO if h % 2 == 0 else B.bpsA[2])), True, True,
                 (lambda h, J=J: obTv[(h % 2) * 64:(h % 2) * 64 + 64, h // 2, J * 512:(J + 1) * 512]), B.bobT[J])


def _phase_dsa(B):
    P = B.P
    B.sreset()
    B.common_scratch(2)
    B.wbuf, B.bwbuf = B.salloc(8 * 1032, BF16)
    xnT1, bxnT1 = B.salloc(1024, BF16)
    B.cs, B.bcs = B.salloc_n(2, 64, F32)
    B.rt, B.brt = B.salloc_n(4, 256, F32)
    qst, bqst = B.salloc(1024, BF16)
    qTt, bqTt = B.salloc(1024, BF16)
    w_t, bw = B.salloc(8, F32)
    B.wk, B.bwk = B.salloc_n(2, 512, F32)
    B.wkb, B.bwkb = B.salloc_n(6, 512, BF16)
    maskD, bmaskD = B.salloc_n(2, 1024, F32)
    otok, botok = B.salloc(512, BF16)
    B.den, B.bden = B.salloc(32, F32)
    score = B.arena[:, 28672:28672 + 8192].bitcast(F32)
    bscore = Buf()
    B.sel, B.bsel = B.arena[:, 36864:36864 + 4096], Buf()
    wq = B.wbuf.rearrange("p (k n) -> p k n", k=8)
    B.load_w(wq[:, :, 0:512], B.w_in, 0, 512, 8, B.bwbuf)
    B.load_w(wq[:, :, 512:1024], B.w_in, 768, 1280, 8, B.bwbuf)
    B.load_w(wq[:, :, 1024:1032], B.w_in, 1344, 1352, 8, B.bwbuf)
    xnT = xnT1.rearrange("p (c t) -> p c t", c=8)
    qTv = qTt.rearrange("p (c t) -> p c t", c=8)
    oaTv = B.oaT[:, :].rearrange("p (c t) -> p c t", c=4)
    wscale = float(8.0 ** -0.5 * 64.0 ** -0.5)
    for ot in range(16):
        J, i = ot // 4, ot % 4
        xt, bxt = B.xt[ot % 2], B.bxt[ot % 2]
        cs, bcs = B.cs[ot % 2], B.bcs[ot % 2]
        md, bmd = maskD[ot % 2], bmaskD[ot % 2]
        P.dma("sp", xt[:, :], B.xown[ot * 128:(ot + 1) * 128, :], writes=[bxt])
        P.dma("sp", cs[:, :], B.cs_own[ot * 128:(ot + 1) * 128, :], writes=[bcs])
        P.dma("sp", md[:, :], B.mdsa[i], writes=[bmd])
        B.norm_T(xt[:, :], bxt, 128, 0, xnT, bxnT1)
        for a, (c0, c1) in enumerate(((0, 512), (512, 1024))):
            for kc in range(8):
                B.mm(B.psA[a][:, :], xnT[:, kc, :], wq[:, kc, c0:c1], kc == 0, kc == 7, [bxnT1, B.bwbuf],
                     [B.bpsA[a]])
        for kc in range(8):
            B.mm(B.psB[0][:, 0:8], xnT[:, kc, :], wq[:, kc, 1024:1032], kc == 0, kc == 7, [bxnT1, B.bwbuf],
                 [B.bpsB[0]])
        for a in range(2):
            B.rope(B.psA[a][:, :].rearrange("p (h j) -> p h j", h=8),
                   qst[:, a * 512:(a + 1) * 512].rearrange("p (h j) -> p h j", h=8), cs, 128, 8,
                   [B.bpsA[a], bcs], [bqst])
        B.ts("dve", w_t[:, :], B.psB[0][:, 0:8], wscale, None, ALU.mult, None, [B.bpsB[0]], [bw])
        k = B.nxt = (B.nxt + 1) % 2
        ps = B.psT[k][:].rearrange("p (c t) -> p c t", c=8)
        for ch in range(8):
            B.tr(ps[:, ch, :], qst[:, ch * 128:(ch + 1) * 128], 128, [bqst], [B.bpsT[k]])
        B.copy("act", qTv, ps, [B.bpsT[k]], [bqTt])
        chunks = []
        for ci in range(2 * (J + 1)):
            mb = md[:, (ci - 2 * J) * 512:(ci - 2 * J + 1) * 512] if ci >= 2 * J else None
            chunks.append(dict(KT=B.KT, k0=ci * 512, nk=512, va=(lambda j, ci=ci: B.Va[:, ci * 4 + j, :]), mb=mb,
                               bmb=bmd, bk=B.bkvblk[ci * 4:ci * 4 + 4]))
        B.dsa_tile(128, (lambda h: qTv[(h % 2) * 64:(h % 2) * 64 + 64, h // 2, :]),
                   (lambda h: qTv[(h % 2) * 64:(h % 2) * 64 + 64, 4 + h // 2, :]), bqTt, w_t, bw, chunks,
                   score, bscore, otok, botok)
        k = B.nxt = (B.nxt + 1) % 2
        ps = B.psT[k][:].rearrange("p (c t) -> p c t", c=8)
        for ch in range(4):
            B.tr(ps[:, ch, :], otok[:, ch * 128:(ch + 1) * 128], 128, [botok], [B.bpsT[k]])
        B.copy("act", oaTv[:, :, ot * 128:(ot + 1) * 128], ps[:, 0:4, :], [B.bpsT[k]], [B.boaT[J]])


def _sample_small(B):
    t = {}
    t["KTn"], _ = B.salloc(7 * 16, BF16)
    t["Van"], _ = B.salloc(NSEQ_S * 128, BF16)
    t["Vbn"], _ = B.salloc(NSEQ_S * 512, BF16)
    t["QbTs"], _ = B.salloc(4 * 16, BF16)
    t["qTts"], _ = B.salloc(8 * 16, BF16)
    t["wts"], _ = B.salloc(NSEQ_S * 8, F32)
    t["xnTs"], _ = B.salloc(8 * 16, BF16)
    t["rbs"], _ = B.salloc(8 * TS, BF16)
    t["part"], _ = B.salloc(8 * TS, F32)
    t["pts"], _ = B.salloc(NSEQ_S * NPAGES, I32 if False else F32)
    t["msbs"], _ = B.salloc(TS, BF16)
    t["mds"], _ = B.salloc(TS, F32)
    if not hasattr(B, "bsm"):
        B.bsm = {k: Buf() for k in t}
    return t


def _dyn_page_dma(B, dst, cache, pts, idx, reads, writes):
    rows = cache.rearrange("n p d -> (n p) d")

    def fn(e):
        return e.indirect_dma_start(out=dst, out_offset=None, in_=rows,
                                    in_offset=bass.IndirectOffsetOnAxis(ap=pts[:, idx:idx + 1], axis=0))
    B.P.dma_fn("pool", fn, reads, writes)


def _phase_sample(B):
    P = B.P
    wscale = float(8.0 ** -0.5 * 64.0 ** -0.5)
    B.sreset()
    t = _sample_small(B)
    bsm = B.bsm
    B.common_scratch(1)
    wall, bwall = B.salloc(8 * 1544, BF16)
    B.kvst, B.bkvst = B.salloc_n(2, 1344, F32)
    B.rt, B.brt = B.salloc_n(4, 256, F32)
    B.cs, B.bcs = B.salloc_n(1, 64, F32)
    B.kst, B.bkst = B.salloc(832, BF16)
    qst, bqst = B.salloc(1024, BF16)
    wkv = wall[:, 0:8 * 1344].rearrange("p (k n) -> p k n", k=8)
    B.load_w(wkv[:, :, 0:256], B.w_in, 512, 768, 8, bwall)
    B.load_w(wkv[:, :, 256:320], B.w_in, 1280, 1344, 8, bwall)
    B.load_w(wkv[:, :, 320:1344], B.w_in, 1864, 2888, 8, bwall)
    pts = t["pts"].bitcast(I32)
    ptb_f, bptb = B.salloc(NSEQ_S * NPAGES, F32)
    ptf, bptf = B.salloc(NSEQ_S * NPAGES, F32)
    pcol, bpcol = B.salloc(1, F32)
    ptb = ptb_f.bitcast(I32)
    P.dma("sp", ptb, B.pt[0].partition_broadcast(128), writes=[bptb])
    P.dma("sp", pcol, B.pcol, writes=[bpcol])
    B.copy("dve", ptf, ptb, [bptb], [bptf])
    B.ts("dve", ptf, ptf, 128.0, pcol[:, 0:1], ALU.mult, ALU.add, [bptf, bpcol], [bptf])
    B.copy("dve", pts, ptf, [bptf], [bsm["pts"]])
    P.dma("pool", t["msbs"][:TS, :], B.msb_s, writes=[bsm["msbs"]])
    P.dma("sp", t["mds"][:TS, :], B.md_s, writes=[bsm["mds"]])
    KTn = t["KTn"].rearrange("p (s k) -> p s k", s=7)
    xnTs = t["xnTs"].rearrange("p (c t) -> p c t", c=8)
    QbTs = t["QbTs"].rearrange("p (c t) -> p c t", c=4)
    qTts = t["qTts"].rearrange("p (c t) -> p c t", c=8)
    cs, bcs = B.cs[0], B.bcs[0]
    P.dma("sp", cs[:TS, :], B.cs_s[:, :], writes=[bcs])
    xt, bxt = B.xt[0], B.bxt[0]
    B.bwbuf = bwall
    for q in range(NSEQ_S):
        qs = slice(q * TS, (q + 1) * TS)
        kvst, bkv = B.kvst[q % 2], B.bkvst[q % 2]
        P.dma("sp", xt[:TS, :], B.xs[q * TS:(q + 1) * TS, :], writes=[bxt])
        B.norm_T(xt[:TS, :], bxt, TS, 0, xnTs[:, :, qs], bsm["xnTs"])
        B.kv_proj(xnTs[:, :, qs], bsm["xnTs"], TS, cs, bcs, kvst, bkv, wkv)
        P.dma("sp", B.kv_s[q * TS:(q + 1) * TS, :], kvst[:TS, :], reads=[bkv])
        B.kv_build(kvst, bkv, TS, dKa=KTn[:, 0:3, qs], dKb=KTn[:, 3:7, qs],
                   dVa=t["Van"][:TS, q * 128:(q + 1) * 128], dVb=t["Vbn"][:TS, q * 512:(q + 1) * 512],
                   bdest=bsm["KTn"])
    wqb = wall[:, 0:8 * 512].rearrange("p (k n) -> p k n", k=8)
    wq = wall[:, 8 * 512:8 * 1544].rearrange("p (k n) -> p k n", k=8)
    bwqb = bwq = bwall
    B.load_w(wqb, B.w_in, 1352, 1864, 8, bwall)
    B.load_w(wq[:, :, 0:512], B.w_in, 0, 512, 8, bwall)
    B.load_w(wq[:, :, 512:1024], B.w_in, 768, 1280, 8, bwall)
    B.load_w(wq[:, :, 1024:1032], B.w_in, 1344, 1352, 8, bwall)
    for q in range(NSEQ_S):
        qs = slice(q * TS, (q + 1) * TS)
        for ch in range(4):
            for kc in range(8):
                B.mm(B.psB[ch % 2][:, 0:TS], wqb[:, kc, ch * 128:(ch + 1) * 128], xnTs[:, kc, qs], kc == 0, kc == 7,
                     [bwqb, bsm["xnTs"]], [B.bpsB[ch % 2]])
            B.copy("dve", QbTs[:, ch, qs], B.psB[ch % 2][:, 0:TS], [B.bpsB[ch % 2]], [bsm["QbTs"]])
        for a, (c0, c1) in enumerate(((0, 512), (512, 1024))):
            for kc in range(8):
                B.mm(B.psA[a][:TS, :], xnTs[:, kc, qs], wq[:, kc, c0:c1], kc == 0, kc == 7, [bsm["xnTs"], bwq],
                     [B.bpsA[a]])
        for kc in range(8):
            B.mm(B.psB[0][:TS, 0:8], xnTs[:, kc, qs], wq[:, kc, 1024:1032], kc == 0, kc == 7, [bsm["xnTs"], bwq],
                 [B.bpsB[0]])
        for a in range(2):
            B.rope(B.psA[a][:TS, :].rearrange("p (h j) -> p h j", h=8),
                   qst[:TS, a * 512:(a + 1) * 512].rearrange("p (h j) -> p h j", h=8), cs, TS, 8,
                   [B.bpsA[a], bcs], [bqst])
        B.ts("dve", t["wts"][:TS, q * 8:(q + 1) * 8], B.psB[0][:TS, 0:8], wscale, None, ALU.mult, None,
             [B.bpsB[0]], [bsm["wts"]])
        k = B.nxt = (B.nxt + 1) % 2
        ps = B.psT[k][:].rearrange("p (c t) -> p c t", c=8)
        for ch in range(8):
            B.tr(ps[:, ch, 0:TS], qst[:TS, ch * 128:(ch + 1) * 128], TS, [bqst], [B.bpsT[k]])
        B.copy("act", qTts[:, :, qs], ps[:, :, 0:TS], [B.bpsT[k]], [bsm["qTts"]])
    oaTv = B.oaT[:, :].rearrange("p (c t) -> p c t", c=4)
    obTv = B.obT[:, :].rearrange("p (c t) -> p c t", c=4)
    for q in range(NSEQ_S):
        qs = slice(q * TS, (q + 1) * TS)
        B.sreset()
        t = _sample_small(B)
        pts = t["pts"].bitcast(I32)
        KTn = t["KTn"].rearrange("p (s k) -> p s k", s=7)
        QbTs = t["QbTs"].rearrange("p (c t) -> p c t", c=4)
        qTts = t["qTts"].rearrange("p (c t) -> p c t", c=8)
        B.kvst, B.bkvst = B.salloc_n(2, 1344, F32)
        B.kst, B.bkst = B.salloc(832, BF16)
        B.wk, B.bwk = B.salloc_n(2, 512, F32)
        B.wkb, B.bwkb = B.salloc_n(6, 512, BF16)
        B.den, B.bden = B.salloc(32, F32)
        otok, botok = B.salloc(512, BF16)
        score, bscore = B.salloc(8196, F32)
        KT3 = B.arena[:, 0:24576].rearrange("p (s k) -> p s k", s=3)
        Vas = B.arena[:, 24576:32768].rearrange("p (b n) -> p b n", b=64)
        B.sel, B.bsel = B.arena[:, 32768:32768 + 8196], Buf()
        bpg = [Buf() for _ in range(NPAGES)]
        for pg in range(NPAGES):
            kvst, bkv = B.kvst[pg % 2], B.bkvst[pg % 2]
            _dyn_page_dma(B, kvst[:, 0:256], B.ckva, pts, q * NPAGES + pg, [bsm["pts"]], [bkv])
            _dyn_page_dma(B, kvst[:, 256:320], B.cki, pts, q * NPAGES + pg, [bsm["pts"]], [bkv])
            B.kv_build(kvst, bkv, 128, dKa=KT3[:, :, pg * 128:(pg + 1) * 128], dVa=Vas[:, pg, :], bdest=bpg[pg])
        chunks = []
        for ci in range(16):
            chunks.append(dict(KT=KT3, k0=ci * 512, nk=512, va=(lambda j, ci=ci: Vas[:, ci * 4 + j, :]), mb=None,
                               bmb=None, bk=bpg[ci * 4:ci * 4 + 4]))
        chunks.append(dict(KT=KTn[:, 0:3, qs], k0=0, nk=TS, va=(lambda j, q=q: t["Van"][:TS, q * 128:(q + 1) * 128]),
                           mb=t["mds"][:TS, :], bmb=bsm["mds"], bk=[bsm["KTn"]]))
        B.dsa_tile(TS, (lambda h: qTts[(h % 2) * 64:(h % 2) * 64 + 64, h // 2, qs]),
                   (lambda h: qTts[(h % 2) * 64:(h % 2) * 64 + 64, 4 + h // 2, qs]), bsm["qTts"],
                   t["wts"][:, q * 8:(q + 1) * 8], bsm["wts"], chunks, score, bscore, otok, botok)
        k = B.nxt = (B.nxt + 1) % 2
        ps = B.psT[k][:].rearrange("p (c t) -> p c t", c=8)
        for ch in range(4):
            B.tr(ps[:, ch, 0:TS], otok[:TS, ch * 128:(ch + 1) * 128], TS, [botok], [B.bpsT[k]])
        B.copy("act", oaTv[:, :, 2048 + q * TS:2048 + (q + 1) * TS], ps[:, 0:4, 0:TS], [B.bpsT[k]], [B.boaT[4]])
        for half in (1, 0):
            B.sreset()
            t = _sample_small(B)
            pts = t["pts"].bitcast(I32)
            KTn = t["KTn"].rearrange("p (s k) -> p s k", s=7)
            QbTs = t["QbTs"].rearrange("p (c t) -> p c t", c=4)
            B.kvst, B.bkvst = B.salloc_n(2, 1344, F32)
            B.kst, B.bkst = B.salloc(832, BF16)
            B.wk, B.bwk = B.salloc_n(2, 512, F32)
            B.wkb, B.bwkb = B.salloc_n(4, 512, BF16)
            KbTh = B.arena[:, 0:16384].rearrange("p (s k) -> p s k", s=4)
            Vbh = B.arena[:, 16384:32768].rearrange("p (b n) -> p b n", b=32)
            bpg = [Buf() for _ in range(32)]
            for j in range(32):
                pg = half * 32 + j
                kvst, bkv = B.kvst[j % 2], B.bkvst[j % 2]
                _dyn_page_dma(B, kvst[:, 320:1344], B.ckvb, pts, q * NPAGES + pg, [bsm["pts"]], [bkv])
                B.kv_build(kvst, bkv, 128, dKb=KbTh[:, :, j * 128:(j + 1) * 128], dVb=Vbh[:, j, :], bdest=bpg[j])
            blocks = []
            if half == 1:
                blocks.append(dict(
                    nk=TS, kT=(lambda h: KTn[(h % 2) * 64:(h % 2) * 64 + 64, 3 + h // 2, qs]),
                    v=(lambda h, q=q: t["Vbn"][:TS, q * 512 + h * 64:q * 512 + (h + 1) * 64]),
                    mask=t["msbs"][:TS, :], bk=bsm["KTn"], bm=bsm["msbs"]))
            for j in range(31, -1, -1):
                blocks.append(dict(
                    nk=128, kT=(lambda h, j=j: KbTh[(h % 2) * 64:(h % 2) * 64 + 64, h // 2, j * 128:(j + 1) * 128]),
                    v=(lambda h, j=j: Vbh[:, j, h * 64:(h + 1) * 64]), mask=None, bk=bpg[j], bm=None))
            B.sb_run(TS, (lambda h: QbTs[(h % 2) * 64:(h % 2) * 64 + 64, h // 2, qs]), bsm["QbTs"], blocks,
                     (lambda h: t["rbs"][:, h * TS:(h + 1) * TS]), (lambda h: bsm["rbs"]),
                     (lambda h: B.psO[0:64, 64 + h * TS:64 + (h + 1) * TS]), (lambda h: B.bpsO), half == 1, half == 0,
                     (lambda h, q=q: obTv[(h % 2) * 64:(h % 2) * 64 + 64, h // 2, 2048 + q * TS:2048 + (q + 1) * TS]),
                     B.bobT[4], part=(lambda h: t["part"][0:64, h * TS:(h + 1) * TS]), bpart=bsm["part"])


def _phase_tail(B, do_sample):
    P = B.P
    B.sreset()
    B.common_scratch(1)
    h1, bh1 = B.salloc(4 * D, F32)
    xnTg, bxnTg = B.salloc(8 * 512, BF16)
    B.wk, B.bwk = B.salloc_n(2, 512, F32)
    t12, bt12 = B.salloc_n(2, 512, F32)
    wgs, bwgs = B.salloc_n(2, 8 * 256, BF16)
    wbrs, bwbrs = B.salloc_n(2, 8 * 128, BF16)
    wffs, bwffs = B.salloc_n(2, 8 * 256, BF16)
    yst, byst = B.salloc(D, F32)
    Wo = B.arena[:, 0:8192].rearrange("p (k n) -> p k n", k=8)
    Wd = B.arena[:, 8192:8192 + 22528].rearrange("p (k n) -> p k n", k=NFF)
    hT = B.arena[:, 30720:30720 + 11264].rearrange("p (k n) -> p k n", k=NFF)
    mT = B.arena[:, 41984:41984 + 4096].rearrange("p (k n) -> p k n", k=8)
    bWo, bWd, bhT, bmT = Buf(), Buf(), Buf(), Buf()
    B.load_w(Wo, B.w_o, 0, D, 8, bWo)
    for q4 in range(2):
        P.dma("pool", Wd[:, q4 * 11:(q4 + 1) * 11, :],
              B.w_d.rearrange("(k p) n -> p k n", p=128)[:, q4 * 11:(q4 + 1) * 11, :], writes=[bWd])
    h1v = h1.rearrange("p (i n) -> p i n", i=4)
    xnTv = xnTg.rearrange("p (c t) -> p c t", c=8)
    oaTv = B.oaT[:, :].rearrange("p (c t) -> p c t", c=4)
    obTv = B.obT[:, :].rearrange("p (c t) -> p c t", c=4)
    groups = [(512, 4, 128, B.xown, J * 512, B.y_own, B.boaT[J], B.bobT[J]) for J in range(4)]
    if do_sample:
        groups.append((16, 1, 16, B.xs, 0, B.y_s, B.boaT[4], B.bobT[4]))
    for gi, (W, ntile, rows, xsrc, r0, ydst, boa, bob) in enumerate(groups):
        co = gi * 512
        xt, bxt = B.xt[0], B.bxt[0]
        for i in range(ntile):
            P.dma("sp", xt[:rows, :], xsrc[r0 + i * rows:r0 + (i + 1) * rows, :], writes=[bxt])
            B.norm_T(xt[:rows, :], bxt, rows, 0, xnTv[:, :, i * 128:i * 128 + rows], bxnTg)
        for fc in range(8):
            wg, bwg = wgs[fc % 2].rearrange("p (k n) -> p k n", k=8), bwgs[fc % 2]
            wb, bwb = wbrs[fc % 2].rearrange("p (k n) -> p k n", k=8), bwbrs[fc % 2]
            B.load_w(wg[:, :, 0:128], B.w_in, 2888 + fc * 128, 2888 + (fc + 1) * 128, 8, bwg)
            B.load_w(wg[:, :, 128:256], B.w_in, 3912 + fc * 128, 3912 + (fc + 1) * 128, 8, bwg)
            B.load_w(wb[:, 0:4, :], B.w_bra, fc * 128, (fc + 1) * 128, 4, bwb)
            B.load_w(wb[:, 4:8, :], B.w_brb, fc * 128, (fc + 1) * 128, 4, bwb)
            for a in range(2):
                for kc in range(8):
                    B.mm(B.psA[a][:, 0:W], wg[:, kc, a * 128:(a + 1) * 128], xnTv[:, kc, 0:W], kc == 0, kc == 7,
                         [bwg, bxnTg], [B.bpsA[a]])
            for a, (oT, bo) in enumerate(((oaTv, boa), (obTv, bob))):
                for c4 in range(4):
                    B.mm(B.psB[a][:, 0:W], wb[:, a * 4 + c4, :], oT[:, c4, co:co + W], c4 == 0, c4 == 3,
                         [bwb, bo], [B.bpsB[a]])
            for a in range(2):
                B.act(B.wk[a][:, 0:W], B.psA[a][:, 0:W], AF.Sigmoid, [B.bpsA[a]], [B.bwk[a]])
                B.tt("dve", t12[a][:, 0:W], B.wk[a][:, 0:W], B.psB[a][:, 0:W], ALU.mult, [B.bwk[a], B.bpsB[a]],
                     [bt12[a]])
            B.tt("pool", mT[:, fc, 0:W], t12[0][:, 0:W], t12[1][:, 0:W], ALU.add, bt12, [bmT])
        for i in range(ntile):
            P.dma("sp", xt[:rows, :], xsrc[r0 + i * rows:r0 + (i + 1) * rows, :], writes=[bxt])
            for hh in range(2):
                for fc in range(8):
                    B.mm(B.psA[hh][:rows, :], mT[:, fc, i * 128:i * 128 + rows], Wo[:, fc, hh * 512:(hh + 1) * 512],
                         fc == 0, fc == 7, [bmT, bWo], [B.bpsA[hh]])
                B.tt("dve", h1v[:rows, i, hh * 512:(hh + 1) * 512], B.psA[hh][:rows, :],
                     xt[:rows, hh * 512:(hh + 1) * 512], ALU.add, [B.bpsA[hh], bxt], [bh1])
            B.norm_T(h1v[:rows, i, :], bh1, rows, 1, xnTv[:, :, i * 128:i * 128 + rows], bxnTg)
        for f in range(NFF):
            wf, bwf = wffs[f % 2].rearrange("p (k n) -> p k n", k=8), bwffs[f % 2]
            B.load_w(wf[:, :, 0:128], B.w_g, f * 128, (f + 1) * 128, 8, bwf)
            B.load_w(wf[:, :, 128:256], B.w_u, f * 128, (f + 1) * 128, 8, bwf)
            for a in range(2):
                for kc in range(8):
                    B.mm(B.psA[a][:, 0:W], wf[:, kc, a * 128:(a + 1) * 128], xnTv[:, kc, 0:W], kc == 0, kc == 7,
                         [bwf, bxnTg], [B.bpsA[a]])
            B.act(B.wk[f % 2][:, 0:W], B.psA[0][:, 0:W], AF.Silu, [B.bpsA[0]], [B.bwk[f % 2]])
            B.tt("dve", hT[:, f, 0:W], B.wk[f % 2][:, 0:W], B.psA[1][:, 0:W], ALU.mult,
                 [B.bwk[f % 2], B.bpsA[1]], [bhT])
        for i in range(ntile):
            for hh in range(2):
                for f in range(NFF):
                    B.mm(B.psB[hh][:rows, :], hT[:, f, i * 128:i * 128 + rows], Wd[:, f, hh * 512:(hh + 1) * 512],
                         f == 0, f == NFF - 1, [bhT, bWd], [B.bpsB[hh]])
                B.tt("dve", h1v[:rows, i, hh * 512:(hh + 1) * 512], B.psB[hh][:rows, :],
                     h1v[:rows, i, hh * 512:(hh + 1) * 512], ALU.add, [B.bpsB[hh], bh1], [bh1])
            ss, rstd = B.st[:rows, 0:1], B.st[:rows, 1:2]
            B.act(B.junk[:rows, :], h1v[:rows, i, :], AF.Square, [bh1, B.bst], [B.bjunk, B.bst], accum_out=ss)
            B.act(rstd, ss, AF.Ln, [B.bst], [B.bst], scale=1.0 / D, bias=1e-6)
            B.act(rstd, rstd, AF.Exp, [B.bst], [B.bst], scale=-0.5)
            B.stt("dve", yst[:rows, :], h1v[:rows, i, :], rstd, B.g[:rows, 2, :], ALU.mult, ALU.mult,
                  [bh1, B.bst, B.bg], [byst])
            P.dma("sp", ydst[r0 + i * rows:r0 + (i + 1) * rows, :], yst[:rows, :], reads=[byst])


def build_program(stage=9, debug=False, npool=2560):
    B = Builder(debug=debug, npool=npool)
    B.load_consts()
    _phase1(B)
    if stage >= 2:
        _phase_sb(B)
    if stage >= 3:
        _phase_dsa(B)
    if stage >= 5:
        _phase_sample(B)
    if stage >= 4:
        _phase_tail(B, stage >= 5)
    if debug:
        B.P.barrier()
        B.P.dma("pool", B.dbg_ob, B.obT[:, :], reads=B.bobT)
        B.P.dma("pool", B.dbg_oa, B.oaT[:, :], reads=B.boaT)
    B.P.emit()
    return B.nc


_ROLE_TILES = None


def _own_tiles(r):
    return [8 * J + 2 * i + r for J in range(4) for i in range(4)]


def _prep(x_prompt, x_sample, cache_kv_a, cache_k_idx, cache_kv_b, page_table, w_in, w_br_a, w_br_b, w_o,
          norm_attn, norm_ffn, w_ffn_gate, w_ffn_up, w_ffn_down, norm_final, cores=range(8), npool=2560):
    f32 = np.float32
    x_prompt = np.asarray(x_prompt, f32)
    x_sample = np.asarray(x_sample, f32)
    ckva = np.ascontiguousarray(np.asarray(cache_kv_a, f32)[0].reshape(-1, 128, 256)[:npool])
    cki = np.ascontiguousarray(np.asarray(cache_k_idx, f32)[0].reshape(-1, 128, 64)[:npool])
    ckvb = np.ascontiguousarray(np.asarray(cache_kv_b, f32)[0].reshape(-1, 128, 1024)[:npool])
    page_table = np.asarray(page_table, np.int32)
    gains = np.stack([np.asarray(norm_attn, f32)[0], np.asarray(norm_ffn, f32)[0], np.asarray(norm_final, f32)])
    cs_all = _rope_tables(np.arange(SEQ))
    cs_s = _rope_tables(8192 + np.arange(TS))
    sidx = np.arange(128)
    consts = np.stack([np.eye(128, dtype=f32),
                       np.where(sidx[:, None] >= sidx[None, :], -8.0, 0.0).astype(f32),
                       np.full((128, 128), -8.0, f32)])
    in_maps = []
    for c in cores:
        b, r = c // 2, c % 2
        tiles = _own_tiles(r)
        rows = np.concatenate([np.arange(t * 128, (t + 1) * 128) for t in tiles])
        qpos = np.concatenate([np.arange((2 * i + r) * 128, (2 * i + r + 1) * 128) for i in range(4)])
        kpos = np.arange(1024)
        msb = (kpos[:, None] < qpos[None, :]).astype(f32).reshape(8, 128, 512)
        mdsa = np.stack([np.where(kpos[None, :] <= qpos[i * 128:(i + 1) * 128, None], 0.0, NEG).astype(f32)
                         for i in range(4)])
        tpos = np.arange(TS)
        in_maps.append({
            "xall": x_prompt[b], "xown": np.ascontiguousarray(x_prompt[b][rows]),
            "xs": np.ascontiguousarray(x_sample[4 * c:4 * c + 4].reshape(16, D)),
            "pt": np.ascontiguousarray(page_table[4 * c:4 * c + 4].reshape(1, 256)),
            "ckva": ckva, "cki": cki, "ckvb": ckvb,
            "w_in": np.asarray(w_in, f32)[0], "w_bra": np.asarray(w_br_a, f32)[0],
            "w_brb": np.asarray(w_br_b, f32)[0], "w_o": np.asarray(w_o, f32)[0],
            "w_g": np.asarray(w_ffn_gate, f32)[0], "w_u": np.asarray(w_ffn_up, f32)[0],
            "w_d": np.asarray(w_ffn_down, f32)[0], "gains": gains,
            "cs_all": cs_all, "cs_own": np.ascontiguousarray(cs_all[rows]), "cs_s": cs_s,
            "msb": msb, "mdsa": mdsa,
            "msb_s": (tpos[:, None] < tpos[None, :]).astype(f32),
            "md_s": np.where(tpos[None, :] <= tpos[:, None], 0.0, NEG).astype(f32),
            "consts": consts, "pcol": np.arange(128, dtype=f32).reshape(128, 1),
        })
    return in_maps


def kernel(x_prompt, x_sample, cache_kv_a, cache_k_idx, cache_kv_b, page_table, w_in, w_br_a, w_br_b, w_o,
           norm_attn, norm_ffn, w_ffn_gate, w_ffn_up, w_ffn_down, norm_final):
    f32 = np.float32
    in_maps = _prep(x_prompt, x_sample, cache_kv_a, cache_k_idx, cache_kv_b, page_table, w_in, w_br_a, w_br_b,
                    w_o, norm_attn, norm_ffn, w_ffn_gate, w_ffn_up, w_ffn_down, norm_final)
    nc = build_program(stage=5)
    res = run_bass_kernel_spmd(nc, in_maps, core_ids=list(range(8)))
    R = res.results
    y_p = np.zeros((4, SEQ, D), f32)
    y_s = np.zeros((32, TS, D), f32)
    kva_p = np.zeros((1, 4, SEQ, 2, 2, 64), f32)
    ki_p = np.zeros((1, 4, SEQ, 64), f32)
    kvb_p = np.zeros((1, 4, SEQ, 2, 8, 64), f32)
    kva_s = np.zeros((1, 32, TS, 2, 2, 64), f32)
    ki_s = np.zeros((1, 32, TS, 64), f32)
    kvb_s = np.zeros((1, 32, TS, 2, 8, 64), f32)
    for c in range(8):
        b, r = c // 2, c % 2
        tiles = _own_tiles(r)
        yo = R[c]["y_own"].reshape(16, 128, D)
        for j, t in enumerate(tiles):
            y_p[b, t * 128:(t + 1) * 128] = yo[j]
        y_s[4 * c:4 * c + 4] = R[c]["y_s"].reshape(4, TS, D)
        if r == 0:
            kv = R[c]["kv_all"]
            kva_p[0, b] = kv[:, 0:256].reshape(SEQ, 2, 2, 64)
            ki_p[0, b] = kv[:, 256:320]
            kvb_p[0, b] = kv[:, 320:1344].reshape(SEQ, 2, 8, 64)
        kvs = R[c]["kv_s"].reshape(4, TS, 1344)
        kva_s[0, 4 * c:4 * c + 4] = kvs[:, :, 0:256].reshape(4, TS, 2, 2, 64)
        ki_s[0, 4 * c:4 * c + 4] = kvs[:, :, 256:320]
        kvb_s[0, 4 * c:4 * c + 4] = kvs[:, :, 320:1344].reshape(4, TS, 2, 8, 64)
    return (y_p, y_s, kva_p, ki_p, kvb_p, kva_s, ki_s, kvb_s)
```

```python
from contextlib import ExitStack
import numpy as np
import concourse.bass as bass
import concourse.mybir as mybir
from concourse.bass_utils import run_bass_kernel_spmd

F32 = mybir.dt.float32
BF16 = mybir.dt.bfloat16
I32 = mybir.dt.int32
AF = mybir.ActivationFunctionType
ALU = mybir.AluOpType
AX = mybir.AxisListType

D = 1024
SEQ = 4096
NSEQ_S = 4
TS = 4
NPAGES = 64
DFF = 2816
NFF = 22
TOPK = 256
NBIS = 16
ENG = ("pe", "act", "dve", "pool", "sp")
NEG = -1.0e30


class Buf:
    __slots__ = ("w", "r")

    def __init__(self):
        self.w = []
        self.r = []


class Op:
    __slots__ = ("id", "eng", "fn", "deps", "cost", "isdma", "skip_self", "tok", "waits")

    def __init__(self, id_, eng, fn, deps, cost, isdma, skip_self):
        self.id, self.eng, self.fn, self.deps, self.cost = id_, eng, fn, deps, cost
        self.isdma, self.skip_self = isdma, skip_self
        self.tok = None
        self.waits = None


class Prog:
    def __init__(self, nc, n_dma_sems=12):
        self.nc = nc
        self.es = ExitStack()
        self.sem = {e: self.es.enter_context(nc.semaphore("c_" + e)) for e in ENG}
        self.dq = ("sp", "pool", "act")
        self.dsem = {q: [self.es.enter_context(nc.semaphore("d%s%d" % (q, i))) for i in range(n_dma_sems)]
                     for q in self.dq}
        self.epochs = [[]]
        self.nops = 0

    def sb(self, name, shape, dt):
        return self.es.enter_context(self.nc.sbuf_tensor(name, list(shape), dt))

    def ps(self, name, shape, dt):
        return self.es.enter_context(self.nc.psum_tensor(name, list(shape), dt))

    def _record(self, eng, fn, reads, writes, cost, isdma, skip_self):
        deps = set()
        for b in reads:
            deps.update(b.w)
        for b in writes:
            deps.update(b.w)
            deps.update(b.r)
        op = Op(self.nops, eng, fn, deps, cost, isdma, skip_self)
        self.nops += 1
        for b in reads:
            b.r.append(op.id)
        for b in writes:
            b.w = [op.id]
            b.r = []
        self.epochs[-1].append(op)

    def op(self, eng, fn, reads=(), writes=(), skip_self=False, cost=0.3):
        self._record(eng, fn, reads, writes, cost, False, skip_self)

    def dma_fn(self, eng, fn, reads=(), writes=(), cost=2.5):
        self._record(eng, fn, reads, writes, cost, True, False)

    def dma(self, eng, out, in_, reads=(), writes=(), cost=2.5):
        self.dma_fn(eng, lambda e: e.dma_start(out=out, in_=in_), reads, writes, cost)

    def barrier(self):
        if self.epochs[-1]:
            self.epochs.append([])

    @staticmethod
    def _schedule(ops):
        import heapq
        ids = {o.id: o for o in ops}
        indeg = {}
        children = {}
        for o in ops:
            d = [x for x in o.deps if x in ids]
            o.deps = d
            indeg[o.id] = len(d)
            for x in d:
                children.setdefault(x, []).append(o.id)
        finish = {}
        engfree = {e: 0.0 for e in ENG}
        heaps = {e: [] for e in ENG}
        for o in ops:
            if indeg[o.id] == 0:
                heapq.heappush(heaps[o.eng], (0.0, o.id))
        order = {e: [] for e in ENG}
        left = len(ops)
        while left:
            best = None
            for e in ENG:
                if heaps[e]:
                    rt, oid = heaps[e][0]
                    st = rt if rt > engfree[e] else engfree[e]
                    if best is None or (st, oid) < best[:2]:
                        best = (st, oid, e)
            st, oid, e = best
            heapq.heappop(heaps[e])
            o = ids[oid]
            if o.isdma:
                engfree[e] = st + 0.08
                finish[oid] = st + o.cost
            else:
                engfree[e] = st + o.cost
                finish[oid] = st + o.cost + 0.06
            order[e].append(o)
            left -= 1
            for c in children.get(oid, ()):
                indeg[c] -= 1
                if indeg[c] == 0:
                    oc = ids[c]
                    rt = max(finish[x] for x in oc.deps)
                    heapq.heappush(heaps[oc.eng], (rt, c))
        return order

    def emit(self):
        ndm = len(self.dsem["sp"])
        cnt = {e: 0 for e in ENG}
        dval = {q: [0] * ndm for q in self.dq}
        dnext = {q: 0 for q in self.dq}
        seen = {e: {} for e in ENG}
        stream = {e: [] for e in ENG}
        for ops in self.epochs:
            if not ops:
                continue
            order = self._schedule(ops)
            ids = {o.id: o for o in ops}
            pre = {}
            for e in ENG:
                for o in order[e]:
                    if o.isdma:
                        i = dnext[e]
                        dnext[e] = (i + 1) % ndm
                        pre[o.id] = (("d", e, i), dval[e][i])
                        dval[e][i] += 16
                        o.tok = (("d", e, i), dval[e][i])
                    else:
                        cnt[e] += 1
                        o.tok = (("e", e), cnt[e])
            for e in ENG:
                sn = seen[e]
                for o in order[e]:
                    need = {}
                    for x in o.deps:
                        k, v = ids[x].tok
                        if o.skip_self and k == ("e", e):
                            continue
                        if need.get(k, 0) < v:
                            need[k] = v
                    if o.isdma:
                        k, v = pre[o.id]
                        if v > 0 and need.get(k, 0) < v:
                            need[k] = v
                    waits = []
                    for k, v in need.items():
                        if sn.get(k, 0) < v:
                            sn[k] = v
                            waits.append((k, v))
                    stream[e].append((waits, o.fn, (o.tok[0], 16 if o.isdma else 1)))
            for e in ENG:
                waits = []
                for f in ENG:
                    k = ("e", f)
                    if cnt[f] > 0 and seen[e].get(k, 0) < cnt[f]:
                        seen[e][k] = cnt[f]
                        waits.append((k, cnt[f]))
                for q in self.dq:
                    for i in range(ndm):
                        k = ("d", q, i)
                        v = dval[q][i]
                        if v > 0 and seen[e].get(k, 0) < v:
                            seen[e][k] = v
                            waits.append((k, v))
                if waits:
                    stream[e].append((waits, None, None))
        self._check(stream)

        def semof(k):
            return self.sem[k[1]] if k[0] == "e" else self.dsem[k[1]][k[2]]

        engmap = {"pe": "tensor", "act": "scalar", "dve": "vector", "pool": "gpsimd", "sp": "sync"}
        with self.nc.Block() as block:
            for e in ENG:
                ops = stream[e]

                def body(engine, ops=ops):
                    for waits, fn, inc in ops:
                        for k, v in waits:
                            engine.wait_ge(semof(k), v)
                        if fn is not None:
                            fn(engine).then_inc(semof(inc[0]), inc[1])

                getattr(block, engmap[e])(body)
        self.es.close()

    @staticmethod
    def _check(stream):
        val = {}
        pos = {e: 0 for e in ENG}
        progress = True
        while progress:
            progress = False
            for e in ENG:
                while pos[e] < len(stream[e]):
                    waits, fn, inc = stream[e][pos[e]]
                    if any(val.get(k, 0) < v for k, v in waits):
                        break
                    if inc is not None:
                        val[inc[0]] = val.get(inc[0], 0) + inc[1]
                    pos[e] += 1
                    progress = True
        for e in ENG:
            assert pos[e] == len(stream[e]), ("schedule deadlock", e, pos[e], len(stream[e]))


class Builder:
    def __init__(self, debug=False, npool=2560):
        nc = bass.Bass("TRN2", target_bir_lowering=False)
        self.nc = nc
        self.P = Prog(nc)
        P = self.P

        def din(name, shape, dt=F32):
            return nc.dram_tensor(name, list(shape), dt, kind="ExternalInput").ap()

        def dout(name, shape):
            return nc.dram_tensor(name, list(shape), F32, kind="ExternalOutput").ap()

        self.xall = din("xall", [SEQ, D])
        self.xown = din("xown", [SEQ // 2, D])
        self.xs = din("xs", [NSEQ_S * TS, D])
        self.pt = din("pt", [1, NSEQ_S * NPAGES], I32)
        self.ckva = din("ckva", [npool, 128, 256])
        self.cki = din("cki", [npool, 128, 64])
        self.ckvb = din("ckvb", [npool, 128, 1024])
        self.w_in = din("w_in", [D, 4936])
        self.w_bra = din("w_bra", [512, D])
        self.w_brb = din("w_brb", [512, D])
        self.w_o = din("w_o", [D, D])
        self.w_g = din("w_g", [D, DFF])
        self.w_u = din("w_u", [D, DFF])
        self.w_d = din("w_d", [DFF, D])
        self.gains = din("gains", [3, D])
        self.cs_all = din("cs_all", [SEQ, 64])
        self.cs_own = din("cs_own", [SEQ // 2, 64])
        self.cs_s = din("cs_s", [TS, 64])
        self.msb = din("msb", [8, 128, 512])
        self.mdsa = din("mdsa", [4, 128, 1024])
        self.msb_s = din("msb_s", [TS, TS])
        self.md_s = din("md_s", [TS, TS])
        self.consts = din("consts", [3, 128, 128])
        self.pcol = din("pcol", [128, 1])
        self.y_own = dout("y_own", [SEQ // 2, D])
        self.y_s = dout("y_s", [NSEQ_S * TS, D])
        self.kv_all = dout("kv_all", [SEQ, 1344])
        self.kv_s = dout("kv_s", [NSEQ_S * TS, 1344])
        if debug:
            self.dbg_ob = dout("dbg_ob", [128, 4 * 2064])
            self.dbg_oa = dout("dbg_oa", [128, 4 * 2064])
            self.dbg_st = dout("dbg_st", [128, 8])
            self.dbg_sc = dout("dbg_sc", [128, 1024])

        self.ident = P.sb("ident", [128, 128], BF16)
        self.negtri = P.sb("negtri", [128, 128], BF16)
        self.negones = P.sb("negones", [128, 128], BF16)
        self.bconst = Buf()
        self.g = P.sb("g", [128, 3, D], F32)
        self.bg = Buf()
        self.psT = [P.ps("psT%d" % i, [128, 1024], BF16) for i in range(2)]
        self.bpsT = [Buf(), Buf()]
        self.psA = [P.ps("psA%d" % i, [128, 512], F32) for i in range(3)]
        self.bpsA = [Buf(), Buf(), Buf()]
        self.psB = [P.ps("psB%d" % i, [128, 512], F32) for i in range(2)]
        self.bpsB = [Buf(), Buf()]
        self.psO = P.ps("psO", [128, 512], F32)
        self.bpsO = Buf()
        self.arena = P.sb("arena", [128, 49152], BF16)
        self.oaT = P.sb("oaT", [128, 4 * 2064], BF16)
        self.obT = P.sb("obT", [128, 4 * 2064], BF16)
        self.boaT = [Buf() for _ in range(5)]
        self.bobT = [Buf() for _ in range(5)]
        self.st = P.sb("st", [128, 8], F32)
        self.bst = Buf()
        self.SCR = 33280
        self.scr = P.sb("scr", [128, self.SCR], BF16)
        self.sptr = 0
        self.nxt = 0

    def sreset(self):
        self.P.barrier()
        self.sptr = 0

    def salloc(self, cols, dt):
        n = cols * (2 if dt == F32 else 1)
        n = (n + 1) // 2 * 2
        assert self.sptr + n <= self.SCR, ("scratch overflow", self.sptr, n)
        v = self.scr[:, self.sptr:self.sptr + n]
        self.sptr += n
        return (v.bitcast(F32) if dt == F32 else v), Buf()

    def salloc_n(self, k, cols, dt):
        r = [self.salloc(cols, dt) for _ in range(k)]
        return [x[0] for x in r], [x[1] for x in r]

    def common_scratch(self, nxt=2):
        self.xt, self.bxt = self.salloc_n(nxt, D, F32)
        self.xn, self.bxn = self.salloc(D, BF16)
        self.junk, self.bjunk = self.salloc(D, BF16)

    @staticmethod
    def _fs(ap):
        n = 1
        for d in ap.shape[1:]:
            n *= int(d)
        return n

    def _cost(self, eng, ap):
        n = self._fs(ap)
        if eng == "pe":
            return 0.05 + n / 2400.0
        if eng == "act":
            return 0.2 + n / 960.0
        if eng == "dve":
            return 0.08 + n / 960.0
        return 0.1 + n / 480.0

    def mm(self, out, lhsT, rhs, start, stop, reads, writes):
        self.P.op("pe", lambda e: e.matmul(out, lhsT=lhsT, rhs=rhs, start=start, stop=stop),
                  reads, writes, skip_self=True, cost=self._cost("pe", out))

    def tr(self, out, in_, rows, reads, writes):
        ident = self.ident[:rows, :rows]
        self.P.op("pe", lambda e: e.transpose(out=out, in_=in_, identity=ident),
                  list(reads) + [self.bconst], writes, skip_self=True, cost=self._cost("pe", out))

    def act(self, out, in_, func, reads, writes, **kw):
        self.P.op("act", lambda e: e.activation(out=out, in_=in_, func=func, **kw), reads, writes,
                  cost=self._cost("act", in_))

    def copy(self, eng, out, in_, reads, writes):
        c = self._cost(eng, out)
        if eng == "act":
            self.P.op("act", lambda e: e.copy(out=out, in_=in_), reads, writes, cost=c)
        else:
            self.P.op(eng, lambda e: e.tensor_copy(out=out, in_=in_), reads, writes, cost=c)

    def tt(self, eng, out, in0, in1, op, reads, writes):
        self.P.op(eng, lambda e: e.tensor_tensor(out=out, in0=in0, in1=in1, op=op), reads, writes,
                  cost=self._cost(eng, out))

    def ts(self, eng, out, in0, s1, s2, op0, op1, reads, writes, accum=None):
        c = self._cost(eng, in0)
        if accum is None:
            if op1 is None:
                self.P.op(eng, lambda e: e.tensor_scalar(out=out, in0=in0, scalar1=s1, scalar2=None, op0=op0),
                          reads, writes, cost=c)
            else:
                self.P.op(eng, lambda e: e.tensor_scalar(out=out, in0=in0, scalar1=s1, scalar2=s2, op0=op0,
                                                         op1=op1), reads, writes, cost=c)
        else:
            self.P.op(eng, lambda e: e.tensor_scalar(out=out, in0=in0, scalar1=s1, scalar2=s2, op0=op0, op1=op1,
                                                     accum_out=accum), reads, writes, cost=c)

    def stt(self, eng, out, in0, scalar, in1, op0, op1, reads, writes, accum=None):
        c = self._cost(eng, in0)
        if accum is None:
            self.P.op(eng, lambda e: e.scalar_tensor_tensor(out=out, in0=in0, scalar=scalar, in1=in1, op0=op0,
                                                            op1=op1), reads, writes, cost=c)
        else:
            self.P.op(eng, lambda e: e.scalar_tensor_tensor(out=out, in0=in0, scalar=scalar, in1=in1, op0=op0,
                                                            op1=op1, accum_out=accum), reads, writes, cost=c)

    def memset(self, eng, ap, val, writes):
        self.P.op(eng, lambda e: e.memset(ap, val), (), writes, cost=self._cost(eng, ap))

    def load_consts(self):
        P = self.P
        P.dma("pool", self.ident[:], self.consts[0], writes=[self.bconst])
        P.dma("pool", self.negtri[:], self.consts[1], writes=[self.bconst])
        P.dma("pool", self.negones[:], self.consts[2], writes=[self.bconst])
        for i in range(3):
            P.dma("sp", self.g[:, i, :], self.gains[i].partition_broadcast(128), writes=[self.bg])

    def load_w(self, dst, src_rows_view, c0, c1, nk, wb):
        src = src_rows_view.rearrange("(k p) n -> p k n", p=128)[:, :, c0:c1]
        self.P.dma("pool", dst, src, writes=[wb])

    def norm_T(self, x_ap, bx, rows, gi, dest, bdest, ncols_dest_off=0):
        ss, rstd = self.st[:rows, 0:1], self.st[:rows, 1:2]
        self.memset("pool", ss, 0.0, [self.bst])
        self.act(self.junk[:rows, :], x_ap, AF.Square, [bx, self.bst], [self.bjunk, self.bst], accum_out=ss)
        self.act(rstd, ss, AF.Ln, [self.bst], [self.bst], scale=1.0 / D, bias=1e-6)
        self.act(rstd, rstd, AF.Exp, [self.bst], [self.bst], scale=-0.5)
        self.stt("dve", self.xn[:rows, :], x_ap, rstd, self.g[:rows, gi, :], ALU.mult, ALU.mult,
                 [bx, self.bst, self.bg], [self.bxn])
        k = self.nxt = (self.nxt + 1) % 2
        ps = self.psT[k][:].rearrange("p (c t) -> p c t", c=8)
        for kc in range(8):
            self.tr(ps[:, kc, 0:rows], self.xn[:rows, kc * 128:(kc + 1) * 128], rows, [self.bxn], [self.bpsT[k]])
        self.copy("act", dest, ps[:, :, 0:rows], [self.bpsT[k]], [bdest])

    def rope(self, src, dst, cs, rows, h, reads, writes):
        x1, x2 = src[:, :, 0:32], src[:, :, 32:64]
        cb = cs[:rows, 0:32].unsqueeze(1).to_broadcast([rows, h, 32])
        sb = cs[:rows, 32:64].unsqueeze(1).to_broadcast([rows, h, 32])
        t = [self.rt[i][:rows, 0:h * 32].rearrange("p (h j) -> p h j", h=h) for i in range(4)]
        bt = self.brt
        self.tt("dve", t[0], x1, cb, ALU.mult, reads, [bt[0]])
        self.tt("dve", t[1], x2, sb, ALU.mult, reads, [bt[1]])
        self.tt("dve", t[2], x2, cb, ALU.mult, reads, [bt[2]])
        self.tt("dve", t[3], x1, sb, ALU.mult, reads, [bt[3]])
        self.tt("pool", dst[:, :, 0:32], t[0], t[1], ALU.subtract, [bt[0], bt[1]], writes)
        self.tt("pool", dst[:, :, 32:64], t[2], t[3], ALU.add, [bt[2], bt[3]], writes)

    def kv_build(self, kvst, bkv, rows, dKa=None, dKb=None, dVa=None, dVb=None, bdest=None):
        if dVa is not None:
            self.copy("pool", dVa, kvst[:rows, 128:256], [bkv], [bdest])
        if dVb is not None:
            self.copy("pool", dVb, kvst[:rows, 832:1344], [bkv], [bdest])
        c0 = 0 if dKa is not None else 320
        c1 = 832 if dKb is not None else 320
        self.copy("pool", self.kst[:rows, c0:c1], kvst[:rows, c0:c1], [bkv], [self.bkst])
        k = self.nxt = (self.nxt + 1) % 2
        ps = self.psT[k][:].rearrange("p (c t) -> p c t", c=8)
        if dKa is not None:
            for s, col in enumerate((0, 64, 256)):
                for half in range(2):
                    self.tr(ps[half * 64:(half + 1) * 64, s, 0:rows], self.kst[:rows, col:col + 64], rows,
                            [self.bkst], [self.bpsT[k]])
            self.copy("act", dKa, ps[:, 0:3, 0:rows], [self.bpsT[k]], [bdest])
        if dKb is not None:
            for j in range(4):
                self.tr(ps[:, 3 + j, 0:rows], self.kst[:rows, 320 + 128 * j:320 + 128 * (j + 1)], rows,
                        [self.bkst], [self.bpsT[k]])
            self.copy("act", dKb, ps[:, 3:7, 0:rows], [self.bpsT[k]], [bdest])

    def kv_proj(self, xnT, bxnT, rows, cs, bcs, kvst, bkv, wkv):
        groups = ((0, 320, 0), (320, 832, 1), (832, 1344, 2))
        for c0, c1, a in groups:
            for kc in range(8):
                self.mm(self.psA[a][:rows, 0:c1 - c0], xnT[:, kc, 0:rows], wkv[:, kc, c0:c1], kc == 0, kc == 7,
                        [bxnT, self.bwbuf], [self.bpsA[a]])
        p0 = self.psA[0]
        self.rope(p0[:rows, 0:128].rearrange("p (h j) -> p h j", h=2),
                  kvst[:rows, 0:128].rearrange("p (h j) -> p h j", h=2), cs, rows, 2,
                  [self.bpsA[0], bcs], [bkv])
        self.rope(p0[:rows, 256:320].rearrange("p (h j) -> p h j", h=1),
                  kvst[:rows, 256:320].rearrange("p (h j) -> p h j", h=1), cs, rows, 1,
                  [self.bpsA[0], bcs], [bkv])
        self.copy("act", kvst[:rows, 128:256], p0[:rows, 128:256], [self.bpsA[0]], [bkv])
        self.copy("act", kvst[:rows, 320:832], self.psA[1][:rows, :], [self.bpsA[1]], [bkv])
        self.copy("dve", kvst[:rows, 832:1344], self.psA[2][:rows, :], [self.bpsA[2]], [bkv])

    def sb_run(self, W, qT, bq, blocks, Rb, bRb, oacc, boacc, first, last, dest, bdest, part=None, bpart=None):
        n = 0
        for h in range(8):
            for bi, blk in enumerate(blocks):
                nk = blk["nk"]
                isfirst = first and bi == 0
                islast = last and bi == len(blocks) - 1
                n += 1
                za, zb = self.psA[n % 2], self.bpsA[n % 2]
                ca, cb = self.psB[n % 2], self.bpsB[n % 2]
                e_sb, be = self.wk[n % 2], self.bwk[n % 2]
                L, bL = self.wkb[n % 2], self.bwkb[n % 2]
                a_sb, ba = self.wkb[2 + n % 2], self.bwkb[2 + n % 2]
                kT, q = blk["kT"](h), qT(h)
                self.mm(za[:nk, 0:W], kT, q, True, True, [blk["bk"], bq], [zb])
                self.act(e_sb[:nk, 0:W], za[:nk, 0:W], AF.Exp, [zb], [be], scale=0.125)
                self.act(L[:nk, 0:W], e_sb[:nk, 0:W], AF.Ln, [be], [bL], bias=1.0)
                if blk["mask"] is not None:
                    self.tt("dve", L[:nk, 0:W], L[:nk, 0:W], blk["mask"], ALU.mult, [bL, blk["bm"]], [bL])
                self.mm(ca[:nk, 0:W], kT, q, True, False, [blk["bk"], bq], [cb])
                self.mm(ca[:nk, 0:W], self.negtri[:nk, :nk], L[:nk, 0:W], False, isfirst, [bL, self.bconst], [cb])
                if not isfirst:
                    self.mm(ca[:nk, 0:W], self.negones[:, :nk], Rb(h), False, True, [bRb(h), self.bconst], [cb])
                self.act(a_sb[:nk, 0:W], ca[:nk, 0:W], AF.Exp, [cb], [ba], scale=0.125)
                if blk["mask"] is not None:
                    self.tt("dve", a_sb[:nk, 0:W], a_sb[:nk, 0:W], blk["mask"], ALU.mult, [ba, blk["bm"]], [ba])
                self.mm(oacc(h), blk["v"](h), a_sb[:nk, 0:W], bi == 0, bi == len(blocks) - 1, [blk["bk"], ba],
                        [boacc(h)])
                if isfirst:
                    self.memset("pool", Rb(h), 0.0, [bRb(h)])
                    self.copy("dve", Rb(h)[:nk, :], L[:nk, 0:W], [bL], [bRb(h)])
                elif not islast:
                    self.tt("dve", Rb(h)[:nk, :], Rb(h)[:nk, :], L[:nk, 0:W], ALU.add, [bL, bRb(h)], [bRb(h)])
            if first and last:
                self.copy("act", dest(h), oacc(h), [boacc(h)], [bdest])
            elif first:
                self.copy("act", part(h), oacc(h), [boacc(h)], [bpart])
            elif last:
                self.tt("dve", dest(h), oacc(h), part(h), ALU.add, [boacc(h), bpart], [bdest])
            else:
                self.tt("dve", part(h), oacc(h), part(h), ALU.add, [boacc(h), bpart], [bpart])

    def sb_run_heads(self, qbd, bq, blocks, Rb, bRb, osb, bosb, first, msk):
        W = 8 * TS
        for bi, blk in enumerate(blocks):
            nk = blk["nk"]
            isfirst = first and bi == 0
            n = bi
            za, zb = self.psA[n % 2], self.bpsA[n % 2]
            ca, cb = self.psB[n % 2], self.bpsB[n % 2]
            pv, bpv = (self.psO, self.bpsO) if n % 2 == 0 else (self.psA[2], self.bpsA[2])
            e_sb, be = self.wk[n % 2], self.bwk[n % 2]
            L, bL = self.wkb[n % 2], self.bwkb[n % 2]
            a_sb, ba = self.wkb[2 + n % 2], self.bwkb[2 + n % 2]
            for c in range(4):
                self.mm(za[:nk, 0:W], blk["kT"](c), qbd[:, c, :], c == 0, c == 3, [blk["bk"], bq], [zb])
            self.act(e_sb[:nk, 0:W], za[:nk, 0:W], AF.Exp, [zb], [be], scale=0.125)
            self.act(L[:nk, 0:W], e_sb[:nk, 0:W], AF.Ln, [be], [bL], bias=1.0)
            if blk["masked"]:
                self.tt("dve", L[:nk, 0:W].rearrange("p (h t) -> p h t", h=8),
                        L[:nk, 0:W].rearrange("p (h t) -> p h t", h=8), msk, ALU.mult, [bL], [bL])
            for c in range(4):
                self.mm(ca[:nk, 0:W], blk["kT"](c), qbd[:, c, :], c == 0, False, [blk["bk"], bq], [cb])
            self.mm(ca[:nk, 0:W], self.negtri[:nk, :nk], L[:nk, 0:W], False, isfirst, [bL, self.bconst], [cb])
            if not isfirst:
                self.mm(ca[:nk, 0:W], self.negones[:, :nk], Rb, False, True, [bRb, self.bconst], [cb])
            self.act(a_sb[:nk, 0:W], ca[:nk, 0:W], AF.Exp, [cb], [ba], scale=0.125)
            if blk["masked"]:
                self.tt("dve", a_sb[:nk, 0:W].rearrange("p (h t) -> p h t", h=8),
                        a_sb[:nk, 0:W].rearrange("p (h t) -> p h t", h=8), msk, ALU.mult, [ba], [ba])
            for c in range(4):
                self.mm(pv[:, c * W:(c + 1) * W], blk["v"](c), a_sb[:nk, 0:W], True, True, [blk["bk"], ba], [bpv])
            if isfirst:
                self.copy("dve", osb, pv[:, 0:4 * W], [bpv], [bosb])
                self.memset("pool", Rb, 0.0, [bRb])
                self.copy("pool", Rb[:nk, :], L[:nk, 0:W], [bL], [bRb])
            else:
                self.tt("dve", osb, pv[:, 0:4 * W], osb, ALU.add, [bpv, bosb], [bosb])
                self.tt("pool", Rb[:nk, :], Rb[:nk, :], L[:nk, 0:W], ALU.add, [bL, bRb], [bRb])

    def dsa_tile(self, rows, qaT, qiT, bq, w_t, bw, chunks, score, bscore, otok, botok):
        P = self.P
        N = sum(c["nk"] for c in chunks)
        col = 0
        cols = []
        for c_ in chunks:
            nk, k0 = c_["nk"], c_["k0"]
            cols.append(col)
            for h in range(8):
                hp = (h % 2) * 64
                pa, bpa = self.psA[h % 2], self.bpsA[h % 2]
                r, br = self.wk[h % 2], self.bwk[h % 2]
                self.mm(pa[:rows, 0:nk], qiT(h), c_["KT"][hp:hp + 64, 2, k0:k0 + nk], True, True,
                        [bq] + c_["bk"], [bpa])
                self.act(r[:rows, 0:nk], pa[:rows, 0:nk], AF.Relu, [bpa], [br])
                sc = score[:rows, col:col + nk]
                if h == 0:
                    self.ts("dve", sc, r[:rows, 0:nk], w_t[:rows, 0:1], None, ALU.mult, None, [br, bw], [bscore])
                else:
                    self.stt("dve", sc, r[:rows, 0:nk], w_t[:rows, h:h + 1], sc, ALU.mult, ALU.add,
                             [br, bw, bscore], [bscore])
            col += nk
        st = self.st
        lo, w0, mid, cnt, tq, rmax = (st[:rows, i:i + 1] for i in range(2, 8))
        bs = self.bst
        sc = score[:rows, 0:N]
        sel = self.sel[:rows, 0:N]
        self.ts("dve", sel, sc, -1.0, -3.0e38, ALU.mult, ALU.max, [bscore, bs], [self.bsel, bs], accum=lo)
        for ci, c_ in enumerate(chunks):
            if c_["mb"] is not None:
                s_ = score[:rows, cols[ci]:cols[ci] + c_["nk"]]
                self.tt("pool", s_, s_, c_["mb"], ALU.add, [bscore, c_["bmb"]], [bscore])
        P.op("dve", lambda e: e.reduce_max(out=rmax, in_=sc, axis=AX.X), [bscore], [bs],
             cost=self._cost("dve", sc))
        self.ts("dve", lo, lo, -1.0, -1.0, ALU.mult, ALU.add, [bs], [bs])
        self.tt("dve", w0, rmax, lo, ALU.subtract, [bs], [bs])
        for it in range(NBIS):
            hstep = 2.0 ** -(it + 1)
            self.stt("dve", mid, w0, hstep, lo, ALU.mult, ALU.add, [bs], [bs])
            self.ts("dve", sel, sc, mid, 0.0, ALU.is_gt, ALU.add, [bscore, bs], [self.bsel, bs], accum=cnt)
            self.ts("dve", tq, cnt, float(TOPK), hstep, ALU.is_ge, ALU.mult, [bs], [bs])
            self.stt("dve", lo, tq, w0, lo, ALU.mult, ALU.add, [bs], [bs])
        self.ts("dve", sel, sc, lo, None, ALU.is_gt, None, [bscore, bs], [self.bsel])
        den = self.den
        nch = len(chunks)
        nblk_total = sum((c_["nk"] + 127) // 128 for c_ in chunks)
        for h in range(8):
            hp = (h % 2) * 64
            c = h // 4
            bdone = 0
            for ci, c_ in enumerate(chunks):
                nk, k0 = c_["nk"], c_["k0"]
                pa, bpa = self.psA[ci % 2], self.bpsA[ci % 2]
                p_sb, bp = self.wkb[ci % 2], self.bwkb[ci % 2]
                pm, bpm = self.wkb[2 + ci % 2], self.bwkb[2 + ci % 2]
                pT, bpT = self.wkb[4 + ci % 2], self.bwkb[4 + ci % 2]
                self.mm(pa[:rows, 0:nk], qaT(h), c_["KT"][hp:hp + 64, c, k0:k0 + nk], True, True,
                        [bq] + c_["bk"], [bpa])
                self.act(p_sb[:rows, 0:nk], pa[:rows, 0:nk], AF.Exp, [bpa], [bp], scale=0.125)
                self.stt("dve", pm[:rows, 0:nk], p_sb[:rows, 0:nk], 1.0, sel[:, cols[ci]:cols[ci] + nk], ALU.mult,
                         ALU.mult, [bp, self.bsel, self.bden], [bpm, self.bden], accum=den[:rows, ci:ci + 1])
                k = self.nxt = (self.nxt + 1) % 2
                ps = self.psT[k][:].rearrange("p (c t) -> p c t", c=8)
                nb = (nk + 127) // 128
                pTv = pT[:, 0:4 * 128].rearrange("p (c t) -> p c t", c=4)
                for j in range(nb):
                    kk = min(128, nk - j * 128)
                    self.tr(ps[:kk, j, 0:rows], pm[:rows, j * 128:j * 128 + kk], rows, [bpm], [self.bpsT[k]])
                kk_last = nk - (nb - 1) * 128
                if kk_last == 128:
                    self.copy("act", pTv[:, 0:nb, 0:rows], ps[:, 0:nb, 0:rows], [self.bpsT[k]], [bpT])
                else:
                    if nb > 1:
                        self.copy("act", pTv[:, 0:nb - 1, 0:rows], ps[:, 0:nb - 1, 0:rows], [self.bpsT[k]], [bpT])
                    self.copy("act", pTv[:kk_last, nb - 1, 0:rows], ps[:kk_last, nb - 1, 0:rows], [self.bpsT[k]],
                              [bpT])
                for j in range(nb):
                    kk = min(128, nk - j * 128)
                    self.mm(self.psO[:rows, 0:64], pTv[:kk, j, 0:rows], c_["va"](j)[:kk, c * 64:(c + 1) * 64],
                            bdone == 0, bdone == nblk_total - 1, [bpT] + c_["bk"], [self.bpsO])
                    bdone += 1
            dsum, rden = st[:rows, 0:1], st[:rows, 1:2]
            P.op("dve", lambda e, dsum=dsum: e.reduce_sum(out=dsum, in_=den[:rows, 0:nch], axis=AX.X),
                 [self.bden], [bs])
            P.op("dve", lambda e, dsum=dsum, rden=rden: e.reciprocal(out=rden, in_=dsum), [bs], [bs])
            self.ts("dve", otok[:rows, h * 64:(h + 1) * 64], self.psO[:rows, 0:64], rden, None, ALU.mult, None,
                    [self.bpsO, bs], [botok])


def _rope_tables(pos):
    half = 32
    inv = (np.float32(10000.0) ** (-(np.arange(half, dtype=np.float32)) / np.float32(half))).astype(np.float32)
    ang = pos.astype(np.float32)[:, None] * inv[None, :]
    return np.concatenate([np.cos(ang), np.sin(ang)], axis=1).astype(np.float32)


def _phase1(B):
    P = B.P
    B.sreset()
    B.common_scratch(2)
    B.wbuf, B.bwbuf = B.salloc(8 * 1344, BF16)
    B.kvst, B.bkvst = B.salloc_n(2, 1344, F32)
    B.rt, B.brt = B.salloc_n(4, 256, F32)
    B.cs, B.bcs = B.salloc_n(2, 64, F32)
    B.kst, B.bkst = B.salloc(832, BF16)
    xnT1, bxnT1 = B.salloc(1024, BF16)
    wkv = B.wbuf.rearrange("p (k n) -> p k n", k=8)
    B.load_w(wkv[:, :, 0:256], B.w_in, 512, 768, 8, B.bwbuf)
    B.load_w(wkv[:, :, 256:320], B.w_in, 1280, 1344, 8, B.bwbuf)
    B.load_w(wkv[:, :, 320:1344], B.w_in, 1864, 2888, 8, B.bwbuf)
    B.KT = B.arena[:, 0:7 * SEQ].rearrange("p (s k) -> p s k", s=7)
    B.Vb = B.arena[:, 28672:28672 + 16384].rearrange("p (b n) -> p b n", b=32)
    B.Va = B.arena[:, 45056:45056 + 4096].rearrange("p (b n) -> p b n", b=32)
    B.bkvblk = [Buf() for _ in range(32)]
    xnT = xnT1.rearrange("p (c t) -> p c t", c=8)
    for tt in range(32):
        xt, bxt = B.xt[tt % 2], B.bxt[tt % 2]
        cs, bcs = B.cs[tt % 2], B.bcs[tt % 2]
        kvst, bkv = B.kvst[tt % 2], B.bkvst[tt % 2]
        P.dma("sp", xt[:, :], B.xall[tt * 128:(tt + 1) * 128, :], writes=[bxt])
        P.dma("sp", cs[:, :], B.cs_all[tt * 128:(tt + 1) * 128, :], writes=[bcs])
        B.norm_T(xt[:, :], bxt, 128, 0, xnT, bxnT1)
        B.kv_proj(xnT, bxnT1, 128, cs, bcs, kvst, bkv, wkv)
        P.dma("sp", B.kv_all[tt * 128:(tt + 1) * 128, :], kvst[:, :], reads=[bkv])
        B.kv_build(kvst, bkv, 128, dKa=B.KT[:, 0:3, tt * 128:(tt + 1) * 128],
                   dKb=B.KT[:, 3:7, tt * 128:(tt + 1) * 128], dVa=B.Va[:, tt, :], dVb=B.Vb[:, tt, :],
                   bdest=B.bkvblk[tt])


def _phase_sb(B):
    P = B.P
    B.sreset()
    B.common_scratch(2)
    B.wbuf, B.bwbuf = B.salloc(8 * 512, BF16)
    xnTg, bxnTg = B.salloc(8 * 512, BF16)
    qT, bqT = B.salloc(4 * 512, BF16)
    B.wk, B.bwk = B.salloc_n(2, 512, F32)
    B.wkb, B.bwkb = B.salloc_n(4, 512, BF16)
    rb, brb = B.salloc_n(2, 512, BF16)
    mask, bmask = B.salloc(8 * 512, BF16)
    wqb = B.wbuf.rearrange("p (k n) -> p k n", k=8)
    B.load_w(wqb, B.w_in, 1352, 1864, 8, B.bwbuf)
    maskv = mask.rearrange("p (j t) -> p j t", j=8)
    P.dma("pool", maskv, B.msb.rearrange("j p t -> p j t"), writes=[bmask])
    xnTv = xnTg.rearrange("p (c t) -> p c t", c=8)
    qTv = qT.rearrange("p (c t) -> p c t", c=4)
    obTv = B.obT[:, :].rearrange("p (c t) -> p c t", c=4)
    for J in range(4):
        for i in range(4):
            ot = J * 4 + i
            xt, bxt = B.xt[ot % 2], B.bxt[ot % 2]
            P.dma("sp", xt[:, :], B.xown[ot * 128:(ot + 1) * 128, :], writes=[bxt])
            B.norm_T(xt[:, :], bxt, 128, 0, xnTv[:, :, i * 128:(i + 1) * 128], bxnTg)
        for ch in range(4):
            for kc in range(8):
                B.mm(B.psB[ch % 2][:, :], wqb[:, kc, ch * 128:(ch + 1) * 128], xnTv[:, kc, :], kc == 0, kc == 7,
                     [B.bwbuf, bxnTg], [B.bpsB[ch % 2]])
            B.copy("dve", qTv[:, ch, :], B.psB[ch % 2][:, :], [B.bpsB[ch % 2]], [bqT])
        blocks = []
        for kb in range(8 * J + 7, -1, -1):
            j = kb - 8 * J
            blocks.append(dict(
                nk=128,
                kT=(lambda h, kb=kb: B.KT[(h % 2) * 64:(h % 2) * 64 + 64, 3 + h // 2, kb * 128:(kb + 1) * 128]),
                v=(lambda h, kb=kb: B.Vb[:, kb, h * 64:(h + 1) * 64]),
                mask=(maskv[:, j, :] if j >= 0 else None), bk=B.bkvblk[kb], bm=bmask))
        B.sb_run(512, (lambda h: qTv[(h % 2) * 64:(h % 2) * 64 + 64, h // 2, :]), bqT, blocks,
                 (lambda h: rb[h % 2]), (lambda h: brb[h % 2]),
                 (lambda h: (B.psO if h % 2 == 0 else B.psA[2])[0:64, :]),
                 (lambda h: (B.bpsO if h % 2 == 0 else B.bpsA[2])), True, True,
                 (lambda h, J=J: obTv[(h % 2) * 64:(h % 2) * 64 + 64, h // 2, J * 512:(J + 1) * 512]), B.bobT[J])


def _phase_dsa(B):
    P = B.P
    B.sreset()
    B.common_scratch(2)
    B.wbuf, B.bwbuf = B.salloc(8 * 1032, BF16)
    xnT1, bxnT1 = B.salloc(1024, BF16)
    B.cs, B.bcs = B.salloc_n(2, 64, F32)
    B.rt, B.brt = B.salloc_n(4, 256, F32)
    qst, bqst = B.salloc(1024, BF16)
    qTt, bqTt = B.salloc(1024, BF16)
    w_t, bw = B.salloc(8, F32)
    B.wk, B.bwk = B.salloc_n(2, 512, F32)
    B.wkb, B.bwkb = B.salloc_n(6, 512, BF16)
    maskD, bmaskD = B.salloc_n(2, 1024, F32)
    otok, botok = B.salloc(512, BF16)
    B.den, B.bden = B.salloc(32, F32)
    score = B.arena[:, 28672:28672 + 8192].bitcast(F32)
    bscore = Buf()
    B.sel, B.bsel = B.arena[:, 36864:36864 + 4096], Buf()
    wq = B.wbuf.rearrange("p (k n) -> p k n", k=8)
    B.load_w(wq[:, :, 0:512], B.w_in, 0, 512, 8, B.bwbuf)
    B.load_w(wq[:, :, 512:1024], B.w_in, 768, 1280, 8, B.bwbuf)
    B.load_w(wq[:, :, 1024:1032], B.w_in, 1344, 1352, 8, B.bwbuf)
    xnT = xnT1.rearrange("p (c t) -> p c t", c=8)
    qTv = qTt.rearrange("p (c t) -> p c t", c=8)
    oaTv = B.oaT[:, :].rearrange("p (c t) -> p c t", c=4)
    wscale = float(8.0 ** -0.5 * 64.0 ** -0.5)
    for ot in range(16):
        J, i = ot // 4, ot % 4
        xt, bxt = B.xt[ot % 2], B.bxt[ot % 2]
        cs, bcs = B.cs[ot % 2], B.bcs[ot % 2]
        md, bmd = maskD[ot % 2], bmaskD[ot % 2]
        P.dma("sp", xt[:, :], B.xown[ot * 128:(ot + 1) * 128, :], writes=[bxt])
        P.dma("sp", cs[:, :], B.cs_own[ot * 128:(ot + 1) * 128, :], writes=[bcs])
        P.dma("sp", md[:, :], B.mdsa[i], writes=[bmd])
        B.norm_T(xt[:, :], bxt, 128, 0, xnT, bxnT1)
        for a, (c0, c1) in enumerate(((0, 512), (512, 1024))):
            for kc in range(8):
                B.mm(B.psA[a][:, :], xnT[:, kc, :], wq[:, kc, c0:c1], kc == 0, kc == 7, [bxnT1, B.bwbuf],
                     [B.bpsA[a]])
        for kc in range(8):
            B.mm(B.psB[0][:, 0:8], xnT[:, kc, :], wq[:, kc, 1024:1032], kc == 0, kc == 7, [bxnT1, B.bwbuf],
                 [B.bpsB[0]])
        for a in range(2):
            B.rope(B.psA[a][:, :].rearrange("p (h j) -> p h j", h=8),
                   qst[:, a * 512:(a + 1) * 512].rearrange("p (h j) -> p h j", h=8), cs, 128, 8,
                   [B.bpsA[a], bcs], [bqst])
        B.ts("dve", w_t[:, :], B.psB[0][:, 0:8], wscale, None, ALU.mult, None, [B.bpsB[0]], [bw])
        k = B.nxt = (B.nxt + 1) % 2
        ps = B.psT[k][:].rearrange("p (c t) -> p c t", c=8)
        for ch in range(8):
            B.tr(ps[:, ch, :], qst[:, ch * 128:(ch + 1) * 128], 128, [bqst], [B.bpsT[k]])
        B.copy("act", qTv, ps, [B.bpsT[k]], [bqTt])
        chunks = []
        for ci in range(2 * (J + 1)):
            mb = md[:, (ci - 2 * J) * 512:(ci - 2 * J + 1) * 512] if ci >= 2 * J else None
            chunks.append(dict(KT=B.KT, k0=ci * 512, nk=512, va=(lambda j, ci=ci: B.Va[:, ci * 4 + j, :]), mb=mb,
                               bmb=bmd, bk=B.bkvblk[ci * 4:ci * 4 + 4]))
        B.dsa_tile(128, (lambda h: qTv[(h % 2) * 64:(h % 2) * 64 + 64, h // 2, :]),
                   (lambda h: qTv[(h % 2) * 64:(h % 2) * 64 + 64, 4 + h // 2, :]), bqTt, w_t, bw, chunks,
                   score, bscore, otok, botok)
        k = B.nxt = (B.nxt + 1) % 2
        ps = B.psT[k][:].rearrange("p (c t) -> p c t", c=8)
        for ch in range(4):
            B.tr(ps[:, ch, :], otok[:, ch * 128:(ch + 1) * 128], 128, [botok], [B.bpsT[k]])
        B.copy("act", oaTv[:, :, ot * 128:(ot + 1) * 128], ps[:, 0:4, :], [B.bpsT[k]], [B.boaT[J]])


def _sample_small(B):
    t = {}
    t["KTn"], _ = B.salloc(7 * 16, BF16)
    t["Van"], _ = B.salloc(NSEQ_S * 128, BF16)
    t["Vbn"], _ = B.salloc(NSEQ_S * 512, BF16)
    t["QbTs"], _ = B.salloc(4 * 16, BF16)
    t["qTts"], _ = B.salloc(8 * 16, BF16)
    t["wts"], _ = B.salloc(NSEQ_S * 8, F32)
    t["xnTs"], _ = B.salloc(8 * 16, BF16)
    t["rbs"], _ = B.salloc(8 * TS, BF16)
    t["qbd"], _ = B.salloc(4 * 8 * TS, BF16)
    t["osb"], _ = B.salloc(4 * 8 * TS, F32)
    t["pts"], _ = B.salloc(NSEQ_S * NPAGES, I32 if False else F32)
    t["msbs"], _ = B.salloc(TS, BF16)
    t["mds"], _ = B.salloc(TS, F32)
    if not hasattr(B, "bsm"):
        B.bsm = {k: Buf() for k in t}
    return t


def _dyn_page_dma(B, dst, cache, pts, idx, reads, writes):
    rows = cache.rearrange("n p d -> (n p) d")

    def fn(e):
        return e.indirect_dma_start(out=dst, out_offset=None, in_=rows,
                                    in_offset=bass.IndirectOffsetOnAxis(ap=pts[:, idx:idx + 1], axis=0))
    B.P.dma_fn("pool", fn, reads, writes)


def _phase_sample(B):
    P = B.P
    wscale = float(8.0 ** -0.5 * 64.0 ** -0.5)
    B.sreset()
    t = _sample_small(B)
    bsm = B.bsm
    B.common_scratch(1)
    wall, bwall = B.salloc(8 * 1544, BF16)
    B.kvst, B.bkvst = B.salloc_n(2, 1344, F32)
    B.rt, B.brt = B.salloc_n(4, 256, F32)
    B.cs, B.bcs = B.salloc_n(1, 64, F32)
    B.kst, B.bkst = B.salloc(832, BF16)
    qst, bqst = B.salloc(1024, BF16)
    wkv = wall[:, 0:8 * 1344].rearrange("p (k n) -> p k n", k=8)
    B.load_w(wkv[:, :, 0:256], B.w_in, 512, 768, 8, bwall)
    B.load_w(wkv[:, :, 256:320], B.w_in, 1280, 1344, 8, bwall)
    B.load_w(wkv[:, :, 320:1344], B.w_in, 1864, 2888, 8, bwall)
    pts = t["pts"].bitcast(I32)
    ptb_f, bptb = B.salloc(NSEQ_S * NPAGES, F32)
    ptf, bptf = B.salloc(NSEQ_S * NPAGES, F32)
    pcol, bpcol = B.salloc(1, F32)
    ptb = ptb_f.bitcast(I32)
    P.dma("sp", ptb, B.pt[0].partition_broadcast(128), writes=[bptb])
    P.dma("sp", pcol, B.pcol, writes=[bpcol])
    B.copy("dve", ptf, ptb, [bptb], [bptf])
    B.ts("dve", ptf, ptf, 128.0, pcol[:, 0:1], ALU.mult, ALU.add, [bptf, bpcol], [bptf])
    B.copy("dve", pts, ptf, [bptf], [bsm["pts"]])
    P.dma("pool", t["msbs"][:TS, :], B.msb_s, writes=[bsm["msbs"]])
    P.dma("sp", t["mds"][:TS, :], B.md_s, writes=[bsm["mds"]])
    KTn = t["KTn"].rearrange("p (s k) -> p s k", s=7)
    xnTs = t["xnTs"].rearrange("p (c t) -> p c t", c=8)
    QbTs = t["QbTs"].rearrange("p (c t) -> p c t", c=4)
    qTts = t["qTts"].rearrange("p (c t) -> p c t", c=8)
    cs, bcs = B.cs[0], B.bcs[0]
    P.dma("sp", cs[:TS, :], B.cs_s[:, :], writes=[bcs])
    xt, bxt = B.xt[0], B.bxt[0]
    B.bwbuf = bwall
    for q in range(NSEQ_S):
        qs = slice(q * TS, (q + 1) * TS)
        kvst, bkv = B.kvst[q % 2], B.bkvst[q % 2]
        P.dma("sp", xt[:TS, :], B.xs[q * TS:(q + 1) * TS, :], writes=[bxt])
        B.norm_T(xt[:TS, :], bxt, TS, 0, xnTs[:, :, qs], bsm["xnTs"])
        B.kv_proj(xnTs[:, :, qs], bsm["xnTs"], TS, cs, bcs, kvst, bkv, wkv)
        P.dma("sp", B.kv_s[q * TS:(q + 1) * TS, :], kvst[:TS, :], reads=[bkv])
        B.kv_build(kvst, bkv, TS, dKa=KTn[:, 0:3, qs], dKb=KTn[:, 3:7, qs],
                   dVa=t["Van"][:TS, q * 128:(q + 1) * 128], dVb=t["Vbn"][:TS, q * 512:(q + 1) * 512],
                   bdest=bsm["KTn"])
    wqb = wall[:, 0:8 * 512].rearrange("p (k n) -> p k n", k=8)
    wq = wall[:, 8 * 512:8 * 1544].rearrange("p (k n) -> p k n", k=8)
    bwqb = bwq = bwall
    B.load_w(wqb, B.w_in, 1352, 1864, 8, bwall)
    B.load_w(wq[:, :, 0:512], B.w_in, 0, 512, 8, bwall)
    B.load_w(wq[:, :, 512:1024], B.w_in, 768, 1280, 8, bwall)
    B.load_w(wq[:, :, 1024:1032], B.w_in, 1344, 1352, 8, bwall)
    for q in range(NSEQ_S):
        qs = slice(q * TS, (q + 1) * TS)
        for ch in range(4):
            for kc in range(8):
                B.mm(B.psB[ch % 2][:, 0:TS], wqb[:, kc, ch * 128:(ch + 1) * 128], xnTs[:, kc, qs], kc == 0, kc == 7,
                     [bwqb, bsm["xnTs"]], [B.bpsB[ch % 2]])
            B.copy("dve", QbTs[:, ch, qs], B.psB[ch % 2][:, 0:TS], [B.bpsB[ch % 2]], [bsm["QbTs"]])
        for a, (c0, c1) in enumerate(((0, 512), (512, 1024))):
            for kc in range(8):
                B.mm(B.psA[a][:TS, :], xnTs[:, kc, qs], wq[:, kc, c0:c1], kc == 0, kc == 7, [bsm["xnTs"], bwq],
                     [B.bpsA[a]])
        for kc in range(8):
            B.mm(B.psB[0][:TS, 0:8], xnTs[:, kc, qs], wq[:, kc, 1024:1032], kc == 0, kc == 7, [bsm["xnTs"], bwq],
                 [B.bpsB[0]])
        for a in range(2):
            B.rope(B.psA[a][:TS, :].rearrange("p (h j) -> p h j", h=8),
                   qst[:TS, a * 512:(a + 1) * 512].rearrange("p (h j) -> p h j", h=8), cs, TS, 8,
                   [B.bpsA[a], bcs], [bqst])
        B.ts("dve", t["wts"][:TS, q * 8:(q + 1) * 8], B.psB[0][:TS, 0:8], wscale, None, ALU.mult, None,
             [B.bpsB[0]], [bsm["wts"]])
        k = B.nxt = (B.nxt + 1) % 2
        ps = B.psT[k][:].rearrange("p (c t) -> p c t", c=8)
        for ch in range(8):
            B.tr(ps[:, ch, 0:TS], qst[:TS, ch * 128:(ch + 1) * 128], TS, [bqst], [B.bpsT[k]])
        B.copy("act", qTts[:, :, qs], ps[:, :, 0:TS], [B.bpsT[k]], [bsm["qTts"]])
    oaTv = B.oaT[:, :].rearrange("p (c t) -> p c t", c=4)
    obTv = B.obT[:, :].rearrange("p (c t) -> p c t", c=4)
    for q in range(NSEQ_S):
        qs = slice(q * TS, (q + 1) * TS)
        B.sreset()
        t = _sample_small(B)
        pts = t["pts"].bitcast(I32)
        KTn = t["KTn"].rearrange("p (s k) -> p s k", s=7)
        QbTs = t["QbTs"].rearrange("p (c t) -> p c t", c=4)
        qTts = t["qTts"].rearrange("p (c t) -> p c t", c=8)
        B.kvst, B.bkvst = B.salloc_n(2, 1344, F32)
        B.kst, B.bkst = B.salloc(832, BF16)
        B.wk, B.bwk = B.salloc_n(2, 512, F32)
        B.wkb, B.bwkb = B.salloc_n(6, 512, BF16)
        B.den, B.bden = B.salloc(32, F32)
        otok, botok = B.salloc(512, BF16)
        score, bscore = B.salloc(8196, F32)
        KT3 = B.arena[:, 0:24576].rearrange("p (s k) -> p s k", s=3)
        Vas = B.arena[:, 24576:32768].rearrange("p (b n) -> p b n", b=64)
        B.sel, B.bsel = B.arena[:, 32768:32768 + 8196], Buf()
        bpg = [Buf() for _ in range(NPAGES)]
        for pg in range(NPAGES):
            kvst, bkv = B.kvst[pg % 2], B.bkvst[pg % 2]
            _dyn_page_dma(B, kvst[:, 0:256], B.ckva, pts, q * NPAGES + pg, [bsm["pts"]], [bkv])
            _dyn_page_dma(B, kvst[:, 256:320], B.cki, pts, q * NPAGES + pg, [bsm["pts"]], [bkv])
            B.kv_build(kvst, bkv, 128, dKa=KT3[:, :, pg * 128:(pg + 1) * 128], dVa=Vas[:, pg, :], bdest=bpg[pg])
        chunks = []
        for ci in range(16):
            chunks.append(dict(KT=KT3, k0=ci * 512, nk=512, va=(lambda j, ci=ci: Vas[:, ci * 4 + j, :]), mb=None,
                               bmb=None, bk=bpg[ci * 4:ci * 4 + 4]))
        chunks.append(dict(KT=KTn[:, 0:3, qs], k0=0, nk=TS, va=(lambda j, q=q: t["Van"][:TS, q * 128:(q + 1) * 128]),
                           mb=t["mds"][:TS, :], bmb=bsm["mds"], bk=[bsm["KTn"]]))
        B.dsa_tile(TS, (lambda h: qTts[(h % 2) * 64:(h % 2) * 64 + 64, h // 2, qs]),
                   (lambda h: qTts[(h % 2) * 64:(h % 2) * 64 + 64, 4 + h // 2, qs]), bsm["qTts"],
                   t["wts"][:, q * 8:(q + 1) * 8], bsm["wts"], chunks, score, bscore, otok, botok)
        k = B.nxt = (B.nxt + 1) % 2
        ps = B.psT[k][:].rearrange("p (c t) -> p c t", c=8)
        for ch in range(4):
            B.tr(ps[:, ch, 0:TS], otok[:TS, ch * 128:(ch + 1) * 128], TS, [botok], [B.bpsT[k]])
        B.copy("act", oaTv[:, :, 2048 + q * TS:2048 + (q + 1) * TS], ps[:, 0:4, 0:TS], [B.bpsT[k]], [B.boaT[4]])
        for half in (1, 0):
            B.sreset()
            t = _sample_small(B)
            pts = t["pts"].bitcast(I32)
            KTn = t["KTn"].rearrange("p (s k) -> p s k", s=7)
            QbTs = t["QbTs"].rearrange("p (c t) -> p c t", c=4)
            qbd = t["qbd"].rearrange("p (c w) -> p c w", c=4)
            B.kvst, B.bkvst = B.salloc_n(2, 1344, F32)
            B.kst, B.bkst = B.salloc(832, BF16)
            B.wk, B.bwk = B.salloc_n(2, 512, F32)
            B.wkb, B.bwkb = B.salloc_n(4, 512, BF16)
            KbTh = B.arena[:, 0:16384].rearrange("p (s k) -> p s k", s=4)
            Vbh = B.arena[:, 16384:32768].rearrange("p (b n) -> p b n", b=32)
            bpg = [Buf() for _ in range(32)]
            if half == 1:
                B.memset("pool", t["qbd"], 0.0, [bsm["qbd"]])
                for c in range(4):
                    for hh in range(2):
                        hcol = (2 * c + hh) * TS
                        B.copy("pool", qbd[hh * 64:(hh + 1) * 64, c, hcol:hcol + TS],
                               QbTs[hh * 64:(hh + 1) * 64, c, qs], [bsm["QbTs"]], [bsm["qbd"]])
            for j in range(32):
                pg = half * 32 + j
                kvst, bkv = B.kvst[j % 2], B.bkvst[j % 2]
                _dyn_page_dma(B, kvst[:, 320:1344], B.ckvb, pts, q * NPAGES + pg, [bsm["pts"]], [bkv])
                B.kv_build(kvst, bkv, 128, dKb=KbTh[:, :, j * 128:(j + 1) * 128], dVb=Vbh[:, j, :], bdest=bpg[j])
            blocks = []
            if half == 1:
                blocks.append(dict(nk=TS, kT=(lambda c: KTn[:, 3 + c, qs]),
                                   v=(lambda c, q=q: t["Vbn"][:TS, q * 512 + c * 128:q * 512 + (c + 1) * 128]),
                                   masked=True, bk=bsm["KTn"]))
            for j in range(31, -1, -1):
                blocks.append(dict(nk=128, kT=(lambda c, j=j: KbTh[:, c, j * 128:(j + 1) * 128]),
                                   v=(lambda c, j=j: Vbh[:, j, c * 128:(c + 1) * 128]), masked=False, bk=bpg[j]))
            msk = t["msbs"][:TS, :].unsqueeze(1).to_broadcast([TS, 8, TS])
            B.sb_run_heads(qbd, bsm["qbd"], blocks, t["rbs"], bsm["rbs"], t["osb"], bsm["osb"], half == 1, msk)
            if half == 0:
                for c in range(4):
                    for hh in range(2):
                        hcol = c * 8 * TS + (2 * c + hh) * TS
                        B.copy("act", obTv[hh * 64:(hh + 1) * 64, c, 2048 + q * TS:2048 + (q + 1) * TS],
                               t["osb"][hh * 64:(hh + 1) * 64, hcol:hcol + TS], [bsm["osb"]], [B.bobT[4]])


def _phase_tail(B, do_sample):
    P = B.P
    B.sreset()
    B.common_scratch(1)
    h1, bh1 = B.salloc(4 * D, F32)
    xnTg, bxnTg = B.salloc(8 * 512, BF16)
    B.wk, B.bwk = B.salloc_n(2, 512, F32)
    t12, bt12 = B.salloc_n(2, 512, F32)
    wgs, bwgs = B.salloc_n(2, 8 * 256, BF16)
    wbrs, bwbrs = B.salloc_n(2, 8 * 128, BF16)
    wffs, bwffs = B.salloc_n(2, 8 * 256, BF16)
    yst, byst = B.salloc(D, F32)
    Wo = B.arena[:, 0:8192].rearrange("p (k n) -> p k n", k=8)
    Wd = B.arena[:, 8192:8192 + 22528].rearrange("p (k n) -> p k n", k=NFF)
    hT = B.arena[:, 30720:30720 + 11264].rearrange("p (k n) -> p k n", k=NFF)
    mT = B.arena[:, 41984:41984 + 4096].rearrange("p (k n) -> p k n", k=8)
    bWo, bWd, bhT, bmT = Buf(), Buf(), Buf(), Buf()
    B.load_w(Wo, B.w_o, 0, D, 8, bWo)
    for q4 in range(2):
        P.dma("pool", Wd[:, q4 * 11:(q4 + 1) * 11, :],
              B.w_d.rearrange("(k p) n -> p k n", p=128)[:, q4 * 11:(q4 + 1) * 11, :], writes=[bWd])
    h1v = h1.rearrange("p (i n) -> p i n", i=4)
    xnTv = xnTg.rearrange("p (c t) -> p c t", c=8)
    oaTv = B.oaT[:, :].rearrange("p (c t) -> p c t", c=4)
    obTv = B.obT[:, :].rearrange("p (c t) -> p c t", c=4)
    groups = [(512, 4, 128, B.xown, J * 512, B.y_own, B.boaT[J], B.bobT[J]) for J in range(4)]
    if do_sample:
        groups.append((16, 1, 16, B.xs, 0, B.y_s, B.boaT[4], B.bobT[4]))
    for gi, (W, ntile, rows, xsrc, r0, ydst, boa, bob) in enumerate(groups):
        co = gi * 512
        xt, bxt = B.xt[0], B.bxt[0]
        for i in range(ntile):
            P.dma("sp", xt[:rows, :], xsrc[r0 + i * rows:r0 + (i + 1) * rows, :], writes=[bxt])
            B.norm_T(xt[:rows, :], bxt, rows, 0, xnTv[:, :, i * 128:i * 128 + rows], bxnTg)
        for fc in range(8):
            wg, bwg = wgs[fc % 2].rearrange("p (k n) -> p k n", k=8), bwgs[fc % 2]
            wb, bwb = wbrs[fc % 2].rearrange("p (k n) -> p k n", k=8), bwbrs[fc % 2]
            B.load_w(wg[:, :, 0:128], B.w_in, 2888 + fc * 128, 2888 + (fc + 1) * 128, 8, bwg)
            B.load_w(wg[:, :, 128:256], B.w_in, 3912 + fc * 128, 3912 + (fc + 1) * 128, 8, bwg)
            B.load_w(wb[:, 0:4, :], B.w_bra, fc * 128, (fc + 1) * 128, 4, bwb)
            B.load_w(wb[:, 4:8, :], B.w_brb, fc * 128, (fc + 1) * 128, 4, bwb)
            for a in range(2):
                for kc in range(8):
                    B.mm(B.psA[a][:, 0:W], wg[:, kc, a * 128:(a + 1) * 128], xnTv[:, kc, 0:W], kc == 0, kc == 7,
                         [bwg, bxnTg], [B.bpsA[a]])
            for a, (oT, bo) in enumerate(((oaTv, boa), (obTv, bob))):
                for c4 in range(4):
                    B.mm(B.psB[a][:, 0:W], wb[:, a * 4 + c4, :], oT[:, c4, co:co + W], c4 == 0, c4 == 3,
                         [bwb, bo], [B.bpsB[a]])
            for a in range(2):
                B.act(B.wk[a][:, 0:W], B.psA[a][:, 0:W], AF.Sigmoid, [B.bpsA[a]], [B.bwk[a]])
                B.tt("dve", t12[a][:, 0:W], B.wk[a][:, 0:W], B.psB[a][:, 0:W], ALU.mult, [B.bwk[a], B.bpsB[a]],
                     [bt12[a]])
            B.tt("pool", mT[:, fc, 0:W], t12[0][:, 0:W], t12[1][:, 0:W], ALU.add, bt12, [bmT])
        for i in range(ntile):
            P.dma("sp", xt[:rows, :], xsrc[r0 + i * rows:r0 + (i + 1) * rows, :], writes=[bxt])
            for hh in range(2):
                for fc in range(8):
                    B.mm(B.psA[hh][:rows, :], mT[:, fc, i * 128:i * 128 + rows], Wo[:, fc, hh * 512:(hh + 1) * 512],
                         fc == 0, fc == 7, [bmT, bWo], [B.bpsA[hh]])
                B.tt("dve", h1v[:rows, i, hh * 512:(hh + 1) * 512], B.psA[hh][:rows, :],
                     xt[:rows, hh * 512:(hh + 1) * 512], ALU.add, [B.bpsA[hh], bxt], [bh1])
            B.norm_T(h1v[:rows, i, :], bh1, rows, 1, xnTv[:, :, i * 128:i * 128 + rows], bxnTg)
        for f in range(NFF):
            wf, bwf = wffs[f % 2].rearrange("p (k n) -> p k n", k=8), bwffs[f % 2]
            B.load_w(wf[:, :, 0:128], B.w_g, f * 128, (f + 1) * 128, 8, bwf)
            B.load_w(wf[:, :, 128:256], B.w_u, f * 128, (f + 1) * 128, 8, bwf)
            for a in range(2):
                for kc in range(8):
                    B.mm(B.psA[a][:, 0:W], wf[:, kc, a * 128:(a + 1) * 128], xnTv[:, kc, 0:W], kc == 0, kc == 7,
                         [bwf, bxnTg], [B.bpsA[a]])
            B.act(B.wk[f % 2][:, 0:W], B.psA[0][:, 0:W], AF.Silu, [B.bpsA[0]], [B.bwk[f % 2]])
            B.tt("dve", hT[:, f, 0:W], B.wk[f % 2][:, 0:W], B.psA[1][:, 0:W], ALU.mult,
                 [B.bwk[f % 2], B.bpsA[1]], [bhT])
        for i in range(ntile):
            for hh in range(2):
                for f in range(NFF):
                    B.mm(B.psB[hh][:rows, :], hT[:, f, i * 128:i * 128 + rows], Wd[:, f, hh * 512:(hh + 1) * 512],
                         f == 0, f == NFF - 1, [bhT, bWd], [B.bpsB[hh]])
                B.tt("dve", h1v[:rows, i, hh * 512:(hh + 1) * 512], B.psB[hh][:rows, :],
                     h1v[:rows, i, hh * 512:(hh + 1) * 512], ALU.add, [B.bpsB[hh], bh1], [bh1])
            ss, rstd = B.st[:rows, 0:1], B.st[:rows, 1:2]
            B.act(B.junk[:rows, :], h1v[:rows, i, :], AF.Square, [bh1, B.bst], [B.bjunk, B.bst], accum_out=ss)
            B.act(rstd, ss, AF.Ln, [B.bst], [B.bst], scale=1.0 / D, bias=1e-6)
            B.act(rstd, rstd, AF.Exp, [B.bst], [B.bst], scale=-0.5)
            B.stt("dve", yst[:rows, :], h1v[:rows, i, :], rstd, B.g[:rows, 2, :], ALU.mult, ALU.mult,
                  [bh1, B.bst, B.bg], [byst])
            P.dma("sp", ydst[r0 + i * rows:r0 + (i + 1) * rows, :], yst[:rows, :], reads=[byst])


def build_program(stage=9, debug=False, npool=2560):
    B = Builder(debug=debug, npool=npool)
    B.load_consts()
    _phase1(B)
    if stage >= 2:
        _phase_sb(B)
    if stage >= 3:
        _phase_dsa(B)
    if stage >= 5:
        _phase_sample(B)
    if stage >= 4:
        _phase_tail(B, stage >= 5)
    if debug:
        B.P.barrier()
        B.P.dma("pool", B.dbg_ob, B.obT[:, :], reads=B.bobT)
        B.P.dma("pool", B.dbg_oa, B.oaT[:, :], reads=B.boaT)
    B.P.emit()
    return B.nc


_ROLE_TILES = None


def _own_tiles(r):
    return [8 * J + 2 * i + r for J in range(4) for i in range(4)]


def _prep(x_prompt, x_sample, cache_kv_a, cache_k_idx, cache_kv_b, page_table, w_in, w_br_a, w_br_b, w_o,
          norm_attn, norm_ffn, w_ffn_gate, w_ffn_up, w_ffn_down, norm_final, cores=range(8), npool=2560):
    f32 = np.float32
    x_prompt = np.asarray(x_prompt, f32)
    x_sample = np.asarray(x_sample, f32)
    ckva = np.ascontiguousarray(np.asarray(cache_kv_a, f32)[0].reshape(-1, 128, 256)[:npool])
    cki = np.ascontiguousarray(np.asarray(cache_k_idx, f32)[0].reshape(-1, 128, 64)[:npool])
    ckvb = np.ascontiguousarray(np.asarray(cache_kv_b, f32)[0].reshape(-1, 128, 1024)[:npool])
    page_table = np.asarray(page_table, np.int32)
    gains = np.stack([np.asarray(norm_attn, f32)[0], np.asarray(norm_ffn, f32)[0], np.asarray(norm_final, f32)])
    cs_all = _rope_tables(np.arange(SEQ))
    cs_s = _rope_tables(8192 + np.arange(TS))
    sidx = np.arange(128)
    consts = np.stack([np.eye(128, dtype=f32),
                       np.where(sidx[:, None] >= sidx[None, :], -8.0, 0.0).astype(f32),
                       np.full((128, 128), -8.0, f32)])
    in_maps = []
    for c in cores:
        b, r = c // 2, c % 2
        tiles = _own_tiles(r)
        rows = np.concatenate([np.arange(t * 128, (t + 1) * 128) for t in tiles])
        qpos = np.concatenate([np.arange((2 * i + r) * 128, (2 * i + r + 1) * 128) for i in range(4)])
        kpos = np.arange(1024)
        msb = (kpos[:, None] < qpos[None, :]).astype(f32).reshape(8, 128, 512)
        mdsa = np.stack([np.where(kpos[None, :] <= qpos[i * 128:(i + 1) * 128, None], 0.0, NEG).astype(f32)
                         for i in range(4)])
        tpos = np.arange(TS)
        in_maps.append({
            "xall": x_prompt[b], "xown": np.ascontiguousarray(x_prompt[b][rows]),
            "xs": np.ascontiguousarray(x_sample[4 * c:4 * c + 4].reshape(16, D)),
            "pt": np.ascontiguousarray(page_table[4 * c:4 * c + 4].reshape(1, 256)),
            "ckva": ckva, "cki": cki, "ckvb": ckvb,
            "w_in": np.asarray(w_in, f32)[0], "w_bra": np.asarray(w_br_a, f32)[0],
            "w_brb": np.asarray(w_br_b, f32)[0], "w_o": np.asarray(w_o, f32)[0],
            "w_g": np.asarray(w_ffn_gate, f32)[0], "w_u": np.asarray(w_ffn_up, f32)[0],
            "w_d": np.asarray(w_ffn_down, f32)[0], "gains": gains,
            "cs_all": cs_all, "cs_own": np.ascontiguousarray(cs_all[rows]), "cs_s": cs_s,
            "msb": msb, "mdsa": mdsa,
            "msb_s": (tpos[:, None] < tpos[None, :]).astype(f32),
            "md_s": np.where(tpos[None, :] <= tpos[:, None], 0.0, NEG).astype(f32),
            "consts": consts, "pcol": np.arange(128, dtype=f32).reshape(128, 1),
        })
    return in_maps


def kernel(x_prompt, x_sample, cache_kv_a, cache_k_idx, cache_kv_b, page_table, w_in, w_br_a, w_br_b, w_o,
           norm_attn, norm_ffn, w_ffn_gate, w_ffn_up, w_ffn_down, norm_final):
    f32 = np.float32
    in_maps = _prep(x_prompt, x_sample, cache_kv_a, cache_k_idx, cache_kv_b, page_table, w_in, w_br_a, w_br_b,
                    w_o, norm_attn, norm_ffn, w_ffn_gate, w_ffn_up, w_ffn_down, norm_final)
    nc = build_program(stage=5)
    res = run_bass_kernel_spmd(nc, in_maps, core_ids=list(range(8)))
    R = res.results
    y_p = np.zeros((4, SEQ, D), f32)
    y_s = np.zeros((32, TS, D), f32)
    kva_p = np.zeros((1, 4, SEQ, 2, 2, 64), f32)
    ki_p = np.zeros((1, 4, SEQ, 64), f32)
    kvb_p = np.zeros((1, 4, SEQ, 2, 8, 64), f32)
    kva_s = np.zeros((1, 32, TS, 2, 2, 64), f32)
    ki_s = np.zeros((1, 32, TS, 64), f32)
    kvb_s = np.zeros((1, 32, TS, 2, 8, 64), f32)
    for c in range(8):
        b, r = c // 2, c % 2
        tiles = _own_tiles(r)
        yo = R[c]["y_own"].reshape(16, 128, D)
        for j, t in enumerate(tiles):
            y_p[b, t * 128:(t + 1) * 128] = yo[j]
        y_s[4 * c:4 * c + 4] = R[c]["y_s"].reshape(4, TS, D)
        if r == 0:
            kv = R[c]["kv_all"]
            kva_p[0, b] = kv[:, 0:256].reshape(SEQ, 2, 2, 64)
            ki_p[0, b] = kv[:, 256:320]
            kvb_p[0, b] = kv[:, 320:1344].reshape(SEQ, 2, 8, 64)
        kvs = R[c]["kv_s"].reshape(4, TS, 1344)
        kva_s[0, 4 * c:4 * c + 4] = kvs[:, :, 0:256].reshape(4, TS, 2, 2, 64)
        ki_s[0, 4 * c:4 * c + 4] = kvs[:, :, 256:320]
        kvb_s[0, 4 * c:4 * c + 4] = kvs[:, :, 320:1344].reshape(4, TS, 2, 8, 64)
    return (y_p, y_s, kva_p, ki_p, kvb_p, kva_s, ki_s, kvb_s)
```

```python
from contextlib import ExitStack
import numpy as np
import concourse.bass as bass
import concourse.mybir as mybir
from concourse.bass_utils import run_bass_kernel_spmd

F32 = mybir.dt.float32
BF16 = mybir.dt.bfloat16
I32 = mybir.dt.int32
AF = mybir.ActivationFunctionType
ALU = mybir.AluOpType
AX = mybir.AxisListType

D = 1024
SEQ = 4096
NSEQ_S = 4
TS = 4
NPAGES = 64
DFF = 2816
NFF = 22
TOPK = 256
NBIS = 16
ENG = ("pe", "act", "dve", "pool", "sp")
NEG = -1.0e30


class Buf:
    __slots__ = ("w", "r")

    def __init__(self):
        self.w = []
        self.r = []


class Op:
    __slots__ = ("id", "eng", "fn", "deps", "cost", "isdma", "skip_self", "tok", "waits")

    def __init__(self, id_, eng, fn, deps, cost, isdma, skip_self):
        self.id, self.eng, self.fn, self.deps, self.cost = id_, eng, fn, deps, cost
        self.isdma, self.skip_self = isdma, skip_self
        self.tok = None
        self.waits = None


class Prog:
    def __init__(self, nc, n_dma_sems=12):
        self.nc = nc
        self.es = ExitStack()
        self.sem = {e: self.es.enter_context(nc.semaphore("c_" + e)) for e in ENG}
        self.dq = ("sp", "pool", "act")
        self.dsem = {q: [self.es.enter_context(nc.semaphore("d%s%d" % (q, i))) for i in range(n_dma_sems)]
                     for q in self.dq}
        self.epochs = [[]]
        self.nops = 0

    def sb(self, name, shape, dt):
        return self.es.enter_context(self.nc.sbuf_tensor(name, list(shape), dt))

    def ps(self, name, shape, dt):
        return self.es.enter_context(self.nc.psum_tensor(name, list(shape), dt))

    def _record(self, eng, fn, reads, writes, cost, isdma, skip_self):
        deps = set()
        for b in reads:
            deps.update(b.w)
        for b in writes:
            deps.update(b.w)
            deps.update(b.r)
        op = Op(self.nops, eng, fn, deps, cost, isdma, skip_self)
        self.nops += 1
        for b in reads:
            b.r.append(op.id)
        for b in writes:
            b.w = [op.id]
            b.r = []
        self.epochs[-1].append(op)

    def op(self, eng, fn, reads=(), writes=(), skip_self=False, cost=0.3):
        self._record(eng, fn, reads, writes, cost, False, skip_self)

    def dma_fn(self, eng, fn, reads=(), writes=(), cost=2.5):
        self._record(eng, fn, reads, writes, cost, True, False)

    def dma(self, eng, out, in_, reads=(), writes=(), cost=2.5):
        self.dma_fn(eng, lambda e: e.dma_start(out=out, in_=in_), reads, writes, cost)

    def barrier(self):
        if self.epochs[-1]:
            self.epochs.append([])

    @staticmethod
    def _schedule(ops):
        import heapq
        ids = {o.id: o for o in ops}
        indeg = {}
        children = {}
        for o in ops:
            d = [x for x in o.deps if x in ids]
            o.deps = d
            indeg[o.id] = len(d)
            for x in d:
                children.setdefault(x, []).append(o.id)
        finish = {}
        engfree = {e: 0.0 for e in ENG}
        heaps = {e: [] for e in ENG}
        for o in ops:
            if indeg[o.id] == 0:
                heapq.heappush(heaps[o.eng], (0.0, o.id))
        order = {e: [] for e in ENG}
        left = len(ops)
        while left:
            best = None
            for e in ENG:
                if heaps[e]:
                    rt, oid = heaps[e][0]
                    st = rt if rt > engfree[e] else engfree[e]
                    if best is None or (st, oid) < best[:2]:
                        best = (st, oid, e)
            st, oid, e = best
            heapq.heappop(heaps[e])
            o = ids[oid]
            if o.isdma:
                engfree[e] = st + 0.08
                finish[oid] = st + o.cost
            else:
                engfree[e] = st + o.cost
                finish[oid] = st + o.cost + 0.06
            order[e].append(o)
            left -= 1
            for c in children.get(oid, ()):
                indeg[c] -= 1
                if indeg[c] == 0:
                    oc = ids[c]
                    rt = max(finish[x] for x in oc.deps)
                    heapq.heappush(heaps[oc.eng], (rt, c))
        return order

    def emit(self):
        ndm = len(self.dsem["sp"])
        cap = {"sp": 8, "pool": 4, "act": 4}
        cnt = {e: 0 for e in ENG}
        dval = {q: [0] * ndm for q in self.dq}
        dnext = {q: 0 for q in self.dq}
        seen = {e: {} for e in ENG}
        stream = {e: [] for e in ENG}
        for ops in self.epochs:
            if not ops:
                continue
            order = self._schedule(ops)
            ids = {o.id: o for o in ops}
            pre = {}
            for e in ENG:
                for o in order[e]:
                    if o.isdma:
                        i = dnext[e]
                        dnext[e] = (i + 1) % cap[e]
                        pre[o.id] = (("d", e, i), dval[e][i])
                        dval[e][i] += 16
                        o.tok = (("d", e, i), dval[e][i])
                    else:
                        cnt[e] += 1
                        o.tok = (("e", e), cnt[e])
            for e in ENG:
                sn = seen[e]
                for o in order[e]:
                    need = {}
                    for x in o.deps:
                        k, v = ids[x].tok
                        if o.skip_self and k == ("e", e):
                            continue
                        if need.get(k, 0) < v:
                            need[k] = v
                    if o.isdma:
                        k, v = pre[o.id]
                        if v > 0 and need.get(k, 0) < v:
                            need[k] = v
                    waits = []
                    for k, v in need.items():
                        if sn.get(k, 0) < v:
                            sn[k] = v
                            waits.append((k, v))
                    stream[e].append((waits, o.fn, (o.tok[0], 16 if o.isdma else 1)))
            for e in ENG:
                waits = []
                for f in ENG:
                    k = ("e", f)
                    if cnt[f] > 0 and seen[e].get(k, 0) < cnt[f]:
                        seen[e][k] = cnt[f]
                        waits.append((k, cnt[f]))
                for q in self.dq:
                    for i in range(ndm):
                        k = ("d", q, i)
                        v = dval[q][i]
                        if v > 0 and seen[e].get(k, 0) < v:
                            seen[e][k] = v
                            waits.append((k, v))
                if waits:
                    stream[e].append((waits, None, None))
        self._check(stream)

        def semof(k):
            return self.sem[k[1]] if k[0] == "e" else self.dsem[k[1]][k[2]]

        engmap = {"pe": "tensor", "act": "scalar", "dve": "vector", "pool": "gpsimd", "sp": "sync"}
        with self.nc.Block() as block:
            for e in ENG:
                ops = stream[e]

                def body(engine, ops=ops):
                    for waits, fn, inc in ops:
                        for k, v in waits:
                            engine.wait_ge(semof(k), v)
                        if fn is not None:
                            fn(engine).then_inc(semof(inc[0]), inc[1])

                getattr(block, engmap[e])(body)
        self.es.close()

    @staticmethod
    def _check(stream):
        val = {}
        pos = {e: 0 for e in ENG}
        progress = True
        while progress:
            progress = False
            for e in ENG:
                while pos[e] < len(stream[e]):
                    waits, fn, inc = stream[e][pos[e]]
                    if any(val.get(k, 0) < v for k, v in waits):
                        break
                    if inc is not None:
                        val[inc[0]] = val.get(inc[0], 0) + inc[1]
                    pos[e] += 1
                    progress = True
        for e in ENG:
            assert pos[e] == len(stream[e]), ("schedule deadlock", e, pos[e], len(stream[e]))


class Builder:
    def __init__(self, debug=False, npool=2560):
        nc = bass.Bass("TRN2", target_bir_lowering=False)
        self.nc = nc
        self.P = Prog(nc)
        P = self.P

        def din(name, shape, dt=F32):
            return nc.dram_tensor(name, list(shape), dt, kind="ExternalInput").ap()

        def dout(name, shape):
            return nc.dram_tensor(name, list(shape), F32, kind="ExternalOutput").ap()

        self.xall = din("xall", [SEQ, D])
        self.xown = din("xown", [SEQ // 2, D])
        self.xs = din("xs", [NSEQ_S * TS, D])
        self.pt = din("pt", [1, NSEQ_S * NPAGES], I32)
        self.ckva = din("ckva", [npool, 128, 256])
        self.cki = din("cki", [npool, 128, 64])
        self.ckvb = din("ckvb", [npool, 128, 1024])
        self.w_in = din("w_in", [D, 4936])
        self.w_bra = din("w_bra", [512, D])
        self.w_brb = din("w_brb", [512, D])
        self.w_o = din("w_o", [D, D])
        self.w_g = din("w_g", [D, DFF])
        self.w_u = din("w_u", [D, DFF])
        self.w_d = din("w_d", [DFF, D])
        self.gains = din("gains", [3, D])
        self.cs_all = din("cs_all", [SEQ, 64])
        self.cs_own = din("cs_own", [SEQ // 2, 64])
        self.cs_s = din("cs_s", [TS, 64])
        self.msb = din("msb", [8, 128, 512])
        self.mdsa = din("mdsa", [4, 128, 1024])
        self.msb_s = din("msb_s", [TS, TS])
        self.md_s = din("md_s", [TS, TS])
        self.consts = din("consts", [3, 128, 128])
        self.pcol = din("pcol", [128, 1])
        self.y_own = dout("y_own", [SEQ // 2, D])
        self.y_s = dout("y_s", [NSEQ_S * TS, D])
        self.kv_all = dout("kv_all", [SEQ, 1344])
        self.kv_s = dout("kv_s", [NSEQ_S * TS, 1344])
        if debug:
            self.dbg_ob = dout("dbg_ob", [128, 4 * 2064])
            self.dbg_oa = dout("dbg_oa", [128, 4 * 2064])
            self.dbg_st = dout("dbg_st", [128, 8])
            self.dbg_sc = dout("dbg_sc", [128, 1024])

        self.ident = P.sb("ident", [128, 128], BF16)
        self.negtri = P.sb("negtri", [128, 128], BF16)
        self.negones = P.sb("negones", [128, 128], BF16)
        self.bconst = Buf()
        self.g = P.sb("g", [128, 3, D], F32)
        self.bg = Buf()
        self.psT = [P.ps("psT%d" % i, [128, 1024], BF16) for i in range(2)]
        self.bpsT = [Buf(), Buf()]
        self.psA = [P.ps("psA%d" % i, [128, 512], F32) for i in range(3)]
        self.bpsA = [Buf(), Buf(), Buf()]
        self.psB = [P.ps("psB%d" % i, [128, 512], F32) for i in range(2)]
        self.bpsB = [Buf(), Buf()]
        self.psO = P.ps("psO", [128, 512], F32)
        self.bpsO = Buf()
        self.arena = P.sb("arena", [128, 49152], BF16)
        self.oaT = P.sb("oaT", [128, 4 * 2064], BF16)
        self.obT = P.sb("obT", [128, 4 * 2064], BF16)
        self.boaT = [Buf() for _ in range(5)]
        self.bobT = [Buf() for _ in range(5)]
        self.st = P.sb("st", [128, 8], F32)
        self.bst = Buf()
        self.SCR = 33280
        self.scr = P.sb("scr", [128, self.SCR], BF16)
        self.sptr = 0
        self.nxt = 0

    def sreset(self):
        self.P.barrier()
        self.sptr = 0

    def salloc(self, cols, dt):
        n = cols * (2 if dt == F32 else 1)
        n = (n + 1) // 2 * 2
        assert self.sptr + n <= self.SCR, ("scratch overflow", self.sptr, n)
        v = self.scr[:, self.sptr:self.sptr + n]
        self.sptr += n
        return (v.bitcast(F32) if dt == F32 else v), Buf()

    def salloc_n(self, k, cols, dt):
        r = [self.salloc(cols, dt) for _ in range(k)]
        return [x[0] for x in r], [x[1] for x in r]

    def common_scratch(self, nxt=2):
        self.xt, self.bxt = self.salloc_n(nxt, D, F32)
        self.xn, self.bxn = self.salloc(D, BF16)
        self.junk, self.bjunk = self.salloc(D, BF16)

    @staticmethod
    def _fs(ap):
        n = 1
        for d in ap.shape[1:]:
            n *= int(d)
        return n

    def _cost(self, eng, ap):
        n = self._fs(ap)
        if eng == "pe":
            return 0.05 + n / 2400.0
        if eng == "act":
            return 0.2 + n / 960.0
        if eng == "dve":
            return 0.08 + n / 960.0
        return 0.1 + n / 480.0

    def mm(self, out, lhsT, rhs, start, stop, reads, writes):
        self.P.op("pe", lambda e: e.matmul(out, lhsT=lhsT, rhs=rhs, start=start, stop=stop),
                  reads, writes, skip_self=True, cost=self._cost("pe", out))

    def tr(self, out, in_, rows, reads, writes):
        ident = self.ident[:rows, :rows]
        self.P.op("pe", lambda e: e.transpose(out=out, in_=in_, identity=ident),
                  list(reads) + [self.bconst], writes, skip_self=True, cost=self._cost("pe", out))

    def act(self, out, in_, func, reads, writes, **kw):
        self.P.op("act", lambda e: e.activation(out=out, in_=in_, func=func, **kw), reads, writes,
                  cost=self._cost("act", in_))

    def copy(self, eng, out, in_, reads, writes):
        c = self._cost(eng, out)
        if eng == "act":
            self.P.op("act", lambda e: e.copy(out=out, in_=in_), reads, writes, cost=c)
        else:
            self.P.op(eng, lambda e: e.tensor_copy(out=out, in_=in_), reads, writes, cost=c)

    def tt(self, eng, out, in0, in1, op, reads, writes):
        self.P.op(eng, lambda e: e.tensor_tensor(out=out, in0=in0, in1=in1, op=op), reads, writes,
                  cost=self._cost(eng, out))

    def ts(self, eng, out, in0, s1, s2, op0, op1, reads, writes, accum=None):
        c = self._cost(eng, in0)
        if accum is None:
            if op1 is None:
                self.P.op(eng, lambda e: e.tensor_scalar(out=out, in0=in0, scalar1=s1, scalar2=None, op0=op0),
                          reads, writes, cost=c)
            else:
                self.P.op(eng, lambda e: e.tensor_scalar(out=out, in0=in0, scalar1=s1, scalar2=s2, op0=op0,
                                                         op1=op1), reads, writes, cost=c)
        else:
            self.P.op(eng, lambda e: e.tensor_scalar(out=out, in0=in0, scalar1=s1, scalar2=s2, op0=op0, op1=op1,
                                                     accum_out=accum), reads, writes, cost=c)

    def stt(self, eng, out, in0, scalar, in1, op0, op1, reads, writes, accum=None):
        c = self._cost(eng, in0)
        if accum is None:
            self.P.op(eng, lambda e: e.scalar_tensor_tensor(out=out, in0=in0, scalar=scalar, in1=in1, op0=op0,
                                                            op1=op1), reads, writes, cost=c)
        else:
            self.P.op(eng, lambda e: e.scalar_tensor_tensor(out=out, in0=in0, scalar=scalar, in1=in1, op0=op0,
                                                            op1=op1, accum_out=accum), reads, writes, cost=c)

    def memset(self, eng, ap, val, writes):
        self.P.op(eng, lambda e: e.memset(ap, val), (), writes, cost=self._cost(eng, ap))

    def load_consts(self):
        P = self.P
        P.dma("pool", self.ident[:], self.consts[0], writes=[self.bconst])
        P.dma("pool", self.negtri[:], self.consts[1], writes=[self.bconst])
        P.dma("pool", self.negones[:], self.consts[2], writes=[self.bconst])
        for i in range(3):
            P.dma("sp", self.g[:, i, :], self.gains[i].partition_broadcast(128), writes=[self.bg])

    def load_w(self, dst, src_rows_view, c0, c1, nk, wb):
        src = src_rows_view.rearrange("(k p) n -> p k n", p=128)[:, :, c0:c1]
        self.P.dma("pool", dst, src, writes=[wb])

    def norm_T(self, x_ap, bx, rows, gi, dest, bdest, ncols_dest_off=0):
        ss, rstd = self.st[:rows, 0:1], self.st[:rows, 1:2]
        self.memset("pool", ss, 0.0, [self.bst])
        self.act(self.junk[:rows, :], x_ap, AF.Square, [bx, self.bst], [self.bjunk, self.bst], accum_out=ss)
        self.act(rstd, ss, AF.Ln, [self.bst], [self.bst], scale=1.0 / D, bias=1e-6)
        self.act(rstd, rstd, AF.Exp, [self.bst], [self.bst], scale=-0.5)
        self.stt("dve", self.xn[:rows, :], x_ap, rstd, self.g[:rows, gi, :], ALU.mult, ALU.mult,
                 [bx, self.bst, self.bg], [self.bxn])
        k = self.nxt = (self.nxt + 1) % 2
        ps = self.psT[k][:].rearrange("p (c t) -> p c t", c=8)
        for kc in range(8):
            self.tr(ps[:, kc, 0:rows], self.xn[:rows, kc * 128:(kc + 1) * 128], rows, [self.bxn], [self.bpsT[k]])
        self.copy("act", dest, ps[:, :, 0:rows], [self.bpsT[k]], [bdest])

    def rope(self, src, dst, cs, rows, h, reads, writes):
        x1, x2 = src[:, :, 0:32], src[:, :, 32:64]
        cb = cs[:rows, 0:32].unsqueeze(1).to_broadcast([rows, h, 32])
        sb = cs[:rows, 32:64].unsqueeze(1).to_broadcast([rows, h, 32])
        t = [self.rt[i][:rows, 0:h * 32].rearrange("p (h j) -> p h j", h=h) for i in range(4)]
        bt = self.brt
        self.tt("dve", t[0], x1, cb, ALU.mult, reads, [bt[0]])
        self.tt("dve", t[1], x2, sb, ALU.mult, reads, [bt[1]])
        self.tt("dve", t[2], x2, cb, ALU.mult, reads, [bt[2]])
        self.tt("dve", t[3], x1, sb, ALU.mult, reads, [bt[3]])
        self.tt("pool", dst[:, :, 0:32], t[0], t[1], ALU.subtract, [bt[0], bt[1]], writes)
        self.tt("pool", dst[:, :, 32:64], t[2], t[3], ALU.add, [bt[2], bt[3]], writes)

    def kv_build(self, kvst, bkv, rows, dKa=None, dKb=None, dVa=None, dVb=None, bdest=None,
                 ceng=("pool", "pool")):
        if dVa is not None:
            self.copy(ceng[1], dVa, kvst[:rows, 128:256], [bkv], [bdest])
        if dVb is not None:
            self.copy(ceng[1], dVb, kvst[:rows, 832:1344], [bkv], [bdest])
        c0 = 0 if dKa is not None else 320
        c1 = 832 if dKb is not None else 320
        self.copy(ceng[0], self.kst[:rows, c0:c1], kvst[:rows, c0:c1], [bkv], [self.bkst])
        k = self.nxt = (self.nxt + 1) % 2
        ps = self.psT[k][:].rearrange("p (c t) -> p c t", c=8)
        if dKa is not None:
            for s, col in enumerate((0, 64, 256)):
                for half in range(2):
                    self.tr(ps[half * 64:(half + 1) * 64, s, 0:rows], self.kst[:rows, col:col + 64], rows,
                            [self.bkst], [self.bpsT[k]])
            self.copy("act", dKa, ps[:, 0:3, 0:rows], [self.bpsT[k]], [bdest])
        if dKb is not None:
            for j in range(4):
                self.tr(ps[:, 3 + j, 0:rows], self.kst[:rows, 320 + 128 * j:320 + 128 * (j + 1)], rows,
                        [self.bkst], [self.bpsT[k]])
            self.copy("act", dKb, ps[:, 3:7, 0:rows], [self.bpsT[k]], [bdest])

    def kv_proj(self, xnT, bxnT, rows, cs, bcs, kvst, bkv, wkv):
        groups = ((0, 320, 0), (320, 832, 1), (832, 1344, 2))
        for c0, c1, a in groups:
            for kc in range(8):
                self.mm(self.psA[a][:rows, 0:c1 - c0], xnT[:, kc, 0:rows], wkv[:, kc, c0:c1], kc == 0, kc == 7,
                        [bxnT, self.bwbuf], [self.bpsA[a]])
        p0 = self.psA[0]
        self.rope(p0[:rows, 0:128].rearrange("p (h j) -> p h j", h=2),
                  kvst[:rows, 0:128].rearrange("p (h j) -> p h j", h=2), cs, rows, 2,
                  [self.bpsA[0], bcs], [bkv])
        self.rope(p0[:rows, 256:320].rearrange("p (h j) -> p h j", h=1),
                  kvst[:rows, 256:320].rearrange("p (h j) -> p h j", h=1), cs, rows, 1,
                  [self.bpsA[0], bcs], [bkv])
        self.copy("act", kvst[:rows, 128:256], p0[:rows, 128:256], [self.bpsA[0]], [bkv])
        self.copy("act", kvst[:rows, 320:832], self.psA[1][:rows, :], [self.bpsA[1]], [bkv])
        self.copy("dve", kvst[:rows, 832:1344], self.psA[2][:rows, :], [self.bpsA[2]], [bkv])

    def sb_run(self, W, qT, bq, blocks, Rb, bRb, oacc, boacc, first, last, dest, bdest, part=None, bpart=None):
        n = 0
        for h in range(8):
            for bi, blk in enumerate(blocks):
                nk = blk["nk"]
                isfirst = first and bi == 0
                islast = last and bi == len(blocks) - 1
                n += 1
                za, zb = self.psA[n % 2], self.bpsA[n % 2]
                ca, cb = self.psB[n % 2], self.bpsB[n % 2]
                e_sb, be = self.wk[n % 2], self.bwk[n % 2]
                L, bL = self.wkb[n % 2], self.bwkb[n % 2]
                a_sb, ba = self.wkb[2 + n % 2], self.bwkb[2 + n % 2]
                kT, q = blk["kT"](h), qT(h)
                self.mm(za[:nk, 0:W], kT, q, True, True, [blk["bk"], bq], [zb])
                self.act(e_sb[:nk, 0:W], za[:nk, 0:W], AF.Exp, [zb], [be], scale=0.125)
                self.act(L[:nk, 0:W], e_sb[:nk, 0:W], AF.Ln, [be], [bL], bias=1.0)
                if blk["mask"] is not None:
                    self.tt("dve", L[:nk, 0:W], L[:nk, 0:W], blk["mask"], ALU.mult, [bL, blk["bm"]], [bL])
                self.mm(ca[:nk, 0:W], kT, q, True, False, [blk["bk"], bq], [cb])
                self.mm(ca[:nk, 0:W], self.negtri[:nk, :nk], L[:nk, 0:W], False, isfirst, [bL, self.bconst], [cb])
                if not isfirst:
                    self.mm(ca[:nk, 0:W], self.negones[:, :nk], Rb(h), False, True, [bRb(h), self.bconst], [cb])
                self.act(a_sb[:nk, 0:W], ca[:nk, 0:W], AF.Exp, [cb], [ba], scale=0.125)
                if blk["mask"] is not None:
                    self.tt("dve", a_sb[:nk, 0:W], a_sb[:nk, 0:W], blk["mask"], ALU.mult, [ba, blk["bm"]], [ba])
                self.mm(oacc(h), blk["v"](h), a_sb[:nk, 0:W], bi == 0, bi == len(blocks) - 1, [blk["bk"], ba],
                        [boacc(h)])
                if isfirst:
                    self.memset("pool", Rb(h), 0.0, [bRb(h)])
                    self.copy("dve", Rb(h)[:nk, :], L[:nk, 0:W], [bL], [bRb(h)])
                elif not islast:
                    self.tt("dve", Rb(h)[:nk, :], Rb(h)[:nk, :], L[:nk, 0:W], ALU.add, [bL, bRb(h)], [bRb(h)])
            if first and last:
                self.copy("act", dest(h), oacc(h), [boacc(h)], [bdest])
            elif first:
                self.copy("act", part(h), oacc(h), [boacc(h)], [bpart])
            elif last:
                self.tt("dve", dest(h), oacc(h), part(h), ALU.add, [boacc(h), bpart], [bdest])
            else:
                self.tt("dve", part(h), oacc(h), part(h), ALU.add, [boacc(h), bpart], [bpart])

    def sb_run_heads(self, qbd, bq, blocks, Rb, bRb, osb, bosb, first, msk):
        W = 8 * TS
        for bi, blk in enumerate(blocks):
            nk = blk["nk"]
            isfirst = first and bi == 0
            n = bi
            za, zb = self.psA[n % 2], self.bpsA[n % 2]
            ca, cb = self.psB[n % 2], self.bpsB[n % 2]
            pv, bpv = (self.psO, self.bpsO) if n % 2 == 0 else (self.psA[2], self.bpsA[2])
            e_sb, be = self.wk[n % 2], self.bwk[n % 2]
            L, bL = self.wkb[n % 2], self.bwkb[n % 2]
            a_sb, ba = self.wkb[2 + n % 2], self.bwkb[2 + n % 2]
            for c in range(4):
                self.mm(za[:nk, 0:W], blk["kT"](c), qbd[:, c, :], c == 0, c == 3, [blk["bk"], bq], [zb])
            self.act(e_sb[:nk, 0:W], za[:nk, 0:W], AF.Exp, [zb], [be], scale=0.125)
            self.act(L[:nk, 0:W], e_sb[:nk, 0:W], AF.Ln, [be], [bL], bias=1.0)
            if blk["masked"]:
                self.tt("dve", L[:nk, 0:W].rearrange("p (h t) -> p h t", h=8),
                        L[:nk, 0:W].rearrange("p (h t) -> p h t", h=8), msk, ALU.mult, [bL], [bL])
            for c in range(4):
                self.mm(ca[:nk, 0:W], blk["kT"](c), qbd[:, c, :], c == 0, False, [blk["bk"], bq], [cb])
            self.mm(ca[:nk, 0:W], self.negtri[:nk, :nk], L[:nk, 0:W], False, isfirst, [bL, self.bconst], [cb])
            if not isfirst:
                self.mm(ca[:nk, 0:W], self.negones[:, :nk], Rb, False, True, [bRb, self.bconst], [cb])
            self.act(a_sb[:nk, 0:W], ca[:nk, 0:W], AF.Exp, [cb], [ba], scale=0.125)
            if blk["masked"]:
                self.tt("dve", a_sb[:nk, 0:W].rearrange("p (h t) -> p h t", h=8),
                        a_sb[:nk, 0:W].rearrange("p (h t) -> p h t", h=8), msk, ALU.mult, [ba], [ba])
            for c in range(4):
                self.mm(pv[:, c * W:(c + 1) * W], blk["v"](c), a_sb[:nk, 0:W], True, True, [blk["bk"], ba], [bpv])
            if isfirst:
                self.copy("dve", osb, pv[:, 0:4 * W], [bpv], [bosb])
                self.memset("pool", Rb, 0.0, [bRb])
                self.copy("pool", Rb[:nk, :], L[:nk, 0:W], [bL], [bRb])
            else:
                self.tt("dve", osb, pv[:, 0:4 * W], osb, ALU.add, [bpv, bosb], [bosb])
                self.tt("pool", Rb[:nk, :], Rb[:nk, :], L[:nk, 0:W], ALU.add, [bL, bRb], [bRb])

    def dsa_tile(self, rows, qaT, qiT, bq, w_t, bw, chunks, score, bscore, otok, botok):
        P = self.P
        N = sum(c["nk"] for c in chunks)
        col = 0
        cols = []
        for c_ in chunks:
            nk, k0 = c_["nk"], c_["k0"]
            cols.append(col)
            for h in range(8):
                hp = (h % 2) * 64
                pa, bpa = self.psA[h % 2], self.bpsA[h % 2]
                r, br = self.wk[h % 2], self.bwk[h % 2]
                self.mm(pa[:rows, 0:nk], qiT(h), c_["KT"][hp:hp + 64, 2, k0:k0 + nk], True, True,
                        [bq] + c_["bk"], [bpa])
                self.act(r[:rows, 0:nk], pa[:rows, 0:nk], AF.Relu, [bpa], [br])
                sc = score[:rows, col:col + nk]
                if h == 0:
                    self.ts("dve", sc, r[:rows, 0:nk], w_t[:rows, 0:1], None, ALU.mult, None, [br, bw], [bscore])
                else:
                    self.stt("dve", sc, r[:rows, 0:nk], w_t[:rows, h:h + 1], sc, ALU.mult, ALU.add,
                             [br, bw, bscore], [bscore])
            col += nk
        st = self.stb
        lo, w0, mid, cnt, tq, rmax = (st[:rows, i:i + 1] for i in range(2, 8))
        bs = self.bstb
        sc = score[:rows, 0:N]
        sel = self.sel[:rows, 0:N]
        self.ts("dve", sel, sc, -1.0, -3.0e38, ALU.mult, ALU.max, [bscore, bs], [self.bsel, bs], accum=lo)
        for ci, c_ in enumerate(chunks):
            if c_["mb"] is not None:
                s_ = score[:rows, cols[ci]:cols[ci] + c_["nk"]]
                self.tt("pool", s_, s_, c_["mb"], ALU.add, [bscore, c_["bmb"]], [bscore])
        P.op("dve", lambda e: e.reduce_max(out=rmax, in_=sc, axis=AX.X), [bscore], [bs],
             cost=self._cost("dve", sc))
        self.ts("dve", lo, lo, -1.0, -1.0, ALU.mult, ALU.add, [bs], [bs])
        self.tt("dve", w0, rmax, lo, ALU.subtract, [bs], [bs])
        for it in range(NBIS):
            hstep = 2.0 ** -(it + 1)
            self.stt("dve", mid, w0, hstep, lo, ALU.mult, ALU.add, [bs], [bs])
            self.ts("dve", sel, sc, mid, 0.0, ALU.is_gt, ALU.add, [bscore, bs], [self.bsel, bs], accum=cnt)
            self.ts("dve", tq, cnt, float(TOPK), hstep, ALU.is_ge, ALU.mult, [bs], [bs])
            self.stt("dve", lo, tq, w0, lo, ALU.mult, ALU.add, [bs], [bs])
        self.ts("dve", sel, sc, lo, None, ALU.is_gt, None, [bscore, bs], [self.bsel])
        den = self.den
        nch = len(chunks)
        nblk_total = sum((c_["nk"] + 127) // 128 for c_ in chunks)
        for h in range(8):
            hp = (h % 2) * 64
            c = h // 4
            bdone = 0
            for ci, c_ in enumerate(chunks):
                nk, k0 = c_["nk"], c_["k0"]
                pa, bpa = self.psA[ci % 2], self.bpsA[ci % 2]
                p_sb, bp = self.wkb[ci % 2], self.bwkb[ci % 2]
                pm, bpm = self.wkb[2 + ci % 2], self.bwkb[2 + ci % 2]
                pT, bpT = self.wkb[4 + ci % 2], self.bwkb[4 + ci % 2]
                self.mm(pa[:rows, 0:nk], qaT(h), c_["KT"][hp:hp + 64, c, k0:k0 + nk], True, True,
                        [bq] + c_["bk"], [bpa])
                self.act(p_sb[:rows, 0:nk], pa[:rows, 0:nk], AF.Exp, [bpa], [bp], scale=0.125)
                self.stt("dve", pm[:rows, 0:nk], p_sb[:rows, 0:nk], 1.0, sel[:, cols[ci]:cols[ci] + nk], ALU.mult,
                         ALU.mult, [bp, self.bsel, self.bden], [bpm, self.bden], accum=den[:rows, ci:ci + 1])
                k = self.nxt = (self.nxt + 1) % 2
                ps = self.psT[k][:].rearrange("p (c t) -> p c t", c=8)
                nb = (nk + 127) // 128
                pTv = pT[:, 0:4 * 128].rearrange("p (c t) -> p c t", c=4)
                for j in range(nb):
                    kk = min(128, nk - j * 128)
                    self.tr(ps[:kk, j, 0:rows], pm[:rows, j * 128:j * 128 + kk], rows, [bpm], [self.bpsT[k]])
                kk_last = nk - (nb - 1) * 128
                if kk_last == 128:
                    self.copy("act", pTv[:, 0:nb, 0:rows], ps[:, 0:nb, 0:rows], [self.bpsT[k]], [bpT])
                else:
                    if nb > 1:
                        self.copy("act", pTv[:, 0:nb - 1, 0:rows], ps[:, 0:nb - 1, 0:rows], [self.bpsT[k]], [bpT])
                    self.copy("act", pTv[:kk_last, nb - 1, 0:rows], ps[:kk_last, nb - 1, 0:rows], [self.bpsT[k]],
                              [bpT])
                for j in range(nb):
                    kk = min(128, nk - j * 128)
                    self.mm(self.psO[:rows, 0:64], pTv[:kk, j, 0:rows], c_["va"](j)[:kk, c * 64:(c + 1) * 64],
                            bdone == 0, bdone == nblk_total - 1, [bpT] + c_["bk"], [self.bpsO])
                    bdone += 1
            dsum, rden = self.std[:rows, 0:1], self.std[:rows, 1:2]
            bsd = self.bstd
            P.op("dve", lambda e, dsum=dsum: e.reduce_sum(out=dsum, in_=den[:rows, 0:nch], axis=AX.X),
                 [self.bden], [bsd])
            P.op("dve", lambda e, dsum=dsum, rden=rden: e.reciprocal(out=rden, in_=dsum), [bsd], [bsd])
            self.ts("dve", otok[:rows, h * 64:(h + 1) * 64], self.psO[:rows, 0:64], rden, None, ALU.mult, None,
                    [self.bpsO, bsd], [botok])


def _rope_tables(pos):
    half = 32
    inv = (np.float32(10000.0) ** (-(np.arange(half, dtype=np.float32)) / np.float32(half))).astype(np.float32)
    ang = pos.astype(np.float32)[:, None] * inv[None, :]
    return np.concatenate([np.cos(ang), np.sin(ang)], axis=1).astype(np.float32)


def _phase1(B):
    P = B.P
    B.sreset()
    B.common_scratch(2)
    B.wbuf, B.bwbuf = B.salloc(8 * 1344, BF16)
    B.kvst, B.bkvst = B.salloc_n(2, 1344, F32)
    B.rt, B.brt = B.salloc_n(4, 256, F32)
    B.cs, B.bcs = B.salloc_n(2, 64, F32)
    B.kst, B.bkst = B.salloc(832, BF16)
    xnT1, bxnT1 = B.salloc(1024, BF16)
    wkv = B.wbuf.rearrange("p (k n) -> p k n", k=8)
    B.load_w(wkv[:, :, 0:256], B.w_in, 512, 768, 8, B.bwbuf)
    B.load_w(wkv[:, :, 256:320], B.w_in, 1280, 1344, 8, B.bwbuf)
    B.load_w(wkv[:, :, 320:1344], B.w_in, 1864, 2888, 8, B.bwbuf)
    B.KT = B.arena[:, 0:7 * SEQ].rearrange("p (s k) -> p s k", s=7)
    B.Vb = B.arena[:, 28672:28672 + 16384].rearrange("p (b n) -> p b n", b=32)
    B.Va = B.arena[:, 45056:45056 + 4096].rearrange("p (b n) -> p b n", b=32)
    B.bkvblk = [Buf() for _ in range(32)]
    xnT = xnT1.rearrange("p (c t) -> p c t", c=8)
    for tt in range(32):
        xt, bxt = B.xt[tt % 2], B.bxt[tt % 2]
        cs, bcs = B.cs[tt % 2], B.bcs[tt % 2]
        kvst, bkv = B.kvst[tt % 2], B.bkvst[tt % 2]
        P.dma("sp", xt[:, :], B.xall[tt * 128:(tt + 1) * 128, :], writes=[bxt])
        P.dma("sp", cs[:, :], B.cs_all[tt * 128:(tt + 1) * 128, :], writes=[bcs])
        B.norm_T(xt[:, :], bxt, 128, 0, xnT, bxnT1)
        B.kv_proj(xnT, bxnT1, 128, cs, bcs, kvst, bkv, wkv)
        P.dma("sp", B.kv_all[tt * 128:(tt + 1) * 128, :], kvst[:, :], reads=[bkv])
        B.kv_build(kvst, bkv, 128, dKa=B.KT[:, 0:3, tt * 128:(tt + 1) * 128],
                   dKb=B.KT[:, 3:7, tt * 128:(tt + 1) * 128], dVa=B.Va[:, tt, :], dVb=B.Vb[:, tt, :],
                   bdest=B.bkvblk[tt])


def _phase_sb(B):
    P = B.P
    B.sreset()
    B.common_scratch(2)
    B.wbuf, B.bwbuf = B.salloc(8 * 512, BF16)
    xnTg, bxnTg = B.salloc(8 * 512, BF16)
    qT, bqT = B.salloc(4 * 512, BF16)
    B.wk, B.bwk = B.salloc_n(2, 512, F32)
    B.wkb, B.bwkb = B.salloc_n(4, 512, BF16)
    rb, brb = B.salloc_n(2, 512, BF16)
    mask, bmask = B.salloc(8 * 512, BF16)
    wqb = B.wbuf.rearrange("p (k n) -> p k n", k=8)
    B.load_w(wqb, B.w_in, 1352, 1864, 8, B.bwbuf)
    maskv = mask.rearrange("p (j t) -> p j t", j=8)
    P.dma("pool", maskv, B.msb.rearrange("j p t -> p j t"), writes=[bmask])
    xnTv = xnTg.rearrange("p (c t) -> p c t", c=8)
    qTv = qT.rearrange("p (c t) -> p c t", c=4)
    obTv = B.obT[:, :].rearrange("p (c t) -> p c t", c=4)
    for J in range(4):
        for i in range(4):
            ot = J * 4 + i
            xt, bxt = B.xt[ot % 2], B.bxt[ot % 2]
            P.dma("sp", xt[:, :], B.xown[ot * 128:(ot + 1) * 128, :], writes=[bxt])
            B.norm_T(xt[:, :], bxt, 128, 0, xnTv[:, :, i * 128:(i + 1) * 128], bxnTg)
        for ch in range(4):
            for kc in range(8):
                B.mm(B.psB[ch % 2][:, :], wqb[:, kc, ch * 128:(ch + 1) * 128], xnTv[:, kc, :], kc == 0, kc == 7,
                     [B.bwbuf, bxnTg], [B.bpsB[ch % 2]])
            B.copy("dve", qTv[:, ch, :], B.psB[ch % 2][:, :], [B.bpsB[ch % 2]], [bqT])
        blocks = []
        for kb in range(8 * J + 7, -1, -1):
            j = kb - 8 * J
            blocks.append(dict(
                nk=128,
                kT=(lambda h, kb=kb: B.KT[(h % 2) * 64:(h % 2) * 64 + 64, 3 + h // 2, kb * 128:(kb + 1) * 128]),
                v=(lambda h, kb=kb: B.Vb[:, kb, h * 64:(h + 1) * 64]),
                mask=(maskv[:, j, :] if j >= 0 else None), bk=B.bkvblk[kb], bm=bmask))
        B.sb_run(512, (lambda h: qTv[(h % 2) * 64:(h % 2) * 64 + 64, h // 2, :]), bqT, blocks,
                 (lambda h: rb[h % 2]), (lambda h: brb[h % 2]),
                 (lambda h: (B.psO if h % 2 == 0 else B.psA[2])[0:64, :]),
                 (lambda h: (B.bpsO if h % 2 == 0 else B.bpsA[2])), True, True,
                 (lambda h, J=J: obTv[(h % 2) * 64:(h % 2) * 64 + 64, h // 2, J * 512:(J + 1) * 512]), B.bobT[J])


def _phase_dsa(B):
    P = B.P
    B.sreset()
    B.common_scratch(2)
    B.wbuf, B.bwbuf = B.salloc(8 * 1032, BF16)
    xnT1, bxnT1 = B.salloc(1024, BF16)
    B.cs, B.bcs = B.salloc_n(2, 64, F32)
    B.rt, B.brt = B.salloc_n(4, 256, F32)
    qsts, bqsts = B.salloc_n(2, 1024, BF16)
    qTts_, bqTts_ = B.salloc_n(2, 1024, BF16)
    w_ts, bws = B.salloc_n(2, 8, F32)
    B.wk, B.bwk = B.salloc_n(2, 512, F32)
    B.wkb, B.bwkb = B.salloc_n(6, 512, BF16)
    maskD, bmaskD = B.salloc_n(2, 1024, F32)
    otoks, botoks = B.salloc_n(2, 512, BF16)
    dens, bdens = B.salloc_n(2, 32, F32)
    stbs, bstbs = B.salloc_n(2, 8, F32)
    stds, bstds = B.salloc_n(2, 8, F32)
    scores = [B.arena[:, 28672:28672 + 8192].bitcast(F32), B.arena[:, 12288:12288 + 8192].bitcast(F32)]
    bscores = [Buf(), Buf()]
    sels = [B.arena[:, 36864:36864 + 4096], B.arena[:, 20480:20480 + 4096]]
    bsels = [Buf(), Buf()]
    wq = B.wbuf.rearrange("p (k n) -> p k n", k=8)
    B.load_w(wq[:, :, 0:512], B.w_in, 0, 512, 8, B.bwbuf)
    B.load_w(wq[:, :, 512:1024], B.w_in, 768, 1280, 8, B.bwbuf)
    B.load_w(wq[:, :, 1024:1032], B.w_in, 1344, 1352, 8, B.bwbuf)
    xnT = xnT1.rearrange("p (c t) -> p c t", c=8)
    oaTv = B.oaT[:, :].rearrange("p (c t) -> p c t", c=4)
    wscale = float(8.0 ** -0.5 * 64.0 ** -0.5)
    for ot in range(16):
        J, i = ot // 4, ot % 4
        xt, bxt = B.xt[ot % 2], B.bxt[ot % 2]
        cs, bcs = B.cs[ot % 2], B.bcs[ot % 2]
        md, bmd = maskD[ot % 2], bmaskD[ot % 2]
        qst, bqst = qsts[ot % 2], bqsts[ot % 2]
        qTt, bqTt = qTts_[ot % 2], bqTts_[ot % 2]
        w_t, bw = w_ts[ot % 2], bws[ot % 2]
        otok, botok = otoks[ot % 2], botoks[ot % 2]
        B.den, B.bden = dens[ot % 2], bdens[ot % 2]
        B.stb, B.bstb = stbs[ot % 2], bstbs[ot % 2]
        B.std, B.bstd = stds[ot % 2], bstds[ot % 2]
        score, bscore = scores[ot % 2], bscores[ot % 2]
        B.sel, B.bsel = sels[ot % 2], bsels[ot % 2]
        qTv = qTt.rearrange("p (c t) -> p c t", c=8)
        P.dma("sp", xt[:, :], B.xown[ot * 128:(ot + 1) * 128, :], writes=[bxt])
        P.dma("sp", cs[:, :], B.cs_own[ot * 128:(ot + 1) * 128, :], writes=[bcs])
        P.dma("sp", md[:, :], B.mdsa[i], writes=[bmd])
        B.norm_T(xt[:, :], bxt, 128, 0, xnT, bxnT1)
        for a, (c0, c1) in enumerate(((0, 512), (512, 1024))):
            for kc in range(8):
                B.mm(B.psA[a][:, :], xnT[:, kc, :], wq[:, kc, c0:c1], kc == 0, kc == 7, [bxnT1, B.bwbuf],
                     [B.bpsA[a]])
        for kc in range(8):
            B.mm(B.psB[0][:, 0:8], xnT[:, kc, :], wq[:, kc, 1024:1032], kc == 0, kc == 7, [bxnT1, B.bwbuf],
                 [B.bpsB[0]])
        for a in range(2):
            B.rope(B.psA[a][:, :].rearrange("p (h j) -> p h j", h=8),
                   qst[:, a * 512:(a + 1) * 512].rearrange("p (h j) -> p h j", h=8), cs, 128, 8,
                   [B.bpsA[a], bcs], [bqst])
        B.ts("dve", w_t[:, :], B.psB[0][:, 0:8], wscale, None, ALU.mult, None, [B.bpsB[0]], [bw])
        k = B.nxt = (B.nxt + 1) % 2
        ps = B.psT[k][:].rearrange("p (c t) -> p c t", c=8)
        for ch in range(8):
            B.tr(ps[:, ch, :], qst[:, ch * 128:(ch + 1) * 128], 128, [bqst], [B.bpsT[k]])
        B.copy("act", qTv, ps, [B.bpsT[k]], [bqTt])
        chunks = []
        for ci in range(2 * (J + 1)):
            mb = md[:, (ci - 2 * J) * 512:(ci - 2 * J + 1) * 512] if ci >= 2 * J else None
            chunks.append(dict(KT=B.KT, k0=ci * 512, nk=512, va=(lambda j, ci=ci: B.Va[:, ci * 4 + j, :]), mb=mb,
                               bmb=bmd, bk=B.bkvblk[ci * 4:ci * 4 + 4]))
        B.dsa_tile(128, (lambda h: qTv[(h % 2) * 64:(h % 2) * 64 + 64, h // 2, :]),
                   (lambda h: qTv[(h % 2) * 64:(h % 2) * 64 + 64, 4 + h // 2, :]), bqTt, w_t, bw, chunks,
                   score, bscore, otok, botok)
        k = B.nxt = (B.nxt + 1) % 2
        ps = B.psT[k][:].rearrange("p (c t) -> p c t", c=8)
        for ch in range(4):
            B.tr(ps[:, ch, :], otok[:, ch * 128:(ch + 1) * 128], 128, [botok], [B.bpsT[k]])
        B.copy("act", oaTv[:, :, ot * 128:(ot + 1) * 128], ps[:, 0:4, :], [B.bpsT[k]], [B.boaT[J]])


def _sample_small(B):
    t = {}
    t["KTn"], _ = B.salloc(7 * 16, BF16)
    t["Van"], _ = B.salloc(NSEQ_S * 128, BF16)
    t["Vbn"], _ = B.salloc(NSEQ_S * 512, BF16)
    t["QbTs"], _ = B.salloc(4 * 16, BF16)
    t["qTts"], _ = B.salloc(8 * 16, BF16)
    t["wts"], _ = B.salloc(NSEQ_S * 8, F32)
    t["xnTs"], _ = B.salloc(8 * 16, BF16)
    t["rbs"], _ = B.salloc(8 * TS, BF16)
    t["qbd"], _ = B.salloc(4 * 8 * TS, BF16)
    t["osb"], _ = B.salloc(4 * 8 * TS, F32)
    t["pts"], _ = B.salloc(NSEQ_S * NPAGES, I32 if False else F32)
    t["msbs"], _ = B.salloc(TS, BF16)
    t["mds"], _ = B.salloc(TS, F32)
    if not hasattr(B, "bsm"):
        B.bsm = {k: Buf() for k in t}
    return t


def _dyn_page_dma(B, dst, cache, pts, idx, reads, writes):
    rows = cache.rearrange("n p d -> (n p) d")

    def fn(e):
        return e.indirect_dma_start(out=dst, out_offset=None, in_=rows,
                                    in_offset=bass.IndirectOffsetOnAxis(ap=pts[:, idx:idx + 1], axis=0))
    B.P.dma_fn("pool", fn, reads, writes)


def _phase_sample(B):
    P = B.P
    wscale = float(8.0 ** -0.5 * 64.0 ** -0.5)
    B.sreset()
    t = _sample_small(B)
    bsm = B.bsm
    B.common_scratch(1)
    wall, bwall = B.salloc(8 * 1544, BF16)
    B.kvst, B.bkvst = B.salloc_n(2, 1344, F32)
    B.rt, B.brt = B.salloc_n(4, 256, F32)
    B.cs, B.bcs = B.salloc_n(1, 64, F32)
    B.kst, B.bkst = B.salloc(832, BF16)
    qst, bqst = B.salloc(1024, BF16)
    wkv = wall[:, 0:8 * 1344].rearrange("p (k n) -> p k n", k=8)
    B.load_w(wkv[:, :, 0:256], B.w_in, 512, 768, 8, bwall)
    B.load_w(wkv[:, :, 256:320], B.w_in, 1280, 1344, 8, bwall)
    B.load_w(wkv[:, :, 320:1344], B.w_in, 1864, 2888, 8, bwall)
    pts = t["pts"].bitcast(I32)
    ptb_f, bptb = B.salloc(NSEQ_S * NPAGES, F32)
    ptf, bptf = B.salloc(NSEQ_S * NPAGES, F32)
    pcol, bpcol = B.salloc(1, F32)
    ptb = ptb_f.bitcast(I32)
    P.dma("sp", ptb, B.pt[0].partition_broadcast(128), writes=[bptb])
    P.dma("sp", pcol, B.pcol, writes=[bpcol])
    B.copy("dve", ptf, ptb, [bptb], [bptf])
    B.ts("dve", ptf, ptf, 128.0, pcol[:, 0:1], ALU.mult, ALU.add, [bptf, bpcol], [bptf])
    B.copy("dve", pts, ptf, [bptf], [bsm["pts"]])
    P.dma("pool", t["msbs"][:TS, :], B.msb_s, writes=[bsm["msbs"]])
    P.dma("sp", t["mds"][:TS, :], B.md_s, writes=[bsm["mds"]])
    KTn = t["KTn"].rearrange("p (s k) -> p s k", s=7)
    xnTs = t["xnTs"].rearrange("p (c t) -> p c t", c=8)
    QbTs = t["QbTs"].rearrange("p (c t) -> p c t", c=4)
    qTts = t["qTts"].rearrange("p (c t) -> p c t", c=8)
    cs, bcs = B.cs[0], B.bcs[0]
    P.dma("sp", cs[:TS, :], B.cs_s[:, :], writes=[bcs])
    xt, bxt = B.xt[0], B.bxt[0]
    B.bwbuf = bwall
    for q in range(NSEQ_S):
        qs = slice(q * TS, (q + 1) * TS)
        kvst, bkv = B.kvst[q % 2], B.bkvst[q % 2]
        P.dma("sp", xt[:TS, :], B.xs[q * TS:(q + 1) * TS, :], writes=[bxt])
        B.norm_T(xt[:TS, :], bxt, TS, 0, xnTs[:, :, qs], bsm["xnTs"])
        B.kv_proj(xnTs[:, :, qs], bsm["xnTs"], TS, cs, bcs, kvst, bkv, wkv)
        P.dma("sp", B.kv_s[q * TS:(q + 1) * TS, :], kvst[:TS, :], reads=[bkv])
        B.kv_build(kvst, bkv, TS, dKa=KTn[:, 0:3, qs], dKb=KTn[:, 3:7, qs],
                   dVa=t["Van"][:TS, q * 128:(q + 1) * 128], dVb=t["Vbn"][:TS, q * 512:(q + 1) * 512],
                   bdest=bsm["KTn"])
    wqb = wall[:, 0:8 * 512].rearrange("p (k n) -> p k n", k=8)
    wq = wall[:, 8 * 512:8 * 1544].rearrange("p (k n) -> p k n", k=8)
    bwqb = bwq = bwall
    B.load_w(wqb, B.w_in, 1352, 1864, 8, bwall)
    B.load_w(wq[:, :, 0:512], B.w_in, 0, 512, 8, bwall)
    B.load_w(wq[:, :, 512:1024], B.w_in, 768, 1280, 8, bwall)
    B.load_w(wq[:, :, 1024:1032], B.w_in, 1344, 1352, 8, bwall)
    for q in range(NSEQ_S):
        qs = slice(q * TS, (q + 1) * TS)
        for ch in range(4):
            for kc in range(8):
                B.mm(B.psB[ch % 2][:, 0:TS], wqb[:, kc, ch * 128:(ch + 1) * 128], xnTs[:, kc, qs], kc == 0, kc == 7,
                     [bwqb, bsm["xnTs"]], [B.bpsB[ch % 2]])
            B.copy("dve", QbTs[:, ch, qs], B.psB[ch % 2][:, 0:TS], [B.bpsB[ch % 2]], [bsm["QbTs"]])
        for a, (c0, c1) in enumerate(((0, 512), (512, 1024))):
            for kc in range(8):
                B.mm(B.psA[a][:TS, :], xnTs[:, kc, qs], wq[:, kc, c0:c1], kc == 0, kc == 7, [bsm["xnTs"], bwq],
                     [B.bpsA[a]])
        for kc in range(8):
            B.mm(B.psB[0][:TS, 0:8], xnTs[:, kc, qs], wq[:, kc, 1024:1032], kc == 0, kc == 7, [bsm["xnTs"], bwq],
                 [B.bpsB[0]])
        for a in range(2):
            B.rope(B.psA[a][:TS, :].rearrange("p (h j) -> p h j", h=8),
                   qst[:TS, a * 512:(a + 1) * 512].rearrange("p (h j) -> p h j", h=8), cs, TS, 8,
                   [B.bpsA[a], bcs], [bqst])
        B.ts("dve", t["wts"][:TS, q * 8:(q + 1) * 8], B.psB[0][:TS, 0:8], wscale, None, ALU.mult, None,
             [B.bpsB[0]], [bsm["wts"]])
        k = B.nxt = (B.nxt + 1) % 2
        ps = B.psT[k][:].rearrange("p (c t) -> p c t", c=8)
        for ch in range(8):
            B.tr(ps[:, ch, 0:TS], qst[:TS, ch * 128:(ch + 1) * 128], TS, [bqst], [B.bpsT[k]])
        B.copy("act", qTts[:, :, qs], ps[:, :, 0:TS], [B.bpsT[k]], [bsm["qTts"]])
    oaTv = B.oaT[:, :].rearrange("p (c t) -> p c t", c=4)
    obTv = B.obT[:, :].rearrange("p (c t) -> p c t", c=4)
    for q in range(NSEQ_S):
        qs = slice(q * TS, (q + 1) * TS)
        B.sreset()
        t = _sample_small(B)
        pts = t["pts"].bitcast(I32)
        KTn = t["KTn"].rearrange("p (s k) -> p s k", s=7)
        QbTs = t["QbTs"].rearrange("p (c t) -> p c t", c=4)
        qTts = t["qTts"].rearrange("p (c t) -> p c t", c=8)
        B.kvst, B.bkvst = B.salloc_n(2, 1344, F32)
        B.kst, B.bkst = B.salloc(832, BF16)
        B.wk, B.bwk = B.salloc_n(2, 512, F32)
        B.wkb, B.bwkb = B.salloc_n(6, 512, BF16)
        B.den, B.bden = B.salloc(32, F32)
        B.stb, B.bstb = B.salloc(8, F32)
        B.std, B.bstd = B.salloc(8, F32)
        otok, botok = B.salloc(512, BF16)
        score, bscore = B.salloc(8196, F32)
        KT3 = B.arena[:, 0:24576].rearrange("p (s k) -> p s k", s=3)
        Vas = B.arena[:, 24576:32768].rearrange("p (b n) -> p b n", b=64)
        B.sel, B.bsel = B.arena[:, 32768:32768 + 8196], Buf()
        bpg = [Buf() for _ in range(NPAGES)]
        for pg in range(NPAGES):
            kvst, bkv = B.kvst[pg % 2], B.bkvst[pg % 2]
            _dyn_page_dma(B, kvst[:, 0:256], B.ckva, pts, q * NPAGES + pg, [bsm["pts"]], [bkv])
            _dyn_page_dma(B, kvst[:, 256:320], B.cki, pts, q * NPAGES + pg, [bsm["pts"]], [bkv])
            B.kv_build(kvst, bkv, 128, dKa=KT3[:, :, pg * 128:(pg + 1) * 128], dVa=Vas[:, pg, :], bdest=bpg[pg],
                       ceng=("dve", "act"))
        chunks = []
        for ci in range(16):
            chunks.append(dict(KT=KT3, k0=ci * 512, nk=512, va=(lambda j, ci=ci: Vas[:, ci * 4 + j, :]), mb=None,
                               bmb=None, bk=bpg[ci * 4:ci * 4 + 4]))
        chunks.append(dict(KT=KTn[:, 0:3, qs], k0=0, nk=TS, va=(lambda j, q=q: t["Van"][:TS, q * 128:(q + 1) * 128]),
                           mb=t["mds"][:TS, :], bmb=bsm["mds"], bk=[bsm["KTn"]]))
        B.dsa_tile(TS, (lambda h: qTts[(h % 2) * 64:(h % 2) * 64 + 64, h // 2, qs]),
                   (lambda h: qTts[(h % 2) * 64:(h % 2) * 64 + 64, 4 + h // 2, qs]), bsm["qTts"],
                   t["wts"][:, q * 8:(q + 1) * 8], bsm["wts"], chunks, score, bscore, otok, botok)
        k = B.nxt = (B.nxt + 1) % 2
        ps = B.psT[k][:].rearrange("p (c t) -> p c t", c=8)
        for ch in range(4):
            B.tr(ps[:, ch, 0:TS], otok[:TS, ch * 128:(ch + 1) * 128], TS, [botok], [B.bpsT[k]])
        B.copy("act", oaTv[:, :, 2048 + q * TS:2048 + (q + 1) * TS], ps[:, 0:4, 0:TS], [B.bpsT[k]], [B.boaT[4]])
        for half in (1, 0):
            B.sreset()
            t = _sample_small(B)
            pts = t["pts"].bitcast(I32)
            KTn = t["KTn"].rearrange("p (s k) -> p s k", s=7)
            QbTs = t["QbTs"].rearrange("p (c t) -> p c t", c=4)
            qbd = t["qbd"].rearrange("p (c w) -> p c w", c=4)
            B.kvst, B.bkvst = B.salloc_n(3, 1344, F32)
            B.kst, B.bkst = B.salloc(832, BF16)
            B.wk, B.bwk = B.salloc_n(2, 512, F32)
            B.wkb, B.bwkb = B.salloc_n(4, 512, BF16)
            KbTh = B.arena[:, 0:16384].rearrange("p (s k) -> p s k", s=4)
            Vbh = B.arena[:, 16384:32768].rearrange("p (b n) -> p b n", b=32)
            bpg = [Buf() for _ in range(32)]
            if half == 1:
                B.memset("pool", t["qbd"], 0.0, [bsm["qbd"]])
                for c in range(4):
                    for hh in range(2):
                        hcol = (2 * c + hh) * TS
                        B.copy("pool", qbd[hh * 64:(hh + 1) * 64, c, hcol:hcol + TS],
                               QbTs[hh * 64:(hh + 1) * 64, c, qs], [bsm["QbTs"]], [bsm["qbd"]])
            for j in range(31, -1, -1):
                pg = half * 32 + j
                kvst, bkv = B.kvst[j % 3], B.bkvst[j % 3]
                _dyn_page_dma(B, kvst[:, 320:1344], B.ckvb, pts, q * NPAGES + pg, [bsm["pts"]], [bkv])
                B.kv_build(kvst, bkv, 128, dKb=KbTh[:, :, j * 128:(j + 1) * 128], dVb=Vbh[:, j, :], bdest=bpg[j],
                           ceng=("dve", "act"))
            blocks = []
            if half == 1:
                blocks.append(dict(nk=TS, kT=(lambda c: KTn[:, 3 + c, qs]),
                                   v=(lambda c, q=q: t["Vbn"][:TS, q * 512 + c * 128:q * 512 + (c + 1) * 128]),
                                   masked=True, bk=bsm["KTn"]))
            for j in range(31, -1, -1):
                blocks.append(dict(nk=128, kT=(lambda c, j=j: KbTh[:, c, j * 128:(j + 1) * 128]),
                                   v=(lambda c, j=j: Vbh[:, j, c * 128:(c + 1) * 128]), masked=False, bk=bpg[j]))
            msk = t["msbs"][:TS, :].unsqueeze(1).to_broadcast([TS, 8, TS])
            B.sb_run_heads(qbd, bsm["qbd"], blocks, t["rbs"], bsm["rbs"], t["osb"], bsm["osb"], half == 1, msk)
            if half == 0:
                for c in range(4):
                    for hh in range(2):
                        hcol = c * 8 * TS + (2 * c + hh) * TS
                        B.copy("act", obTv[hh * 64:(hh + 1) * 64, c, 2048 + q * TS:2048 + (q + 1) * TS],
                               t["osb"][hh * 64:(hh + 1) * 64, hcol:hcol + TS], [bsm["osb"]], [B.bobT[4]])


def _phase_tail(B, do_sample):
    P = B.P
    B.sreset()
    B.common_scratch(1)
    h1, bh1 = B.salloc(4 * D, F32)
    xnTg, bxnTg = B.salloc(8 * 512, BF16)
    B.wk, B.bwk = B.salloc_n(2, 512, F32)
    t12, bt12 = B.salloc_n(2, 512, F32)
    wgs, bwgs = B.salloc_n(2, 8 * 256, BF16)
    wbrs, bwbrs = B.salloc_n(2, 8 * 128, BF16)
    wffs, bwffs = B.salloc_n(2, 8 * 256, BF16)
    yst, byst = B.salloc(D, F32)
    Wo = B.arena[:, 0:8192].rearrange("p (k n) -> p k n", k=8)
    Wd = B.arena[:, 8192:8192 + 22528].rearrange("p (k n) -> p k n", k=NFF)
    hT = B.arena[:, 30720:30720 + 11264].rearrange("p (k n) -> p k n", k=NFF)
    mT = B.arena[:, 41984:41984 + 4096].rearrange("p (k n) -> p k n", k=8)
    bWo, bWd, bhT, bmT = Buf(), Buf(), Buf(), Buf()
    B.load_w(Wo, B.w_o, 0, D, 8, bWo)
    for q4 in range(2):
        P.dma("pool", Wd[:, q4 * 11:(q4 + 1) * 11, :],
              B.w_d.rearrange("(k p) n -> p k n", p=128)[:, q4 * 11:(q4 + 1) * 11, :], writes=[bWd])
    h1v = h1.rearrange("p (i n) -> p i n", i=4)
    xnTv = xnTg.rearrange("p (c t) -> p c t", c=8)
    oaTv = B.oaT[:, :].rearrange("p (c t) -> p c t", c=4)
    obTv = B.obT[:, :].rearrange("p (c t) -> p c t", c=4)
    groups = [(512, 4, 128, B.xown, J * 512, B.y_own, B.boaT[J], B.bobT[J]) for J in range(4)]
    if do_sample:
        groups.append((16, 1, 16, B.xs, 0, B.y_s, B.boaT[4], B.bobT[4]))
    for gi, (W, ntile, rows, xsrc, r0, ydst, boa, bob) in enumerate(groups):
        co = gi * 512
        xt, bxt = B.xt[0], B.bxt[0]
        for i in range(ntile):
            P.dma("sp", xt[:rows, :], xsrc[r0 + i * rows:r0 + (i + 1) * rows, :], writes=[bxt])
            B.norm_T(xt[:rows, :], bxt, rows, 0, xnTv[:, :, i * 128:i * 128 + rows], bxnTg)
        for fc in range(8):
            wg, bwg = wgs[fc % 2].rearrange("p (k n) -> p k n", k=8), bwgs[fc % 2]
            wb, bwb = wbrs[fc % 2].rearrange("p (k n) -> p k n", k=8), bwbrs[fc % 2]
            B.load_w(wg[:, :, 0:128], B.w_in, 2888 + fc * 128, 2888 + (fc + 1) * 128, 8, bwg)
            B.load_w(wg[:, :, 128:256], B.w_in, 3912 + fc * 128, 3912 + (fc + 1) * 128, 8, bwg)
            B.load_w(wb[:, 0:4, :], B.w_bra, fc * 128, (fc + 1) * 128, 4, bwb)
            B.load_w(wb[:, 4:8, :], B.w_brb, fc * 128, (fc + 1) * 128, 4, bwb)
            for a in range(2):
                for kc in range(8):
                    B.mm(B.psA[a][:, 0:W], wg[:, kc, a * 128:(a + 1) * 128], xnTv[:, kc, 0:W], kc == 0, kc == 7,
                         [bwg, bxnTg], [B.bpsA[a]])
            for a, (oT, bo) in enumerate(((oaTv, boa), (obTv, bob))):
                for c4 in range(4):
                    B.mm(B.psB[a][:, 0:W], wb[:, a * 4 + c4, :], oT[:, c4, co:co + W], c4 == 0, c4 == 3,
                         [bwb, bo], [B.bpsB[a]])
            for a in range(2):
                B.act(B.wk[a][:, 0:W], B.psA[a][:, 0:W], AF.Sigmoid, [B.bpsA[a]], [B.bwk[a]])
                B.tt("dve", t12[a][:, 0:W], B.wk[a][:, 0:W], B.psB[a][:, 0:W], ALU.mult, [B.bwk[a], B.bpsB[a]],
                     [bt12[a]])
            B.tt("pool", mT[:, fc, 0:W], t12[0][:, 0:W], t12[1][:, 0:W], ALU.add, bt12, [bmT])
        for i in range(ntile):
            P.dma("sp", xt[:rows, :], xsrc[r0 + i * rows:r0 + (i + 1) * rows, :], writes=[bxt])
            for hh in range(2):
                for fc in range(8):
                    B.mm(B.psA[hh][:rows, :], mT[:, fc, i * 128:i * 128 + rows], Wo[:, fc, hh * 512:(hh + 1) * 512],
                         fc == 0, fc == 7, [bmT, bWo], [B.bpsA[hh]])
                B.tt("dve", h1v[:rows, i, hh * 512:(hh + 1) * 512], B.psA[hh][:rows, :],
                     xt[:rows, hh * 512:(hh + 1) * 512], ALU.add, [B.bpsA[hh], bxt], [bh1])
            B.norm_T(h1v[:rows, i, :], bh1, rows, 1, xnTv[:, :, i * 128:i * 128 + rows], bxnTg)
        for f in range(NFF):
            wf, bwf = wffs[f % 2].rearrange("p (k n) -> p k n", k=8), bwffs[f % 2]
            B.load_w(wf[:, :, 0:128], B.w_g, f * 128, (f + 1) * 128, 8, bwf)
            B.load_w(wf[:, :, 128:256], B.w_u, f * 128, (f + 1) * 128, 8, bwf)
            for a in range(2):
                for kc in range(8):
                    B.mm(B.psA[a][:, 0:W], wf[:, kc, a * 128:(a + 1) * 128], xnTv[:, kc, 0:W], kc == 0, kc == 7,
                         [bwf, bxnTg], [B.bpsA[a]])
            B.act(B.wk[f % 2][:, 0:W], B.psA[0][:, 0:W], AF.Silu, [B.bpsA[0]], [B.bwk[f % 2]])
            B.tt("dve", hT[:, f, 0:W], B.wk[f % 2][:, 0:W], B.psA[1][:, 0:W], ALU.mult,
                 [B.bwk[f % 2], B.bpsA[1]], [bhT])
        for i in range(ntile):
            for hh in range(2):
                for f in range(NFF):
                    B.mm(B.psB[hh][:rows, :], hT[:, f, i * 128:i * 128 + rows], Wd[:, f, hh * 512:(hh + 1) * 512],
                         f == 0, f == NFF - 1, [bhT, bWd], [B.bpsB[hh]])
                B.tt("dve", h1v[:rows, i, hh * 512:(hh + 1) * 512], B.psB[hh][:rows, :],
                     h1v[:rows, i, hh * 512:(hh + 1) * 512], ALU.add, [B.bpsB[hh], bh1], [bh1])
            ss, rstd = B.st[:rows, 0:1], B.st[:rows, 1:2]
            B.act(B.junk[:rows, :], h1v[:rows, i, :], AF.Square, [bh1, B.bst], [B.bjunk, B.bst], accum_out=ss)
            B.act(rstd, ss, AF.Ln, [B.bst], [B.bst], scale=1.0 / D, bias=1e-6)
            B.act(rstd, rstd, AF.Exp, [B.bst], [B.bst], scale=-0.5)
            B.stt("dve", yst[:rows, :], h1v[:rows, i, :], rstd, B.g[:rows, 2, :], ALU.mult, ALU.mult,
                  [bh1, B.bst, B.bg], [byst])
            P.dma("sp", ydst[r0 + i * rows:r0 + (i + 1) * rows, :], yst[:rows, :], reads=[byst])


def build_program(stage=9, debug=False, npool=2560):
    B = Builder(debug=debug, npool=npool)
    B.load_consts()
    _phase1(B)
    if stage >= 2:
        _phase_sb(B)
    if stage >= 3:
        _phase_dsa(B)
    if stage >= 5:
        _phase_sample(B)
    if stage >= 4:
        _phase_tail(B, stage >= 5)
    if debug:
        B.P.barrier()
        B.P.dma("pool", B.dbg_ob, B.obT[:, :], reads=B.bobT)
        B.P.dma("pool", B.dbg_oa, B.oaT[:, :], reads=B.boaT)
    B.P.emit()
    return B.nc


_ROLE_TILES = None


def _own_tiles(r):
    return [8 * J + 2 * i + r for J in range(4) for i in range(4)]


def _prep(x_prompt, x_sample, cache_kv_a, cache_k_idx, cache_kv_b, page_table, w_in, w_br_a, w_br_b, w_o,
          norm_attn, norm_ffn, w_ffn_gate, w_ffn_up, w_ffn_down, norm_final, cores=range(8), npool=2560):
    f32 = np.float32
    x_prompt = np.asarray(x_prompt, f32)
    x_sample = np.asarray(x_sample, f32)
    ckva = np.ascontiguousarray(np.asarray(cache_kv_a, f32)[0].reshape(-1, 128, 256)[:npool])
    cki = np.ascontiguousarray(np.asarray(cache_k_idx, f32)[0].reshape(-1, 128, 64)[:npool])
    ckvb = np.ascontiguousarray(np.asarray(cache_kv_b, f32)[0].reshape(-1, 128, 1024)[:npool])
    page_table = np.asarray(page_table, np.int32)
    gains = np.stack([np.asarray(norm_attn, f32)[0], np.asarray(norm_ffn, f32)[0], np.asarray(norm_final, f32)])
    cs_all = _rope_tables(np.arange(SEQ))
    cs_s = _rope_tables(8192 + np.arange(TS))
    sidx = np.arange(128)
    consts = np.stack([np.eye(128, dtype=f32),
                       np.where(sidx[:, None] >= sidx[None, :], -8.0, 0.0).astype(f32),
                       np.full((128, 128), -8.0, f32)])
    in_maps = []
    for c in cores:
        b, r = c // 2, c % 2
        tiles = _own_tiles(r)
        rows = np.concatenate([np.arange(t * 128, (t + 1) * 128) for t in tiles])
        qpos = np.concatenate([np.arange((2 * i + r) * 128, (2 * i + r + 1) * 128) for i in range(4)])
        kpos = np.arange(1024)
        msb = (kpos[:, None] < qpos[None, :]).astype(f32).reshape(8, 128, 512)
        mdsa = np.stack([np.where(kpos[None, :] <= qpos[i * 128:(i + 1) * 128, None], 0.0, NEG).astype(f32)
                         for i in range(4)])
        tpos = np.arange(TS)
        in_maps.append({
            "xall": x_prompt[b], "xown": np.ascontiguousarray(x_prompt[b][rows]),
            "xs": np.ascontiguousarray(x_sample[4 * c:4 * c + 4].reshape(16, D)),
            "pt": np.ascontiguousarray(page_table[4 * c:4 * c + 4].reshape(1, 256)),
            "ckva": ckva, "cki": cki, "ckvb": ckvb,
            "w_in": np.asarray(w_in, f32)[0], "w_bra": np.asarray(w_br_a, f32)[0],
            "w_brb": np.asarray(w_br_b, f32)[0], "w_o": np.asarray(w_o, f32)[0],
            "w_g": np.asarray(w_ffn_gate, f32)[0], "w_u": np.asarray(w_ffn_up, f32)[0],
            "w_d": np.asarray(w_ffn_down, f32)[0], "gains": gains,
            "cs_all": cs_all, "cs_own": np.ascontiguousarray(cs_all[rows]), "cs_s": cs_s,
            "msb": msb, "mdsa": mdsa,
            "msb_s": (tpos[:, None] < tpos[None, :]).astype(f32),
            "md_s": np.where(tpos[None, :] <= tpos[:, None], 0.0, NEG).astype(f32),
            "consts": consts, "pcol": np.arange(128, dtype=f32).reshape(128, 1),
        })
    return in_maps


def kernel(x_prompt, x_sample, cache_kv_a, cache_k_idx, cache_kv_b, page_table, w_in, w_br_a, w_br_b, w_o,
           norm_attn, norm_ffn, w_ffn_gate, w_ffn_up, w_ffn_down, norm_final):
    f32 = np.float32
    in_maps = _prep(x_prompt, x_sample, cache_kv_a, cache_k_idx, cache_kv_b, page_table, w_in, w_br_a, w_br_b,
                    w_o, norm_attn, norm_ffn, w_ffn_gate, w_ffn_up, w_ffn_down, norm_final)
    nc = build_program(stage=5)
    res = run_bass_kernel_spmd(nc, in_maps, core_ids=list(range(8)))
    R = res.results
    y_p = np.zeros((4, SEQ, D), f32)
    y_s = np.zeros((32, TS, D), f32)
    kva_p = np.zeros((1, 4, SEQ, 2, 2, 64), f32)
    ki_p = np.zeros((1, 4, SEQ, 64), f32)
    kvb_p = np.zeros((1, 4, SEQ, 2, 8, 64), f32)
    kva_s = np.zeros((1, 32, TS, 2, 2, 64), f32)
    ki_s = np.zeros((1, 32, TS, 64), f32)
    kvb_s = np.zeros((1, 32, TS, 2, 8, 64), f32)
    for c in range(8):
        b, r = c // 2, c % 2
        tiles = _own_tiles(r)
        yo = R[c]["y_own"].reshape(16, 128, D)
        for j, t in enumerate(tiles):
            y_p[b, t * 128:(t + 1) * 128] = yo[j]
        y_s[4 * c:4 * c + 4] = R[c]["y_s"].reshape(4, TS, D)
        if r == 0:
            kv = R[c]["kv_all"]
            kva_p[0, b] = kv[:, 0:256].reshape(SEQ, 2, 2, 64)
            ki_p[0, b] = kv[:, 256:320]
            kvb_p[0, b] = kv[:, 320:1344].reshape(SEQ, 2, 8, 64)
        kvs = R[c]["kv_s"].reshape(4, TS, 1344)
        kva_s[0, 4 * c:4 * c + 4] = kvs[:, :, 0:256].reshape(4, TS, 2, 2, 64)
        ki_s[0, 4 * c:4 * c + 4] = kvs[:, :, 256:320]
        kvb_s[0, 4 * c:4 * c + 4] = kvs[:, :, 320:1344].reshape(4, TS, 2, 8, 64)
    return (y_p, y_s, kva_p, ki_p, kvb_p, kva_s, ki_s, kvb_s)
```
